# Optimizing a Trainium2 kernel written in Bass

```python
import math
import jax, jax.numpy as jnp
from jax import lax
import numpy as np

D_MODEL = 2048
BATCH = 2
SEQ = 4096
DEPTH = 2
DEC_BATCH = 8
DEC_SEQ = 4
PAST_LEN = 16384
PAGE_SIZE = 128

HEAD_DIM = 128
ATTN_WIDTH = D_MODEL // 2
HEADS_PER_GROUP = ATTN_WIDTH // HEAD_DIM
WINDOWS = (128, 512, 2048)
DILATIONS = (1, 4, 16)
N_DIL = len(WINDOWS)
QKV_COLS = N_DIL * HEADS_PER_GROUP * HEAD_DIM
SSM_WIDTH = D_MODEL // 2
SSM_GROUP_CH = 16
SSM_GROUPS = SSM_WIDTH // SSM_GROUP_CH
SSM_STATE = 64
IN_WIDTHS = (QKV_COLS, QKV_COLS, QKV_COLS, ATTN_WIDTH, SSM_WIDTH, SSM_WIDTH, D_MODEL, D_MODEL)
IN_COLS = 3 * QKV_COLS + ATTN_WIDTH + 2 * SSM_WIDTH + 2 * D_MODEL
ROPE_THETA = 10000.0
NORM_EPS = 1e-6
DT_MIN = 1e-3
DT_MAX = 1e-1

kernel_name = "dilated_swa_s5_gated_hybrid_step"


def _rmsnorm(x, w):
    xf = x.astype(jnp.float32)
    y = xf * lax.rsqrt(jnp.mean(xf * xf, axis=-1, keepdims=True) + NORM_EPS)
    return (y * w.astype(jnp.float32)).astype(x.dtype)


def _rope(x, pos):
    half = HEAD_DIM // 2
    inv_freq = jnp.power(ROPE_THETA, -jnp.arange(half, dtype=jnp.float32) * (2.0 / HEAD_DIM))
    ang = pos[:, None] * inv_freq[None, :]
    cos = jnp.cos(ang)[None, :, None, :]
    sin = jnp.sin(ang)[None, :, None, :]
    xf = x.astype(jnp.float32)
    x1, x2 = xf[..., :half], xf[..., half:]
    return jnp.concatenate([x1 * cos - x2 * sin, x2 * cos + x1 * sin], axis=-1).astype(x.dtype)


def _softmax_stats(s, mask):
    s = jnp.where(mask, s, -jnp.inf)
    m = jnp.max(s, axis=-1, keepdims=True)
    p = jnp.exp(s - m)
    l = jnp.sum(p, axis=-1, keepdims=True)
    return p / l, (m + jnp.log(l))[..., 0]


def _dilated_attn_prompt(q, k, v, window, dilation):
    nb, S, H, Dh = q.shape
    span = window // dilation
    L = S // dilation
    n_blk = -(-L // span)
    Lp = n_blk * span

    def blocks(t):
        t = t.reshape(nb, L, dilation, H, Dh).transpose(0, 2, 1, 3, 4)
        t = jnp.pad(t, ((0, 0), (0, 0), (0, Lp - L), (0, 0), (0, 0)))
        return t.reshape(nb, dilation, n_blk, span, H, Dh)

    qb = blocks(q).astype(jnp.float32)
    kb = blocks(k)
    vb = blocks(v)
    pad_prev = ((0, 0), (0, 0), (1, 0), (0, 0), (0, 0), (0, 0))
    kk = jnp.concatenate([jnp.pad(kb, pad_prev)[:, :, :-1], kb], axis=3).astype(jnp.float32)
    vv = jnp.concatenate([jnp.pad(vb, pad_prev)[:, :, :-1], vb], axis=3).astype(jnp.float32)
    s = jnp.einsum('brnqhd,brnkhd->brnqhk', qb, kk) * (Dh ** -0.5)
    a = jnp.arange(span)[:, None]
    c = jnp.arange(2 * span)[None, :]
    blk = jnp.arange(n_blk)[:, None, None]
    mask = (c >= a) & (c <= a + span) & ((blk > 0) | (c >= span))
    p, lse = _softmax_stats(s, mask[:, :, None, :])
    o = jnp.einsum('brnqhk,brnkhd->brnqhd', p, vv)
    o = o.reshape(nb, dilation, Lp, H, Dh)[:, :, :L].transpose(0, 2, 1, 3, 4).reshape(nb, S, H, Dh)
    lse = lse.reshape(nb, dilation, Lp, H)[:, :, :L].transpose(0, 2, 1, 3).reshape(nb, S, H)
    return o, lse


def _dilated_attn_sample(q, k, v, kv_cache, window, dilation):
    nb, T, H, Dh = q.shape
    buf = kv_cache.shape[1]
    ext = jnp.concatenate([kv_cache.astype(k.dtype), jnp.stack([k, v], axis=2)], axis=1)
    span = window // dilation
    idx = buf + jnp.arange(T)[:, None] - dilation * jnp.arange(span + 1)[None, :]
    valid = idx >= 0
    kvg = ext[:, jnp.maximum(idx, 0)].astype(jnp.float32)
    s = jnp.einsum('bthd,btjhd->bthj', q.astype(jnp.float32), kvg[:, :, :, 0]) * (Dh ** -0.5)
    p, lse = _softmax_stats(s, valid[None, :, None, :])
    o = jnp.einsum('bthj,btjhd->bthd', p, kvg[:, :, :, 1])
    return o, lse, ext[:, T:]


def _linear_recurrence_combine(left, right):
    a_l, b_l = left
    a_r, b_r = right
    return a_r * a_l, a_r * b_l + b_r


def _s5_scan(u, h0, lam_re, lam_im, log_dt, b_re, b_im, c_re, c_im, ssm_d):
    nb, T, _ = u.shape
    f32 = jnp.float32
    lam = lax.complex(lam_re.astype(f32), lam_im.astype(f32))
    dt = jnp.exp(log_dt.astype(f32))[:, None]
    a_bar = jnp.exp(lam * dt)
    b_bar = ((a_bar - 1.0) / lam)[..., None] * lax.complex(b_re.astype(f32), b_im.astype(f32))
    c_mat = lax.complex(c_re.astype(f32), c_im.astype(f32))
    uf = u.astype(f32)
    ug = uf.reshape(nb, T, SSM_GROUPS, SSM_GROUP_CH).astype(jnp.complex64)
    bu = jnp.einsum('gpc,btgc->btgp', b_bar, ug)
    if h0 is not None:
        bu = bu.at[:, 0].add(a_bar * h0)
    a = jnp.broadcast_to(a_bar, bu.shape)
    _, h = lax.associative_scan(_linear_recurrence_combine, (a, bu), axis=1)
    y = jnp.einsum('gcp,btgp->btgc', c_mat, h).real.reshape(nb, T, SSM_WIDTH)
    return y + ssm_d.astype(f32) * uf, h[:, -1]


def _decoder_layer(x, pos, kv_caches, ssm_h0, norm_w, w_in, q_norm_w, k_norm_w,
                   lam_re, lam_im, log_dt, b_re, b_im, c_re, c_im, ssm_d,
                   w_glu, b_glu, w_br_attn, w_br_ssm, w_out):
    nb, T, _ = x.shape
    xn = _rmsnorm(x, norm_w)
    proj = xn @ w_in
    split_at = np.cumsum(IN_WIDTHS)[:-1].tolist()
    q, k, v, g_attn, u, g_ssm, m_attn, m_ssm = jnp.split(proj, split_at, axis=-1)
    n_heads = N_DIL * HEADS_PER_GROUP
    q = _rope(_rmsnorm(q.reshape(nb, T, n_heads, HEAD_DIM), q_norm_w), pos)
    k = _rope(_rmsnorm(k.reshape(nb, T, n_heads, HEAD_DIM), k_norm_w), pos)
    v = v.reshape(nb, T, n_heads, HEAD_DIM)

    outs, lses, new_kv = [], [], []
    for g in range(N_DIL):
        hs = slice(g * HEADS_PER_GROUP, (g + 1) * HEADS_PER_GROUP)
        qg, kg, vg = q[:, :, hs], k[:, :, hs], v[:, :, hs]
        if kv_caches is None:
            o, lse = _dilated_attn_prompt(qg, kg, vg, WINDOWS[g], DILATIONS[g])
            keep = min(WINDOWS[g], T)
            new_kv.append(jnp.stack([kg[:, T - keep:], vg[:, T - keep:]], axis=2))
        else:
            o, lse, kv = _dilated_attn_sample(qg, kg, vg, kv_caches[g], WINDOWS[g], DILATIONS[g])
            new_kv.append(kv)
        outs.append(o)
        lses.append(lse)
    w_grp = jax.nn.softmax(jnp.stack(lses, axis=0), axis=0)
    attn = jnp.sum(w_grp[..., None] * jnp.stack(outs, axis=0), axis=0)
    attn = attn.reshape(nb, T, ATTN_WIDTH).astype(x.dtype) * jax.nn.silu(g_attn)

    y_ssm, h_last = _s5_scan(u, ssm_h0, lam_re, lam_im, log_dt, b_re, b_im, c_re, c_im, ssm_d)
    s = jax.nn.gelu(y_ssm)
    y_ssm = (s * jax.nn.sigmoid(s @ w_glu.astype(jnp.float32) + b_glu.astype(jnp.float32))).astype(x.dtype)
    y_ssm = y_ssm * jax.nn.silu(g_ssm)

    merged = jax.nn.sigmoid(m_attn) * (attn @ w_br_attn) + jax.nn.sigmoid(m_ssm) * (y_ssm @ w_br_ssm)
    x_out = x + merged @ w_out
    ssm_state = jnp.stack([h_last.real, h_last.imag], axis=-1)
    return x_out, new_kv, ssm_state


def setup_inputs(seed: int = 0) -> dict:
    key = jax.random.key(seed)
    ks = jax.random.split(key, 24)
    f32 = jnp.float32

    def nrm(k, shape, scale):
        return jax.random.normal(k, shape, f32) * scale

    H = HEADS_PER_GROUP
    lam_im_base = jnp.pi * jnp.arange(SSM_STATE, dtype=f32)
    return {
        'x_prompt': nrm(ks[0], (BATCH, SEQ, D_MODEL), 1.0),
        'x_sample': nrm(ks[1], (DEC_BATCH, DEC_SEQ, D_MODEL), 1.0),
        'cache_kv_d1': nrm(ks[2], (DEPTH, DEC_BATCH, min(WINDOWS[0], PAST_LEN), 2, H, HEAD_DIM), 1.0),
        'cache_kv_d4': nrm(ks[3], (DEPTH, DEC_BATCH, min(WINDOWS[1], PAST_LEN), 2, H, HEAD_DIM), 1.0),
        'cache_kv_d16': nrm(ks[4], (DEPTH, DEC_BATCH, min(WINDOWS[2], PAST_LEN), 2, H, HEAD_DIM), 1.0),
        'state_ssm': nrm(ks[5], (DEPTH, DEC_BATCH, SSM_GROUPS, SSM_STATE, 2), 0.1),
        'norm_w': 1.0 + nrm(ks[6], (DEPTH, D_MODEL), 0.01),
        'w_in': nrm(ks[7], (DEPTH, D_MODEL, IN_COLS), D_MODEL ** -0.5),
        'q_norm_w': 1.0 + nrm(ks[8], (DEPTH, HEAD_DIM), 0.01),
        'k_norm_w': 1.0 + nrm(ks[9], (DEPTH, HEAD_DIM), 0.01),
        'ssm_lambda_re': -0.5 + nrm(ks[10], (DEPTH, SSM_GROUPS, SSM_STATE), 0.01),
        'ssm_lambda_im': lam_im_base + nrm(ks[11], (DEPTH, SSM_GROUPS, SSM_STATE), 0.01),
        'ssm_log_dt': jax.random.uniform(ks[12], (DEPTH, SSM_GROUPS), f32, math.log(DT_MIN), math.log(DT_MAX)),
        'ssm_b_re': nrm(ks[13], (DEPTH, SSM_GROUPS, SSM_STATE, SSM_GROUP_CH), (2 * SSM_GROUP_CH) ** -0.5),
        'ssm_b_im': nrm(ks[14], (DEPTH, SSM_GROUPS, SSM_STATE, SSM_GROUP_CH), (2 * SSM_GROUP_CH) ** -0.5),
        'ssm_c_re': nrm(ks[15], (DEPTH, SSM_GROUPS, SSM_GROUP_CH, SSM_STATE), (2 * SSM_STATE) ** -0.5),
        'ssm_c_im': nrm(ks[16], (DEPTH, SSM_GROUPS, SSM_GROUP_CH, SSM_STATE), (2 * SSM_STATE) ** -0.5),
        'ssm_d': nrm(ks[17], (DEPTH, SSM_WIDTH), 1.0),
        'w_glu': nrm(ks[18], (DEPTH, SSM_WIDTH, SSM_WIDTH), SSM_WIDTH ** -0.5),
        'b_glu': nrm(ks[19], (DEPTH, SSM_WIDTH), 0.01),
        'w_br_attn': nrm(ks[20], (DEPTH, ATTN_WIDTH, D_MODEL), ATTN_WIDTH ** -0.5),
        'w_br_ssm': nrm(ks[21], (DEPTH, SSM_WIDTH, D_MODEL), SSM_WIDTH ** -0.5),
        'w_out': nrm(ks[22], (DEPTH, D_MODEL, D_MODEL), D_MODEL ** -0.5),
    }


def reference(x_prompt, x_sample, cache_kv_d1, cache_kv_d4, cache_kv_d16, state_ssm,
              norm_w, w_in, q_norm_w, k_norm_w, ssm_lambda_re, ssm_lambda_im, ssm_log_dt,
              ssm_b_re, ssm_b_im, ssm_c_re, ssm_c_im, ssm_d, w_glu, b_glu,
              w_br_attn, w_br_ssm, w_out):
    f32 = jnp.float32
    layer_params = (norm_w, w_in, q_norm_w, k_norm_w, ssm_lambda_re, ssm_lambda_im, ssm_log_dt,
                    ssm_b_re, ssm_b_im, ssm_c_re, ssm_c_im, ssm_d, w_glu, b_glu,
                    w_br_attn, w_br_ssm, w_out)
    pos_prompt = jnp.arange(x_prompt.shape[1], dtype=f32)
    pos_sample = PAST_LEN + jnp.arange(x_sample.shape[1], dtype=f32)

    hp, hs = x_prompt, x_sample
    kv_p = [[] for _ in range(N_DIL)]
    kv_s = [[] for _ in range(N_DIL)]
    ssm_p, ssm_s = [], []
    for l in range(DEPTH):
        lw = [w[l] for w in layer_params]
        hp, new_kv, st = _decoder_layer(hp, pos_prompt, None, None, *lw)
        for g in range(N_DIL):
            kv_p[g].append(new_kv[g])
        ssm_p.append(st)

        caches = (cache_kv_d1[l], cache_kv_d4[l], cache_kv_d16[l])
        h0 = lax.complex(state_ssm[l, ..., 0].astype(f32), state_ssm[l, ..., 1].astype(f32))
        hs, new_kv_s, st_s = _decoder_layer(hs, pos_sample, caches, h0, *lw)
        for g in range(N_DIL):
            kv_s[g].append(new_kv_s[g])
        ssm_s.append(st_s)

    kv_d1_prompt = jnp.stack(kv_p[0])
    kv_d4_prompt = jnp.stack(kv_p[1])
    kv_d16_prompt = jnp.stack(kv_p[2])
    ssm_prompt = jnp.stack(ssm_p)
    kv_d1_sample = jnp.stack(kv_s[0])
    kv_d4_sample = jnp.stack(kv_s[1])
    kv_d16_sample = jnp.stack(kv_s[2])
    ssm_sample = jnp.stack(ssm_s)
    return (hp, hs, kv_d1_prompt, kv_d4_prompt, kv_d16_prompt, ssm_prompt,
            kv_d1_sample, kv_d4_sample, kv_d16_sample, ssm_sample)
```

```python
import numpy as np
from contextlib import ExitStack
import concourse.bass as bass
import concourse.mybir as mybir
from concourse.bass_utils import run_bass_kernel_spmd

F32 = mybir.dt.float32
BF16 = mybir.dt.bfloat16
AF = mybir.ActivationFunctionType
ALU = mybir.AluOpType
AX = mybir.AxisListType

ENGINES = ["pe", "act", "dve", "pool", "sp"]
EPOCH = 20000
NSLOT = 8


class Op:
    __slots__ = ("eng", "fn", "dma", "idx", "deps", "dma_deps", "signal", "count", "slot", "target")

    def __init__(self, eng, fn, dma, idx):
        self.eng = eng
        self.fn = fn
        self.dma = dma
        self.idx = idx
        self.deps = {}
        self.dma_deps = set()
        self.signal = False
        self.count = 0
        self.slot = None
        self.target = 0


class Prog:
    def __init__(self):
        self.ops = {e: [] for e in ENGINES}
        self.lastw = {}
        self.readers = {}
        self.ndma = {e: 0 for e in ENGINES}
        self.all_dma = []

    def op(self, eng, fn, reads=(), writes=(), dma=False):
        o = Op(eng, fn, dma, len(self.ops[eng]))
        if "__ph" not in writes and "__nobar" not in writes:
            reads = list(reads) + ["__ph"]
        writes = [w for w in writes if w != "__nobar"]

        def add(p):
            if p is None or p is o:
                return
            if p.dma:
                o.dma_deps.add(p)
            else:
                if p.eng == "pe" and eng == "pe" and not dma:
                    return
                cur = o.deps.get(p.eng)
                if cur is None or p.idx > cur.idx:
                    o.deps[p.eng] = p

        for r in reads:
            add(self.lastw.get(r))
        for w in writes:
            add(self.lastw.get(w))
            for rd in self.readers.get(w, ()):
                add(rd)
        for r in reads:
            self.readers.setdefault(r, []).append(o)
        for w in writes:
            self.lastw[w] = o
            self.readers[w] = []
        if dma:
            i = self.ndma[eng]
            self.ndma[eng] += 1
            o.slot = (eng, i % NSLOT)
            o.target = 16 * (i // NSLOT + 1)
            self.all_dma.append(o)
        self.ops[eng].append(o)
        return o

    def dma_in(self, eng, out, in_, reads=(), writes=(), **kw):
        return self.op(eng, lambda e: e.dma_start(out=out, in_=in_, **kw), reads, writes, dma=True)

    def emit(self, nc, es):
        for e in ENGINES:
            for o in self.ops[e]:
                for p in o.deps.values():
                    p.signal = True
        sems = {}
        for e in ENGINES:
            n = 0
            for o in self.ops[e]:
                if o.signal:
                    n += 1
                    o.count = n
            nep = n // EPOCH + 1
            sems[e] = [es.enter_context(nc.semaphore(f"s_{e}_{k}")) for k in range(nep)]
        slot_sems = {}
        for e in ENGINES:
            if self.ndma[e]:
                for k in range(NSLOT):
                    slot_sems[(e, k)] = es.enter_context(nc.semaphore(f"d_{e}_{k}"))
        final_targets = {}
        for o in self.all_dma:
            final_targets[o.slot] = max(final_targets.get(o.slot, 0), o.target)

        def run_engine(ename, handle):
            waited = {}
            dwaited = {}
            for o in self.ops[ename]:
                for pe_, p in o.deps.items():
                    if waited.get(pe_, 0) >= p.count:
                        continue
                    waited[pe_] = p.count
                    k, v = divmod(p.count - 1, EPOCH)
                    handle.wait_ge(sems[pe_][k], v + 1)
                for p in o.dma_deps:
                    if dwaited.get(p.slot, 0) >= p.target:
                        continue
                    dwaited[p.slot] = p.target
                    handle.wait_ge(slot_sems[p.slot], p.target)
                if o.dma and o.target > 16:
                    if dwaited.get(o.slot, 0) < o.target - 16:
                        dwaited[o.slot] = o.target - 16
                        handle.wait_ge(slot_sems[o.slot], o.target - 16)
                ins = o.fn(handle)
                if o.dma:
                    ins.then_inc(slot_sems[o.slot], 16)
                elif o.signal:
                    k, v = divmod(o.count - 1, EPOCH)
                    ins.then_inc(sems[ename][k], 1)
            if ename == "sp":
                for slot, tgt in final_targets.items():
                    if dwaited.get(slot, 0) < tgt:
                        handle.wait_ge(slot_sems[slot], tgt)

        with nc.Block() as block:
            @block.tensor
            def _(h):
                run_engine("pe", h)

            @block.scalar
            def _(h):
                run_engine("act", h)

            @block.vector
            def _(h):
                run_engine("dve", h)

            @block.gpsimd
            def _(h):
                run_engine("pool", h)

            @block.sync
            def _(h):
                run_engine("sp", h)


D_MODEL = 2048
SEQ = 4096
DEPTH = 2
NS = 4
PAST = 16384
HD = 128
NH = 8
WINDOWS = (128, 512, 2048)
DILS = (1, 4, 16)
QKV = 3072
TC = 512
NTOK = TC + NS
EPS = 1e-6
O_Q, O_K, O_V, O_GA, O_U, O_GS, O_MA, O_MS = 0, 3072, 6144, 9216, 10240, 11264, 12288, 14336
IN_COLS = 16384
SCALE = float(HD) ** -0.5
C0 = 384
TWO_PI = 6.283185307179586
CW1 = 6.28125
CW2 = TWO_PI - CW1
MAGIC = 12582912.0


def _rope_tables():
    half = HD // 2
    inv = np.power(np.float32(10000.0), -np.arange(half, dtype=np.float32) * np.float32(2.0 / HD)).astype(np.float32)
    pos = np.concatenate([np.arange(SEQ, dtype=np.float32), PAST + np.arange(NS, dtype=np.float32)])
    ang = (pos[:, None] * inv[None, :]).astype(np.float32)
    return np.cos(ang).astype(np.float32), np.sin(ang).astype(np.float32)


def _const_inputs():
    c = {}
    c["cos_t"], c["sin_t"] = _rope_tables()
    c["ident"] = np.eye(128, dtype=np.float32)
    k = np.arange(128)[:, None]
    for g in range(3):
        wid = 896 + WINDOWS[g]
        d = (np.arange(wid)[None, :] - C0) - k
        c[f"mask{g}"] = ((d >= 0) & (d <= WINDOWS[g]) & (d % DILS[g] == 0)).astype(np.float32)
        nt = WINDOWS[g] // 128
        e = (np.arange(nt)[None, :, None] * 128 + k[:, :, None])
        tq = np.arange(NS)[None, None, :]
        dd = WINDOWS[g] + tq - e
        c[f"smask{g}"] = ((dd >= 0) & (dd <= WINDOWS[g]) & (dd % DILS[g] == 0)).astype(np.float32)
    sn = np.zeros((NS, 3, NS), np.float32)
    for g in range(3):
        for e in range(NS):
            for t in range(NS):
                dd = t - e
                sn[e, g, t] = float(dd >= 0 and dd % DILS[g] == 0 and dd <= WINDOWS[g])
    c["snew"] = sn
    s_idx = np.arange(128) // 16
    c["imask"] = (s_idx[:, None] <= s_idx[None, :]).astype(np.float32)
    zz = np.zeros((128, 8, 240), np.float32)
    for sp in range(8):
        for cc in range(16):
            zz[sp * 16 + cc, sp, 112 + cc] = 1.0
    c["zz"] = zz
    return c


CONST_SHAPES = None


def build_program(nch=8, debug=False):
    nc = bass.Bass("TRN2", target_bir_lowering=False)
    consts = _const_inputs()
    din = lambda n, s: nc.dram_tensor(n, list(s), F32, kind="ExternalInput").ap()
    dout = lambda n, s: nc.dram_tensor(n, list(s), F32, kind="ExternalOutput").ap()
    xp = din("xp", [SEQ, D_MODEL])
    xs = din("xs", [NS, D_MODEL])
    cks = [din(f"ckv{i}", [DEPTH, WINDOWS[i], 2, NH, HD]) for i in range(3)]
    sst = din("sst", [DEPTH, 64, 64, 2])
    w_in = din("w_in", [DEPTH, D_MODEL, IN_COLS])
    w_glu = din("w_glu", [DEPTH, 1024, 1024])
    w_bra = din("w_bra", [DEPTH, 1024, D_MODEL])
    w_brs = din("w_brs", [DEPTH, 1024, D_MODEL])
    w_out = din("w_out", [DEPTH, D_MODEL, D_MODEL])
    normw = din("normw", [DEPTH, D_MODEL])
    qnw = din("qnw", [DEPTH, HD])
    knw = din("knw", [DEPTH, HD])
    b_glu = din("b_glu", [DEPTH, 1024])
    ssm_d = din("ssm_d", [DEPTH, 1024])
    lam_re = din("lam_re", [DEPTH, 64, 64])
    lam_im = din("lam_im", [DEPTH, 64, 64])
    log_dt = din("log_dt", [DEPTH, 64])
    b_re = din("b_re", [DEPTH, 64, 64, 16])
    b_im = din("b_im", [DEPTH, 64, 64, 16])
    c_re = din("c_re", [DEPTH, 64, 16, 64])
    c_im = din("c_im", [DEPTH, 64, 16, 64])
    cd = {k: din(k, v.shape) for k, v in consts.items()}
    y_p = dout("y_p", [SEQ, D_MODEL])
    y_s = dout("y_s", [NS, D_MODEL])
    kvp = [dout(f"kvp{i}", [DEPTH, WINDOWS[i], 2, NH, HD]) for i in range(3)]
    kvs = [dout(f"kvs{i}", [DEPTH, WINDOWS[i], 2, NH, HD]) for i in range(3)]
    ssm_p = dout("ssm_p", [DEPTH, 64, 64, 2])
    ssm_s = dout("ssm_s", [DEPTH, 64, 64, 2])
    dbg = {}
    if debug:
        for n_, s_ in [("d_xnT", [128, 16, NTOK]), ("d_sT", [128, 8, NTOK]), ("d_yT", [128, 8, NTOK]),
                       ("d_attnT", [128, 8, NTOK]), ("d_mergedT", [128, 16, NTOK]), ("d_U", [128, 64, 65]),
                       ("d_Ys", [128, 64, 65])]:
            dbg[n_] = dout(n_, s_)
    dscr = lambda n, s, d: nc.dram_tensor(n, list(s), d, kind="Internal").ap()
    kT_scr = dscr("kT_scr", [DEPTH, 24, 128, SEQ], BF16)
    v_scr = dscr("v_scr", [DEPTH, 24, SEQ, HD], BF16)
    wst_scr = dscr("wst_scr", [DEPTH, 2, 128, 64, 64], BF16)
    wi_scr = dscr("wi_scr", [DEPTH, 128, 64, 128], BF16)
    wc_scr = dscr("wc_scr", [DEPTH, 2, 64, 64, 128], BF16)

    P = Prog()
    with ExitStack() as es:
        sb = lambda n, s, d: es.enter_context(nc.sbuf_tensor("sb_" + n, list(s), d))
        psb = lambda n, s, d=F32: es.enter_context(nc.psum_tensor("ps_" + n, list(s), d))
        xres = sb("xres", [128, 5, D_MODEL], F32)
        xnT = sb("xnT", [128, 16, NTOK], BF16)
        wbuf = [sb(f"wbuf{i}", [128, 16, 384], BF16) for i in range(2)]
        wsm = [sb(f"wsm{i}", [128, 16, 128], BF16) for i in range(2)]
        attnT = sb("attnT", [128, 8, NTOK], BF16)
        yT = sb("yT", [128, 8, NTOK], BF16)
        identf = sb("identf", [128, 128], F32)
        identb = sb("identb", [128, 128], BF16)
        onesb = sb("onesb", [128, 128], BF16)
        zzb = sb("zzb", [128, 8, 240], BF16)
        imask = sb("imask", [128, 128], F32)
        smallc = sb("smallc", [128, 64], F32)
        bglu_t = sb("bglu_t", [128, DEPTH, 8], F32)
        dcol = sb("dcol", [128, DEPTH, 64], F32)
        qnw_bc = sb("qnw_bc", [128, DEPTH, HD], F32)
        knw_bc = sb("knw_bc", [128, DEPTH, HD], F32)
        Hst = sb("Hst", [64, DEPTH, 2, 64], F32)
        A8r = sb("A8r", [64, DEPTH, 2, 64], F32)
        A8i = sb("A8i", [64, DEPTH, 2, 64], F32)
        A4 = sb("A4", [64, DEPTH, 2, 64], F32)
        stat = sb("stat", [128, 16], F32)
        bar = sb("bar", [128, 2], F32)
        NAB, NAF = 27800, 7900
        ARB = sb("ARB", [128, NAB], BF16)
        ARF = sb("ARF", [128, NAF], F32)
        Bf = psb("Bf", [128, 6, 512])
        pm = [Bf[:, i, :] for i in range(2)]
        pt = [psb(f"pt{i}", [128, 8, 128], BF16) for i in range(2)]
        pa = [Bf[:, 2 + i, :] for i in range(4)]
        PMK = ["pm0", "pm1"]
        PTK = ["pt0", "pt1"]
        PAK = ["pa0", "pa1", "pa2", "pa3"]

        OP = P.op
        cnt = {"w": 0, "ws": 0, "pm": 0, "pt": 0, "pa": 0}

        class Carve:
            def __init__(self, t, n):
                self.t, self.n, self.off = t, n, 0

            def reset(self):
                self.off = 0

            def get(self, parts, *shape):
                size = int(np.prod(shape))
                assert self.off + size <= self.n, (self.off, size, self.n)
                ap = self.t[0:parts, self.off:self.off + size]
                self.off += size
                if len(shape) > 1:
                    names = " ".join(f"d{i}" for i in range(len(shape)))
                    kw = {f"d{i}": int(shape[i]) for i in range(len(shape))}
                    ap = ap.rearrange(f"p ({names}) -> p {names}", **kw)
                return ap

        cb = Carve(ARB, NAB)
        cf = Carve(ARF, NAF)

        def barrier():
            OP("dve", lambda e: e.memset(bar[:, 0:1], 0.0), [], ["__ph"])
            cb.reset()
            cf.reset()

        def dma(out, in_, r=(), w=(), q="sp", nobar=False, **kw):
            kw.setdefault("allow_slow_non_contiguous", True)
            if nobar:
                return P.op(q, lambda e: e.dma_start(out=out, in_=in_, **kw), list(r), list(w) + ["__nobar"], dma=True)
            return P.dma_in(q, out, in_, reads=r, writes=w, **kw)

        def next_pm():
            i = cnt["pm"] % 2
            cnt["pm"] += 1
            return i

        def next_pt():
            i = cnt["pt"] % 2
            cnt["pt"] += 1
            return i

        def load_w(src_ap, wd):
            i = cnt["w"] % 2
            cnt["w"] += 1
            dma(wbuf[i][:, :, 0:wd], src_ap, w=[f"wbuf{i}"], q="pool", nobar=True)
            return i

        def load_ws(src_ap, nk):
            i = cnt["ws"] % 2
            cnt["ws"] += 1
            dma(wsm[i][:, 0:nk, :], src_ap, w=[f"wsm{i}"], q="pool", nobar=True)
            return i

        def wcols(l, c0, wd):
            return w_in[l][:, c0:c0 + wd].rearrange("(k p) n -> p k n", p=128)


        def MM(out, lhsT, rhs, start, stop, r, w):
            OP("pe", lambda e: e.matmul(out, lhsT=lhsT, rhs=rhs, start=start, stop=stop), r, w)

        def TR(out, in_, ident, r, w):
            OP("pe", lambda e: e.transpose(out=out, in_=in_, identity=ident), r, w)

        def ACTF(out, in_, func, r, w, **kw):
            OP("act", lambda e: e.activation(out=out, in_=in_, func=func, **kw), r, w)

        def ACOPY(out, in_, r, w):
            OP("act", lambda e: e.copy(out=out, in_=in_), r, w)

        def VCOPY(out, in_, r, w, eng="dve"):
            OP(eng, lambda e: e.tensor_copy(out=out, in_=in_), r, w)

        def TT(out, in0, in1, op, r, w, eng="dve"):
            OP(eng, lambda e: e.tensor_tensor(out=out, in0=in0, in1=in1, op=op), r, w)

        def TS(out, in0, s1, s2, op0, op1, r, w, eng="dve"):
            if s2 is None:
                OP(eng, lambda e: e.tensor_scalar(out=out, in0=in0, scalar1=s1, scalar2=None, op0=op0), r, w)
            else:
                OP(eng, lambda e: e.tensor_scalar(out=out, in0=in0, scalar1=s1, scalar2=s2, op0=op0, op1=op1), r, w)

        def STT(out, in0, scalar, in1, op0, op1, r, w, eng="dve"):
            OP(eng, lambda e: e.scalar_tensor_tensor(out=out, in0=in0, scalar=scalar, in1=in1, op0=op0, op1=op1), r, w)

        def MSET(ap, val, r, w, eng="dve"):
            OP(eng, lambda e: e.memset(ap, val), r, w)

        def RECIP(out, in_, r, w):
            OP("dve", lambda e: e.reciprocal(out=out, in_=in_), r, w)

        def RSUM(out, in_, r, w):
            OP("dve", lambda e: e.tensor_reduce(out=out, in_=in_, axis=AX.X, op=ALU.add), r, w)

        dma(identf[:], cd["ident"], w=["identf"])
        VCOPY(identb[:], identf[:], ["identf"], ["identb"])
        MSET(onesb[:], 1.0, [], ["onesb"])
        dma(zzb[:], cd["zz"], w=["zzb"], q="pool")
        dma(imask[:], cd["imask"], w=["imask"])
        MSET(Hst[:], 0.0, [], ["Hst"])
        for l in range(DEPTH):
            for o in range(8):
                dma(bglu_t[:, l, o:o + 1], b_glu[l:l + 1, o * 128:(o + 1) * 128].rearrange("a p -> p a"), w=["bglu_t"])
            for s in range(8):
                dma(dcol[s * 16:(s + 1) * 16, l, :], ssm_d[l:l + 1, :].rearrange("a (g c) -> c (a g)", c=16), w=["dcol"])
            dma(qnw_bc[:, l, :], qnw[l:l + 1, :].partition_broadcast(128), w=["qnw_bc"])
            dma(knw_bc[:, l, :], knw[l:l + 1, :].partition_broadcast(128), w=["knw_bc"])
        for g in range(3):
            W = WINDOWS[g]
            for l in range(DEPTH):
                dma(kvs[g][l, 0:W - NS].rearrange("t a h d -> t (a h d)"), cks[g][l, NS:W].rearrange("t a h d -> t (a h d)"),
                    w=[f"kvs{g}"])

        def cmul(out_r, out_i, ar, ai, br, bi, t1, t2, tag, extra=()):
            rd = [tag] + list(extra)
            TT(t1, ar, br, ALU.mult, rd, [tag])
            TT(t2, ai, bi, ALU.mult, rd, [tag])
            TT(out_r, t1, t2, ALU.subtract, rd, [tag])
            TT(t1, ar, bi, ALU.mult, rd, [tag])
            TT(t2, ai, br, ALU.mult, rd, [tag])
            TT(out_i, t1, t2, ALU.add, rd, [tag])

        def sin_reduced(out, x, tmp, tag):
            TS(tmp, x, 1.0 / TWO_PI, MAGIC, ALU.mult, ALU.add, [tag], [tag])
            TS(tmp, tmp, -MAGIC, None, ALU.add, None, [tag], [tag])
            STT(out, tmp, -CW1, x, ALU.mult, ALU.add, [tag], [tag])
            STT(out, tmp, -CW2, out, ALU.mult, ALU.add, [tag], [tag])
            TS(out, out, 3.1415925, -3.1415925, ALU.min, ALU.max, [tag], [tag])
            ACTF(out, out, AF.Sin, [tag], [tag])

        def ssm_prep(l):
            barrier()
            T = "prep"
            g64 = lambda: cf.get(64, 64)
            lre, lim, dt, LR, LI = g64(), g64(), g64(), g64(), g64()
            tA, tB, tC = g64(), g64(), g64()
            stage = cf.get(128, 64)
            for src, dst in ((lam_re, lre), (lam_im, lim)):
                dma(stage[0:64, :], src[l], w=[T])
                pi = next_pm()
                TR(pm[pi][0:64, 0:64], stage[0:64, :], identf[0:64, 0:64], [T, "identf"], [PMK[pi]])
                VCOPY(dst, pm[pi][0:64, 0:64], [], [PMK[pi], T])
            dma(dt, log_dt[l:l + 1, :].partition_broadcast(64), w=[T])
            ACTF(dt, dt, AF.Exp, [T], [T])
            TT(LR, lre, dt, ALU.mult, [T], [T])
            TT(LI, lim, dt, ALU.mult, [T], [T])
            PP = cf.get(64, 9, 2, 64)
            PN = cf.get(64, 8, 2, 64)
            mag, sn, cs, xc = g64(), g64(), g64(), g64()
            ACTF(mag, LR, AF.Exp, [T], [T])
            sin_reduced(sn, LI, tA, T)
            TS(xc, LI, TWO_PI / 4, None, ALU.add, None, [T], [T])
            sin_reduced(cs, xc, tA, T)
            TT(PP[:, 1, 0, :], mag, cs, ALU.mult, [T], [T])
            TT(PP[:, 1, 1, :], mag, sn, ALU.mult, [T], [T])
            for arr in (PP, PN):
                MSET(arr[:, 0, 0, :], 1.0, [T], [T])
                MSET(arr[:, 0, 1, :], 0.0, [T], [T])
            TT(tB, mag, mag, ALU.mult, [T], [T])
            RECIP(tB, tB, [T], [T])
            TT(PN[:, 1, 0, :], PP[:, 1, 0, :], tB, ALU.mult, [T], [T])
            STT(PN[:, 1, 1, :], PP[:, 1, 1, :], -1.0, tB, ALU.mult, ALU.mult, [T], [T])
            for k in range(2, 9):
                cmul(PP[:, k, 0, :], PP[:, k, 1, :], PP[:, k - 1, 0, :], PP[:, k - 1, 1, :], PP[:, 1, 0, :], PP[:, 1, 1, :], tA, tC, T)
            for k in range(2, 8):
                cmul(PN[:, k, 0, :], PN[:, k, 1, :], PN[:, k - 1, 0, :], PN[:, k - 1, 1, :], PN[:, 1, 0, :], PN[:, 1, 1, :], tA, tC, T)
            VCOPY(A8r[:, l, 0, :], PP[:, 8, 0, :], [T], ["A8"])
            VCOPY(A8r[:, l, 1, :], PP[:, 8, 0, :], [T], ["A8"])
            TS(A8i[:, l, 0, :], PP[:, 8, 1, :], -1.0, None, ALU.mult, None, [T], ["A8"])
            VCOPY(A8i[:, l, 1, :], PP[:, 8, 1, :], [T], ["A8"])
            VCOPY(A4[:, l, :, :], PP[:, 4, :, :], [T], ["A8"])
            qr, qi = g64(), g64()
            TS(tA, PP[:, 1, 0, :], -1.0, None, ALU.add, None, [T], [T])
            TT(tB, lre, lre, ALU.mult, [T], [T])
            TT(tC, lim, lim, ALU.mult, [T], [T])
            TT(tB, tB, tC, ALU.add, [T], [T])
            RECIP(tB, tB, [T], [T])
            TT(qr, tA, lre, ALU.mult, [T], [T])
            TT(tC, PP[:, 1, 1, :], lim, ALU.mult, [T], [T])
            TT(qr, qr, tC, ALU.add, [T], [T])
            TT(qr, qr, tB, ALU.mult, [T], [T])
            TT(qi, PP[:, 1, 1, :], lre, ALU.mult, [T], [T])
            TT(tC, tA, lim, ALU.mult, [T], [T])
            TT(qi, qi, tC, ALU.subtract, [T], [T])
            TT(qi, qi, tB, ALU.mult, [T], [T])
            GQ = 2
            Bre, Bim = cf.get(64, GQ, 16), cf.get(64, GQ, 16)
            bbr, bbi = cf.get(64, GQ, 16), cf.get(64, GQ, 16)
            Cr, Ci = cf.get(64, GQ, 16), cf.get(64, GQ, 16)
            Fr, Fi = cf.get(64, GQ, 8, 16), cf.get(64, GQ, 8, 16)
            Er, Eni = cf.get(64, GQ, 8, 16), cf.get(64, GQ, 8, 16)
            Gr, Gi = cf.get(64, GQ, 8, 16), cf.get(64, GQ, 8, 16)
            u1, u2 = cf.get(64, GQ, 8, 16), cf.get(64, GQ, 8, 16)
            cst = cf.get(128, 64)
            Gb = cb.get(128, 2, GQ, 64)
            Wib = cb.get(128, GQ, 128)
            Wcb = cb.get(64, 2, GQ, 128)
            bc3 = lambda ap, n: ap.unsqueeze(2).to_broadcast([64, GQ, n])
            bc4 = lambda ap: ap.unsqueeze(2).unsqueeze(3).to_broadcast([64, GQ, 8, 16])
            fl = lambda ap: ap.rearrange("p s c -> p (s c)")
            for g0 in range(0, 64, GQ):
                gs = slice(g0, g0 + GQ)
                dma(Bre, b_re[l, gs].rearrange("g p c -> p g c"), w=[T])
                dma(Bim, b_im[l, gs].rearrange("g p c -> p g c"), w=[T])
                for src, dst in ((c_re, Cr), (c_im, Ci)):
                    dma(cst[0:GQ * 16, :], src[l, gs].rearrange("g c p -> (g c) p"), w=[T])
                    pi = next_pm()
                    TR(pm[pi][0:64, 0:GQ * 16], cst[0:GQ * 16, :], identf[0:GQ * 16, 0:GQ * 16], [T, "identf"], [PMK[pi]])
                    VCOPY(dst.rearrange("p g c -> p (g c)"), pm[pi][0:64, 0:GQ * 16], [], [PMK[pi], T])
                cmul(bbr, bbi, bc3(qr[:, gs], 16), bc3(qi[:, gs], 16), Bre, Bim, u1[:, :, 0, :], u2[:, :, 0, :], T)
                pw = lambda arr, ri: arr[:, 0:8, ri, gs].rearrange("p s g -> p g s").unsqueeze(3).to_broadcast([64, GQ, 8, 16])
                bs_ = lambda ap: ap.unsqueeze(2).to_broadcast([64, GQ, 8, 16])
                cmul(Fr, Fi, pw(PN, 0), pw(PN, 1), bs_(bbr), bs_(bbi), u1, u2, T)
                cmul(Er, Eni, pw(PP, 0), pw(PP, 1), bs_(Cr), bs_(Ci), u1, u2, T)
                cmul(Gr, Gi, bc4(PP[:, 7, 0, gs]), bc4(PP[:, 7, 1, gs]), Fr, Fi, u1, u2, T)
                for ri, Gx in ((0, Gr), (1, Gi)):
                    pi = next_pm()
                    for gg in range(GQ):
                        TR(pm[pi][:, gg * 64:(gg + 1) * 64], fl(Gx[:, gg]), identf[0:64, 0:64], [T, "identf"], [PMK[pi]])
                    ACOPY(Gb[:, ri].rearrange("p g q -> p (g q)"), pm[pi][:, 0:GQ * 64], [], [PMK[pi], "Gb"])
                dma(wst_scr[l, :, :, gs, :].rearrange("r k g p -> k r g p"), Gb, r=["Gb"], w=["wst_scr"])
                cmul(Gr, Gi, bc4(PN[:, 7, 0, gs]), bc4(PN[:, 7, 1, gs]), Er, Eni, u1, u2, T)
                ACOPY(Wcb[:, 0].rearrange("p g n -> p (g n)"), Gr.rearrange("p g s c -> p (g s c)"), [T], ["Wcb"])
                ACTF(Wcb[:, 1].rearrange("p g n -> p (g n)"), Gi.rearrange("p g s c -> p (g s c)"), AF.Copy, [T], ["Wcb"], scale=-1.0)
                dma(wc_scr[l, :, :, gs, :].rearrange("r p g n -> p r g n"), Wcb, r=["Wcb"], w=["wc_scr"])
                TS(Eni, Eni, -1.0, None, ALU.mult, None, [T], [T])
                pi = next_pm()
                for gg in range(GQ):
                    MM(pm[pi][:, gg * 128:(gg + 1) * 128], fl(Fr[:, gg]), fl(Er[:, gg]), True, False, [T], [PMK[pi]])
                    MM(pm[pi][:, gg * 128:(gg + 1) * 128], fl(Fi[:, gg]), fl(Eni[:, gg]), False, True, [T], [PMK[pi]])
                for gg in range(GQ):
                    g = g0 + gg
                    TT(Wib[:, gg, :], pm[pi][:, gg * 128:(gg + 1) * 128], imask[:], ALU.mult, ["imask"], [PMK[pi], "Wib"])
                    STT(Wib[:, gg, :], identb[:], dcol[:, l, g:g + 1], Wib[:, gg, :], ALU.mult, ALU.add,
                        ["identb", "dcol", "Wib"], ["Wib"])
                dma(wi_scr[l, :, gs, :], Wib, r=["Wib"], w=["wi_scr"])

        for l in range(DEPTH):
            ssm_prep(l)

        ALLT = ["xnT", "attnT", "yT", "U", "mergedT"]

        def fm_linear(ws_i, nk, rhsT, has_s, bm, bmk, bs, bsk, s_off):
            for k in range(nk):
                MM(bm[:, 0:TC], wsm[ws_i][:, k, :], rhsT[:, k, 0:TC], k == 0, k == nk - 1, ALLT + [f"wsm{ws_i}"], [bmk])
            if has_s:
                for k in range(nk):
                    MM(bs[:, s_off:s_off + NS], wsm[ws_i][:, k, :], rhsT[:, k, TC:NTOK], k == 0, k == nk - 1,
                       ALLT + [f"wsm{ws_i}"], [bsk])

        def norm_tile(x_ap, M, col0, junk, xn_tok, normw_bc):
            ACTF(junk[0:M, :], x_ap, AF.Square, ["xres"], ["junk", "stat"], accum_out=stat[0:M, 0:1])
            TS(stat[0:M, 1:2], stat[0:M, 0:1], 1.0 / D_MODEL, EPS, ALU.mult, ALU.add, ["stat"], ["stat"])
            ACTF(stat[0:M, 1:2], stat[0:M, 1:2], AF.Sqrt, ["stat"], ["stat"])
            RECIP(stat[0:M, 2:3], stat[0:M, 1:2], ["stat"], ["stat"])
            STT(xn_tok[0:M, :], x_ap, stat[0:M, 2:3], normw_bc[0:M, :], ALU.mult, ALU.mult, ["xres", "stat", "normw_bc"], ["xn_tok"])
            for kb in range(2):
                ti = next_pt()
                for kk in range(8):
                    k = kb * 8 + kk
                    TR(pt[ti][:, kk, 0:M], xn_tok[0:M, k * 128:(k + 1) * 128], identb[0:M, 0:M], ["xn_tok", "identb"], [PTK[ti]])
                ACOPY(xnT[:, kb * 8:(kb + 1) * 8, col0:col0 + M], pt[ti][:, :, 0:M], [], [PTK[ti], "xnT"])

        def head_norm_rope(s3, srck, M, nh, wn_bc, cosb, sinb, cs_slot, o3, ok, tmp):
            ksq, kn, t1, t2 = tmp
            ACTF(ksq[0:M, 0:nh], s3, AF.Square, [], [srck, "hn_ksq"])
            RSUM(stat[0:M, 4:4 + nh], ksq[0:M, 0:nh], ["hn_ksq"], ["stat"])
            TS(stat[0:M, 4:4 + nh], stat[0:M, 4:4 + nh], 1.0 / HD, EPS, ALU.mult, ALU.add, ["stat"], ["stat"])
            ACTF(stat[0:M, 4:4 + nh], stat[0:M, 4:4 + nh], AF.Sqrt, ["stat"], ["stat"])
            RECIP(stat[0:M, 4:4 + nh], stat[0:M, 4:4 + nh], ["stat"], ["stat"])
            TT(kn[0:M, 0:nh], s3, stat[0:M, 4:4 + nh].unsqueeze(2).to_broadcast([M, nh, HD]), ALU.mult, ["stat"], [srck, "hn_kn"])
            TT(kn[0:M, 0:nh], kn[0:M, 0:nh], wn_bc[0:M].unsqueeze(1).to_broadcast([M, nh, HD]), ALU.mult,
               ["hn_kn", "qnw_bc", "knw_bc"], ["hn_kn"])
            cbk = cosb[0:M, cs_slot].unsqueeze(1).to_broadcast([M, nh, 64])
            snk = sinb[0:M, cs_slot].unsqueeze(1).to_broadcast([M, nh, 64])
            x1 = kn[0:M, 0:nh, 0:64]
            x2 = kn[0:M, 0:nh, 64:128]
            TT(t1[0:M, 0:nh], x1, cbk, ALU.mult, ["hn_kn", "cs"], ["hn_t1"])
            TT(t2[0:M, 0:nh], x2, snk, ALU.mult, ["hn_kn", "cs"], ["hn_t2"])
            TT(o3[0:M, 0:nh, 0:64], t1[0:M, 0:nh], t2[0:M, 0:nh], ALU.subtract, ["hn_t1", "hn_t2"], [ok])
            TT(t1[0:M, 0:nh], x2, cbk, ALU.mult, ["hn_kn", "cs"], ["hn_t1"])
            TT(t2[0:M, 0:nh], x1, snk, ALU.mult, ["hn_kn", "cs"], ["hn_t2"])
            TT(o3[0:M, 0:nh, 64:128], t1[0:M, 0:nh], t2[0:M, 0:nh], ALU.add, ["hn_t1", "hn_t2"], [ok])

        def head_norm_rope_qk(s4, srckeys, M, wqk, cosb, sinb, cs_slot, o4, ok, tmp):
            ksq, kn, t1, t2, t3, t4 = tmp
            st = stat[0:M, 4:10].rearrange("p (a h) -> p a h", a=2)
            ACTF(ksq[0:M], s4, AF.Square, [], list(srckeys) + ["hn_ksq"])
            RSUM(st, ksq[0:M], ["hn_ksq"], ["stat"])
            TS(st, st, 1.0 / HD, EPS, ALU.mult, ALU.add, ["stat"], ["stat"])
            ACTF(st, st, AF.Sqrt, ["stat"], ["stat"])
            RECIP(st, st, ["stat"], ["stat"])
            TT(kn[0:M], s4, st.unsqueeze(3).to_broadcast([M, 2, 3, HD]), ALU.mult, ["stat"], list(srckeys) + ["hn_kn"])
            TT(kn[0:M], kn[0:M], wqk[0:M].unsqueeze(2).to_broadcast([M, 2, 3, HD]), ALU.mult, ["hn_kn", "wqk"], ["hn_kn"])
            cbk = cosb[0:M, cs_slot].unsqueeze(1).unsqueeze(1).to_broadcast([M, 2, 3, 64])
            snk = sinb[0:M, cs_slot].unsqueeze(1).unsqueeze(1).to_broadcast([M, 2, 3, 64])
            x1 = kn[0:M, :, :, 0:64]
            x2 = kn[0:M, :, :, 64:128]
            TT(t1[0:M], x1, cbk, ALU.mult, ["hn_kn", "cs"], ["hn_t1"])
            TT(t2[0:M], x2, snk, ALU.mult, ["hn_kn", "cs"], ["hn_t2"])
            TT(o4[0:M, :, :, 0:64], t1[0:M], t2[0:M], ALU.subtract, ["hn_t1", "hn_t2"], [ok])
            TT(t3[0:M], x2, cbk, ALU.mult, ["hn_kn", "cs"], ["hn_t1"])
            TT(t4[0:M], x1, snk, ALU.mult, ["hn_kn", "cs"], ["hn_t2"])
            TT(o4[0:M, :, :, 64:128], t3[0:M], t4[0:M], ALU.add, ["hn_t1", "hn_t2"], [ok + "b"])

        def phase_norm(l, ci, has_s):
            barrier()
            junk = cb.get(128, D_MODEL)
            xn_tok = cb.get(128, D_MODEL)
            normw_bc = cf.get(128, D_MODEL)
            dma(normw_bc, normw[l:l + 1, :].partition_broadcast(128), w=["normw_bc"])
            for tt in range(4):
                norm_tile(xres[:, tt, :], 128, tt * 128, junk, xn_tok, normw_bc)
            if has_s:
                norm_tile(xres[0:NS, 4, :], NS, TC, junk, xn_tok, normw_bc)
            if debug and ci == 0 and l == 0:
                dma(dbg["d_xnT"], xnT[:], r=["xnT"], q="pool")

        def phase_ssm(l, ci, has_s):
            barrier()
            UT = cb.get(65, 64, 8, 16)
            U = cb.get(128, 64, 65)
            Ys = cb.get(128, 64, 65)
            Hmov = cb.get(64, 2, 32, 65)
            wst = cb.get(128, 2, 8, 64)
            wib = cb.get(128, 8, 128)
            wcb = cb.get(64, 2, 8, 128)
            us = cb.get(NS, 1024)
            Sall = cf.get(64, 65, 2, 32)
            Tt = cf.get(64, 2, 32)
            M1 = cf.get(64, 2, 32)
            M2 = cf.get(64, 2, 32)
            h0 = cf.get(64, 64, 2)
            gA = cf.get(128, 260)
            gB = cf.get(128, 260)
            fin = cf.get(64, 32, 2)
            sT = U.rearrange("p g j -> p (g j)")[:, 0:8 * NTOK].rearrange("p (t n) -> p t n", t=8)
            xnTp = xnT[:, :, 0:TC].rearrange("p k (j s) -> p k s j", s=8)
            MSET(UT[64:65].rearrange("p g s c -> p (g s c)"), 0.0, [], ["UT"])
            for c0, wd in ((0, 384), (384, 384), (768, 256)):
                wi = load_w(wcols(l, O_U + c0, wd), wd)
                g0, ng = c0 // 16, wd // 16
                for s in range(8):
                    pi = next_pm()
                    for k in range(16):
                        MM(pm[pi][0:64, 0:wd], xnTp[:, k, s, :], wbuf[wi][:, k, 0:wd], k == 0, k == 15, ["xnT", f"wbuf{wi}"], [PMK[pi]])
                    ACOPY(UT[0:64, g0:g0 + ng, s, :], pm[pi][0:64, 0:wd].rearrange("p (g c) -> p g c", c=16), [], [PMK[pi], "UT"])
                if has_s:
                    pi = next_pm()
                    for k in range(16):
                        MM(pm[pi][0:NS, 0:wd], xnT[:, k, TC:NTOK], wbuf[wi][:, k, 0:wd], k == 0, k == 15, ["xnT", f"wbuf{wi}"], [PMK[pi]])
                    ACOPY(us[:, c0:c0 + wd], pm[pi][0:NS, 0:wd], [], [PMK[pi], "us"])
            if has_s:
                for tau in range(NS):
                    dma(UT[64:65, :, 4 + tau, :], us[tau:tau + 1, :].rearrange("p (g c) -> p g c", c=16), r=["us"], w=["UT"])
                dma(h0, sst[l].rearrange("g p r -> p g r"), w=["h0"])
            for gb in range(8):
                ti = next_pt()
                for gg in range(8):
                    g = gb * 8 + gg
                    TR(pt[ti][:, gg, 0:65], UT[0:65, g].rearrange("p s c -> p (s c)"), identb[0:65, 0:65], ["UT", "identb"], [PTK[ti]])
                ACOPY(U[:, gb * 8:(gb + 1) * 8, :], pt[ti][:, :, 0:65], [], [PTK[ti], "U"])
            if debug and ci == 0 and l == 0:
                dma(dbg["d_U"], U, r=["U"], q="pool")
            for hf in range(2):
                h0g = hf * 32
                for e8 in range(4):
                    ge = h0g + e8 * 8
                    dma(wst, wst_scr[l, :, :, ge:ge + 8, :].rearrange("r k g p -> k r g p"), r=["wst_scr"], w=["wst"])
                    for b3 in range(0, 8, 3):
                        ng = min(3, 8 - b3)
                        pi = next_pm()
                        for gg in range(ng):
                            for ri in range(2):
                                sl = (gg * 2 + ri) * 65
                                MM(pm[pi][0:64, sl:sl + 65], wst[:, ri, b3 + gg, :], U[:, ge + b3 + gg, :], True, True, ["wst", "U"], [PMK[pi]])
                        gl = e8 * 8 + b3
                        ACOPY(Sall[:, :, :, gl:gl + ng].rearrange("p j r g -> p g r j"),
                              pm[pi][0:64, 0:ng * 130].rearrange("p (g r j) -> p g r j", g=ng, r=2), [], [PMK[pi], "Sall"])
                Hv = Hst[:, l, :, h0g:h0g + 32]
                ar2 = A8r[:, l, :, h0g:h0g + 32]
                ai2 = A8i[:, l, :, h0g:h0g + 32]
                ACOPY(Hmov[:, :, :, 0], Hv, ["Hst"], ["Hmov"])
                for j in range(64):
                    TT(Tt, Hv, Sall[:, j, :, :], ALU.add, ["Hst", "Sall"], ["Tt"])
                    if j == 63 and ci == 7:
                        VCOPY(fin, Tt.rearrange("p r g -> p g r"), ["Tt"], ["fin"])
                        dma(ssm_p[l, h0g:h0g + 32].rearrange("g p r -> p g r"), fin, r=["fin"])
                    TT(M1, Tt, ar2, ALU.mult, ["Tt", "A8"], ["M1"])
                    TT(M2[:, 0, :], Tt[:, 1, :], ai2[:, 0, :], ALU.mult, ["Tt", "A8"], ["M2"])
                    TT(M2[:, 1, :], Tt[:, 0, :], ai2[:, 1, :], ALU.mult, ["Tt", "A8"], ["M2"])
                    TT(Hv, M1, M2, ALU.add, ["M1", "M2"], ["Hst"])
                    if j < 63:
                        ACOPY(Hmov[:, :, :, j + 1], Hv, ["Hst"], ["Hmov"])
                if has_s:
                    hr, hi = h0[:, h0g:h0g + 32, 0], h0[:, h0g:h0g + 32, 1]
                    a4r, a4i = A4[:, l, 0, h0g:h0g + 32], A4[:, l, 1, h0g:h0g + 32]
                    cmul(M1[:, 0, :], M1[:, 1, :], hr, hi, a4r, a4i, M2[:, 0, :], M2[:, 1, :], "M1", extra=["h0", "A8", "M2", "Tt"])
                    ACOPY(Hmov[:, :, :, 64], M1, ["M1", "h0", "A8"], ["Hmov"])
                    TT(Tt, M1, Sall[:, 64, :, :], ALU.add, ["M1", "Sall"], ["Tt"])
                    VCOPY(fin, Tt.rearrange("p r g -> p g r"), ["Tt"], ["fin"])
                    dma(ssm_s[l, h0g:h0g + 32].rearrange("g p r -> p g r"), fin, r=["fin"])
                else:
                    MSET(Hmov[:, :, :, 64], 0.0, [], ["Hmov"])
                for e8 in range(4):
                    ge = h0g + e8 * 8
                    dma(wib, wi_scr[l, :, ge:ge + 8, :], r=["wi_scr"], w=["wib"])
                    dma(wcb, wc_scr[l, :, :, ge:ge + 8, :].rearrange("r p g n -> p r g n"), r=["wc_scr"], w=["wcb"])
                    for b4 in range(2):
                        pi = next_pm()
                        for gg in range(4):
                            gi = b4 * 4 + gg
                            dst = pm[pi][:, gg * 65:(gg + 1) * 65]
                            MM(dst, wib[:, gi, :], U[:, ge + gi, :], True, False, ["wib", "U"], [PMK[pi]])
                            MM(dst, wcb[:, 0, gi, :], Hmov[:, 0, e8 * 8 + gi, :], False, False, ["wcb", "Hmov"], [PMK[pi]])
                            MM(dst, wcb[:, 1, gi, :], Hmov[:, 1, e8 * 8 + gi, :], False, True, ["wcb", "Hmov"], [PMK[pi]])
                        gq = ge + b4 * 4
                        ACOPY(gA, pm[pi][:, 0:260], [], [PMK[pi], "gA"])
                        TT(gB, gA, gA, ALU.mult, ["gA"], ["gB"])
                        TS(gB, gB, 0.044715, 1.0, ALU.mult, ALU.add, ["gB"], ["gB"])
                        TT(gB, gB, gA, ALU.mult, ["gA", "gB"], ["gB"])
                        ACTF(gB, gB, AF.Sigmoid, ["gB"], ["gB"], scale=1.5957691216057308)
                        TT(Ys[:, gq:gq + 4, :].rearrange("p g j -> p (g j)"), gA, gB, ALU.mult, ["gA", "gB"], ["Ys"])
            if debug and ci == 0 and l == 0:
                dma(dbg["d_Ys"], Ys, r=["Ys"], q="pool")
            for tl in range(8):
                for h in range(2):
                    for s4 in range(4):
                        sp = h * 4 + s4
                        for g8 in range(8):
                            MM(pa[h][:, s4 * 65:(s4 + 1) * 65], zzb[:, sp, 112 - 16 * g8:240 - 16 * g8], Ys[:, tl * 8 + g8, :],
                               g8 == 0, g8 == 7, ["zzb", "Ys"], [PAK[h]])
                    src = pa[h][:, 0:260].rearrange("p (s j) -> p s j", s=4)
                    ACOPY(sT[:, tl, h * 256:(h + 1) * 256].rearrange("p (s j) -> p s j", s=4), src[:, :, 0:64], [], [PAK[h], "U"])
                    if has_s and h == 1:
                        ACOPY(sT[:, tl, TC:NTOK], src[:, :, 64], [], [PAK[h], "U"])
            if debug and ci == 0 and l == 0:
                dma(dbg["d_sT"], sT, r=["U"], q="pool")
            return sT

        def phase_glu(l, ci, has_s, sT):
            t1 = cf.get(128, NTOK)
            t2 = cf.get(128, NTOK)
            for ob in range(8):
                wg = load_ws(w_glu[l][:, ob * 128:(ob + 1) * 128].rearrange("(k p) n -> p k n", p=128), 8)
                bA, bAk, bB, bBk = (pa[2], PAK[2], pm[0], PMK[0]) if ob % 2 == 0 else (pa[0], PAK[0], pa[1], PAK[1])
                so = 0 if ob % 2 == 0 else 16
                fm_linear(wg, 8, sT, has_s, bA, bAk, pa[3], PAK[3], so)
                ws_ = load_ws(wcols(l, O_GS + ob * 128, 128), 16)
                fm_linear(ws_, 16, xnT, has_s, bB, bBk, pa[3], PAK[3], so + 8)
                bias = bglu_t[:, l, ob:ob + 1]
                ACTF(t1[:, 0:TC], bA[:, 0:TC], AF.Sigmoid, ["bglu_t"], [bAk, "g_t1"], bias=bias)
                ACTF(t2[:, 0:TC], bB[:, 0:TC], AF.Silu, [], [bBk, "g_t2"])
                TT(t1[:, 0:TC], t1[:, 0:TC], sT[:, ob, 0:TC], ALU.mult, ["g_t1", "U"], ["g_t1"])
                TT(yT[:, ob, 0:TC].rearrange("p (j s) -> p s j", s=8), t1[:, 0:TC].rearrange("p (s j) -> p s j", s=8),
                   t2[:, 0:TC].rearrange("p (j s) -> p s j", s=8), ALU.mult, ["g_t1", "g_t2"], ["yT"])
                if has_s:
                    ACTF(t1[:, TC:NTOK], pa[3][:, so:so + NS], AF.Sigmoid, ["bglu_t"], [PAK[3], "g_t1"], bias=bias)
                    ACTF(t2[:, TC:NTOK], pa[3][:, so + 8:so + 8 + NS], AF.Silu, [], [PAK[3], "g_t2"])
                    TT(t1[:, TC:NTOK], t1[:, TC:NTOK], sT[:, ob, TC:NTOK], ALU.mult, ["g_t1", "U"], ["g_t1"])
                    TT(yT[:, ob, TC:NTOK], t1[:, TC:NTOK], t2[:, TC:NTOK], ALU.mult, ["g_t1", "g_t2"], ["yT"])
            if debug and ci == 0 and l == 0:
                dma(dbg["d_yT"], yT[:], r=["yT"], q="pool")

        def phase_attn(l, ci, has_s):
            barrier()
            t0 = ci * TC
            mk = [cb.get(128, 896 + WINDOWS[g]) for g in range(3)]
            qbs = [cb.get(128, 2, 3, 128) for _ in range(2)]
            qT = cb.get(128, 3, NTOK)
            kT = cb.get(128, 3, NTOK)
            vtok = cb.get(128, 5, 3, 128)
            kTh = [cb.get(128, WINDOWS[g]) for g in range(3)]
            vh = [cb.get(128, WINDOWS[g] // 128, 128) for g in range(3)]
            Eb = [cb.get(128, TC) for _ in range(2)]
            Pb = [cb.get(128, TC) for _ in range(2)]
            cosb = cf.get(128, 5, 64)
            sinb = cf.get(128, 5, 64)
            tmp = (cf.get(128, 2, 3, 128), cf.get(128, 2, 3, 128), cf.get(128, 2, 3, 64), cf.get(128, 2, 3, 64),
                   cf.get(128, 2, 3, 64), cf.get(128, 2, 3, 64))
            ko = [cf.get(128, 2, 3, 128) for _ in range(2)]
            vo = [cf.get(128, 3, 128) for _ in range(2)]
            ao = cf.get(128, NTOK)
            rl = cf.get(128, NTOK)
            sg = cf.get(128, NTOK)
            if has_s:
                kc = cb.get(128, 21, 128)
                kTc = cb.get(128, 21, 128)
                vc = cb.get(128, 21, 128)
                smk = cb.get(128, 21, NS)
                snw = cb.get(NS, 3, NS)
                Es = cb.get(128, 96)
                Ps = cb.get(128, 96)
                toff = (0, 1, 5)
                for g in range(3):
                    nt = WINDOWS[g] // 128
                    dma(smk[:, toff[g]:toff[g] + nt, :], cd[f"smask{g}"], w=["smk"], q="pool")
                dma(snw, cd["snew"], w=["snw"], q="pool")
            for g in range(3):
                dma(mk[g], cd[f"mask{g}"], w=[f"mk{g}"], q="pool", max_dma_last_dim=4096)
            dma(cosb[:, 0:4, :], cd["cos_t"][t0:t0 + TC].rearrange("(t p) d -> p t d", p=128), w=["cs"])
            dma(sinb[:, 0:4, :], cd["sin_t"][t0:t0 + TC].rearrange("(t p) d -> p t d", p=128), w=["cs"])
            if has_s:
                dma(cosb[0:NS, 4, :], cd["cos_t"][SEQ:SEQ + NS], w=["cs"])
                dma(sinb[0:NS, 4, :], cd["sin_t"][SEQ:SEQ + NS], w=["cs"])
            tiles = [(tt, 128, tt * 128, tt) for tt in range(4)] + ([(4, NS, TC, 4)] if has_s else [])
            kcnt = [0]
            BQ = [Bf[:, 0:2, :], Bf[:, 2:4, :]]
            BQK = [[PMK[0], PMK[1]], [PAK[0], PAK[1]]]
            wqk = cf.get(128, 2, HD)
            VCOPY(wqk[:, 0, :], qnw_bc[:, l, :], ["qnw_bc"], ["wqk"])
            VCOPY(wqk[:, 1, :], knw_bc[:, l, :], ["knw_bc"], ["wqk"])
            nxt_w = [0, 0]

            def issue_qk(jj):
                out = []
                for base in (O_Q, O_K):
                    wi = cnt["w"] % 2
                    cnt["w"] += 1
                    for g in range(3):
                        dma(wbuf[wi][:, :, g * 128:(g + 1) * 128], wcols(l, base + g * 1024 + jj * 128, 128), w=[f"wbuf{wi}"], q="pool", nobar=True)
                    out.append(wi)
                return out

            for j in range(NH):
                pend = []

                def flush():
                    while pend:
                        pend.pop(0)()

                if j == 0:
                    nxt_w[:] = issue_qk(0)
                wis = list(nxt_w)
                ws_ga = load_ws(wcols(l, O_GA + j * 128, 128), 16)
                for tt, M, col0, cslot in tiles:
                    bq = kcnt[0] % 2
                    kcnt[0] += 1
                    pair, pkeys = BQ[bq], BQK[bq]
                    for a in range(2):
                        for k in range(16):
                            MM(pair[0:M, a, 0:384], xnT[:, k, col0:col0 + M], wbuf[wis[a]][:, k, 0:384], k == 0, k == 15,
                               ["xnT", f"wbuf{wis[a]}"], [pkeys[a]])
                    flush()
                    s4 = pair[0:M, :, 0:384].rearrange("p a (h d) -> p a h d", h=3)
                    kob, kok = ko[bq], f"ko{bq}"
                    head_norm_rope_qk(s4, pkeys, M, wqk, cosb, sinb, cslot, kob, kok, tmp)
                    qbi, qbk = qbs[bq], f"qb{bq}"
                    ACOPY(qbi[0:M], kob[0:M], [kok, kok + "b"], [qbk])
                    for g in range(3):
                        W = WINDOWS[g]
                        if tt < 4:
                            r0 = t0 + tt * 128 - (SEQ - W)
                            if r0 >= 0:
                                dma(kvp[g][l, r0:r0 + 128, 0, j, :], kob[:, 1, g, :], r=[kok, kok + "b"])
                        else:
                            dma(kvs[g][l, W - NS:W, 0, j, :], kob[0:NS, 1, g, :], r=[kok, kok + "b"], w=[f"kvs{g}"])

                    def later(M=M, col0=col0, qbi=qbi, qbk=qbk):
                        ti = next_pt()
                        for a in range(2):
                            for g in range(3):
                                TR(pt[ti][:, a * 3 + g, 0:M], qbi[0:M, a, g, :], identb[0:M, 0:M], [qbk, "identb"], [PTK[ti]])
                        ACOPY(qT[:, :, col0:col0 + M], pt[ti][:, 0:3, 0:M], [], [PTK[ti], "qT"])
                        ACOPY(kT[:, :, col0:col0 + M], pt[ti][:, 3:6, 0:M], [], [PTK[ti], "kT"])
                    pend.append(later)
                wi = cnt["w"] % 2
                cnt["w"] += 1
                for g in range(3):
                    dma(wbuf[wi][:, :, g * 128:(g + 1) * 128], wcols(l, O_V + g * 1024 + j * 128, 128), w=[f"wbuf{wi}"], q="pool", nobar=True)
                for tt, M, col0, cslot in tiles:
                    pi = next_pm()
                    for k in range(16):
                        MM(pm[pi][0:M, 0:384], xnT[:, k, col0:col0 + M], wbuf[wi][:, k, 0:384], k == 0, k == 15,
                           ["xnT", f"wbuf{wi}"], [PMK[pi]])
                    flush()
                    vb = kcnt[0] % 2
                    kcnt[0] += 1
                    vob, vok = vo[vb], f"vo{vb}"
                    ACOPY(vob[0:M].rearrange("p h d -> p (h d)"), pm[pi][0:M, 0:384], [], [PMK[pi], vok])
                    VCOPY(vtok[0:M, tt], vob[0:M], [vok], ["vtok"])
                    for g in range(3):
                        W = WINDOWS[g]
                        if tt < 4:
                            r0 = t0 + tt * 128 - (SEQ - W)
                            if r0 >= 0:
                                dma(kvp[g][l, r0:r0 + 128, 1, j, :], vob[:, g, :], r=[vok])
                        else:
                            dma(kvs[g][l, W - NS:W, 1, j, :], vob[0:NS, g, :], r=[vok], w=[f"kvs{g}"])
                flush()
                if j + 1 < NH:
                    nxt_w[:] = issue_qk(j + 1)
                for g in range(3):
                    dma(kT_scr[l, g * 8 + j, :, t0:t0 + TC], kT[:, g, 0:TC], r=["kT"], w=["kscr"])
                    dma(v_scr[l, g * 8 + j, t0:t0 + TC, :].rearrange("(t p) d -> p t d", p=128), vtok[:, 0:4, g, :],
                        r=["vtok"], w=["vscr"])
                nht = [min(WINDOWS[g], t0) // 128 for g in range(3)]
                for g in range(3):
                    if nht[g]:
                        h = g * 8 + j
                        lo = t0 - nht[g] * 128
                        dma(kTh[g][:, 0:nht[g] * 128], kT_scr[l, h, :, lo:t0], r=["kscr"], w=[f"kTh{g}"])
                        dma(vh[g][:, 0:nht[g], :], v_scr[l, h, lo:t0, :].rearrange("(t p) d -> p t d", p=128), r=["vscr"], w=[f"vh{g}"])
                klist = []
                for g in range(3):
                    for i in range(nht[g]):
                        klist.append((g, -(nht[g] - i) * 128, kTh[g][:, i * 128:(i + 1) * 128], vh[g][:, i, :], [f"kTh{g}", f"vh{g}"]))
                    for tt in range(4):
                        klist.append((g, tt * 128, kT[:, g, tt * 128:(tt + 1) * 128], vtok[:, tt, g, :], ["kT", "vtok"]))
                def score(idx):
                    g, o, Kap, Vap, keys = klist[idx]
                    ai = idx % 2
                    MM(pa[ai][:, 0:TC], Kap, qT[:, g, 0:TC], True, True, ["qT"] + keys, [PAK[ai]])
                    ACTF(Eb[ai], pa[ai][:, 0:TC], AF.Exp, [], [PAK[ai], f"Eb{ai}"], scale=SCALE)
                    TT(Pb[ai], Eb[ai], mk[g][:, C0 - o:C0 - o + TC], ALU.mult, [f"Eb{ai}", f"mk{g}"], [f"Pb{ai}"], eng="pool")

                def pv(idx):
                    g, o, Kap, Vap, keys = klist[idx]
                    ai = idx % 2
                    first, last = idx == 0, idx == len(klist) - 1
                    MM(pa[2][:, 0:TC], Vap, Pb[ai], first, last, [f"Pb{ai}"] + keys, [PAK[2]])
                    MM(pa[3][:, 0:TC], onesb[:], Pb[ai], first, last, [f"Pb{ai}", "onesb"], [PAK[3]])

                score(0)
                for idx in range(len(klist)):
                    if idx + 1 < len(klist):
                        score(idx + 1)
                    pv(idx)
                RECIP(rl[:, 0:TC], pa[3][:, 0:TC], [], [PAK[3], "rl"])
                TT(ao[:, 0:TC], pa[2][:, 0:TC], rl[:, 0:TC], ALU.mult, ["rl"], [PAK[2], "ao"])
                fm_linear(ws_ga, 16, xnT, has_s, pm[0], PMK[0], pm[1], PMK[1], 0)
                ACTF(sg[:, 0:TC], pm[0][:, 0:TC], AF.Silu, [], [PMK[0], "sg"])
                TT(attnT[:, j, 0:TC], ao[:, 0:TC], sg[:, 0:TC], ALU.mult, ["ao", "sg"], ["attnT"])
                if has_s:
                    for g in range(3):
                        nt = WINDOWS[g] // 128
                        o_ = toff[g]
                        dma(kc[:, o_:o_ + nt, :], cks[g][l, :, 0, j, :].rearrange("(t p) d -> p t d", p=128), w=["kc"], q="pool")
                        dma(vc[:, o_:o_ + nt, :], cks[g][l, :, 1, j, :].rearrange("(t p) d -> p t d", p=128), w=["vc"], q="pool")
                    for b0 in range(0, 21, 8):
                        n = min(8, 21 - b0)
                        ti = next_pt()
                        for t in range(n):
                            TR(pt[ti][:, t, :], kc[:, b0 + t, :], identb[:], ["kc", "identb"], [PTK[ti]])
                        ACOPY(kTc[:, b0:b0 + n, :], pt[ti][:, 0:n, :], [], [PTK[ti], "kTc"])
                    for g in range(3):
                        nt = WINDOWS[g] // 128
                        for t in range(nt):
                            c = (toff[g] + t) * NS
                            MM(pa[0][:, c:c + NS], kTc[:, toff[g] + t, :], qT[:, g, TC:NTOK], True, True, ["kTc", "qT"], [PAK[0]])
                        MM(pa[0][0:NS, 84 + g * NS:84 + (g + 1) * NS], kT[:, g, TC:NTOK], qT[:, g, TC:NTOK], True, True, ["kT", "qT"], [PAK[0]])
                    ACTF(Es[:, 0:84], pa[0][:, 0:84], AF.Exp, [], [PAK[0], "Es"], scale=SCALE)
                    ACTF(Es[0:NS, 84:96], pa[0][0:NS, 84:96], AF.Exp, [], [PAK[0], "Es"], scale=SCALE)
                    TT(Ps[:, 0:84], Es[:, 0:84], smk.rearrange("p t q -> p (t q)"), ALU.mult, ["Es", "smk"], ["Ps"])
                    TT(Ps[0:NS, 84:96], Es[0:NS, 84:96], snw.rearrange("p g q -> p (g q)"), ALU.mult, ["Es", "snw"], ["Ps"])
                    for dst0, use_v in ((16, True), (24, False)):
                        items = []
                        for g in range(3):
                            nt = WINDOWS[g] // 128
                            for t in range(nt):
                                c = (toff[g] + t) * NS
                                items.append((vc[:, toff[g] + t, :] if use_v else onesb[:], Ps[:, c:c + NS]))
                            items.append((vtok[0:NS, 4, g, :] if use_v else onesb[0:NS, :], Ps[0:NS, 84 + g * NS:84 + (g + 1) * NS]))
                        for ii, (lh, rh) in enumerate(items):
                            MM(pm[1][:, dst0:dst0 + NS], lh, rh, ii == 0, ii == len(items) - 1, ["vc", "vtok", "Ps", "onesb"], [PMK[1]])
                    RECIP(rl[:, TC:NTOK], pm[1][:, 24:24 + NS], [], [PMK[1], "rl"])
                    TT(ao[:, TC:NTOK], pm[1][:, 16:16 + NS], rl[:, TC:NTOK], ALU.mult, ["rl"], [PMK[1], "ao"])
                    ACTF(sg[:, TC:NTOK], pm[1][:, 0:NS], AF.Silu, [], [PMK[1], "sg"])
                    TT(attnT[:, j, TC:NTOK], ao[:, TC:NTOK], sg[:, TC:NTOK], ALU.mult, ["ao", "sg"], ["attnT"])
            if debug and ci == 0 and l == 0:
                dma(dbg["d_attnT"], attnT[:], r=["attnT"], q="pool")

        def phase_merge(l, ci, has_s):
            barrier()
            mergedT = cb.get(128, 16, NTOK)
            m1 = cf.get(128, NTOK)
            m2 = cf.get(128, NTOK)
            cols = [(slice(0, TC), None)] + ([(slice(TC, NTOK), 0)] if has_s else [])
            for mb in range(16):
                cs_ = slice(mb * 128, (mb + 1) * 128)
                wa = load_ws(w_bra[l][:, cs_].rearrange("(k p) n -> p k n", p=128), 8)
                fm_linear(wa, 8, attnT, has_s, pm[0], PMK[0], pm[1], PMK[1], 0)
                wb = load_ws(w_brs[l][:, cs_].rearrange("(k p) n -> p k n", p=128), 8)
                fm_linear(wb, 8, yT, has_s, pa[0], PAK[0], pm[1], PMK[1], 4)
                wma = load_ws(wcols(l, O_MA + mb * 128, 128), 16)
                fm_linear(wma, 16, xnT, has_s, pa[1], PAK[1], pm[1], PMK[1], 8)
                wms = load_ws(wcols(l, O_MS + mb * 128, 128), 16)
                fm_linear(wms, 16, xnT, has_s, pa[2], PAK[2], pm[1], PMK[1], 12)
                ACTF(m1[:, 0:TC], pa[1][:, 0:TC], AF.Sigmoid, [], [PAK[1], "m1"])
                ACTF(m2[:, 0:TC], pa[2][:, 0:TC], AF.Sigmoid, [], [PAK[2], "m2"])
                TT(m1[:, 0:TC], pm[0][:, 0:TC], m1[:, 0:TC], ALU.mult, ["m1"], [PMK[0], "m1"])
                TT(m2[:, 0:TC], pa[0][:, 0:TC], m2[:, 0:TC], ALU.mult, ["m2"], [PAK[0], "m2"])
                TT(mergedT[:, mb, 0:TC], m1[:, 0:TC], m2[:, 0:TC], ALU.add, ["m1", "m2"], ["mergedT"])
                if has_s:
                    sc = slice(TC, NTOK)
                    ACTF(m1[:, sc], pm[1][:, 8:12], AF.Sigmoid, [], [PMK[1], "m1"])
                    ACTF(m2[:, sc], pm[1][:, 12:16], AF.Sigmoid, [], [PMK[1], "m2"])
                    TT(m1[:, sc], pm[1][:, 0:4], m1[:, sc], ALU.mult, ["m1"], [PMK[1], "m1"])
                    TT(m2[:, sc], pm[1][:, 4:8], m2[:, sc], ALU.mult, ["m2"], [PMK[1], "m2"])
                    TT(mergedT[:, mb, sc], m1[:, sc], m2[:, sc], ALU.add, ["m1", "m2"], ["mergedT"])
            if debug and ci == 0 and l == 0:
                dma(dbg["d_mergedT"], mergedT, r=["mergedT"], q="pool")
            tiles = [(tt, 128, tt * 128) for tt in range(4)] + ([(4, NS, TC)] if has_s else [])
            for c0, wd in ((0, 384), (384, 384), (768, 384), (1152, 384), (1536, 384), (1920, 128)):
                wi = load_w(w_out[l][:, c0:c0 + wd].rearrange("(k p) n -> p k n", p=128), wd)
                for tt, M, col0 in tiles:
                    pi = next_pm()
                    for k in range(16):
                        MM(pm[pi][0:M, 0:wd], mergedT[:, k, col0:col0 + M], wbuf[wi][:, k, 0:wd], k == 0, k == 15,
                           ["mergedT", f"wbuf{wi}"], [PMK[pi]])
                    TT(xres[0:M, tt, c0:c0 + wd], pm[pi][0:M, 0:wd], xres[0:M, tt, c0:c0 + wd], ALU.add, ["xres"], [PMK[pi], "xres"])

        for ci in range(nch):
            has_s = ci == 0
            t0 = ci * TC
            dma(xres[:, 0:4, :], xp[t0:t0 + TC].rearrange("(t p) d -> p t d", p=128), w=["xres"])
            if has_s:
                dma(xres[0:NS, 4, :], xs, w=["xres"])
            for l in range(DEPTH):
                phase_norm(l, ci, has_s)
                sT = phase_ssm(l, ci, has_s)
                phase_glu(l, ci, has_s, sT)
                phase_attn(l, ci, has_s)
                phase_merge(l, ci, has_s)
            dma(y_p[t0:t0 + TC].rearrange("(t p) d -> p t d", p=128), xres[:, 0:4, :], r=["xres"])
            if has_s:
                dma(y_s, xres[0:NS, 4, :], r=["xres"])
        P.emit(nc, es)
    return nc, consts


_CACHE = {}


def _core_inputs(c, I, consts):
    f = lambda a: np.ascontiguousarray(np.asarray(a, dtype=np.float32))
    m = dict(xp=f(I["x_prompt"][c % 2]), xs=f(I["x_sample"][c]),
             ckv0=f(I["cache_kv_d1"][:, c]), ckv1=f(I["cache_kv_d4"][:, c]), ckv2=f(I["cache_kv_d16"][:, c]),
             sst=f(I["state_ssm"][:, c]))
    for k_, n_ in (("w_in", "w_in"), ("w_glu", "w_glu"), ("w_bra", "w_br_attn"), ("w_brs", "w_br_ssm"), ("w_out", "w_out"),
                   ("normw", "norm_w"), ("qnw", "q_norm_w"), ("knw", "k_norm_w"), ("b_glu", "b_glu"), ("ssm_d", "ssm_d"),
                   ("lam_re", "ssm_lambda_re"), ("lam_im", "ssm_lambda_im"), ("log_dt", "ssm_log_dt"),
                   ("b_re", "ssm_b_re"), ("b_im", "ssm_b_im"), ("c_re", "ssm_c_re"), ("c_im", "ssm_c_im")):
        m[k_] = I["_shared"][n_]
    m.update(consts)
    return m


def kernel(**I):
    f = lambda a: np.ascontiguousarray(np.asarray(a, dtype=np.float32))
    if "nc" not in _CACHE:
        _CACHE["nc"] = build_program()
    nc, consts = _CACHE["nc"]
    I = dict(I)
    I["_shared"] = {n: f(I[n]) for n in ("w_in", "w_glu", "w_br_attn", "w_br_ssm", "w_out", "norm_w", "q_norm_w", "k_norm_w",
                                           "b_glu", "ssm_d", "ssm_lambda_re", "ssm_lambda_im", "ssm_log_dt", "ssm_b_re",
                                           "ssm_b_im", "ssm_c_re", "ssm_c_im")}
    in_maps = [_core_inputs(c, I, consts) for c in range(8)]
    res = run_bass_kernel_spmd(nc, in_maps, core_ids=list(range(8)))
    R = res.results
    B = I["x_prompt"].shape[0]
    y_prompt = np.stack([R[b]["y_p"] for b in range(B)], axis=0)
    y_sample = np.stack([R[b]["y_s"] for b in range(8)], axis=0)
    kvp = [np.stack([R[b][f"kvp{i}"] for b in range(B)], axis=1) for i in range(3)]
    kvs = [np.stack([R[b][f"kvs{i}"] for b in range(8)], axis=1) for i in range(3)]
    ssm_p = np.stack([R[b]["ssm_p"] for b in range(B)], axis=1)
    ssm_s = np.stack([R[b]["ssm_s"] for b in range(8)], axis=1)
    return (y_prompt, y_sample, kvp[0], kvp[1], kvp[2], ssm_p, kvs[0], kvs[1], kvs[2], ssm_s)
```

```python
import numpy as np
from contextlib import ExitStack
import concourse.bass as bass
import concourse.mybir as mybir
from concourse.bass_utils import run_bass_kernel_spmd

F32 = mybir.dt.float32
BF16 = mybir.dt.bfloat16
AF = mybir.ActivationFunctionType
ALU = mybir.AluOpType
AX = mybir.AxisListType

ENGINES = ["pe", "act", "dve", "pool", "sp"]
EPOCH = 20000
NSLOT = 8


class Op:
    __slots__ = ("eng", "fn", "dma", "idx", "deps", "dma_deps", "signal", "count", "slot", "target")

    def __init__(self, eng, fn, dma, idx):
        self.eng = eng
        self.fn = fn
        self.dma = dma
        self.idx = idx
        self.deps = {}
        self.dma_deps = set()
        self.signal = False
        self.count = 0
        self.slot = None
        self.target = 0


class Prog:
    def __init__(self):
        self.ops = {e: [] for e in ENGINES}
        self.lastw = {}
        self.readers = {}
        self.ndma = {e: 0 for e in ENGINES}
        self.all_dma = []

    def op(self, eng, fn, reads=(), writes=(), dma=False):
        o = Op(eng, fn, dma, len(self.ops[eng]))
        if "__ph" not in writes and "__nobar" not in writes:
            reads = list(reads) + ["__ph"]
        writes = [w for w in writes if w != "__nobar"]

        def add(p):
            if p is None or p is o:
                return
            if p.dma:
                o.dma_deps.add(p)
            else:
                if p.eng == "pe" and eng == "pe" and not dma:
                    return
                cur = o.deps.get(p.eng)
                if cur is None or p.idx > cur.idx:
                    o.deps[p.eng] = p

        for r in reads:
            add(self.lastw.get(r))
        for w in writes:
            add(self.lastw.get(w))
            for rd in self.readers.get(w, ()):
                add(rd)
        for r in reads:
            self.readers.setdefault(r, []).append(o)
        for w in writes:
            self.lastw[w] = o
            self.readers[w] = []
        if dma:
            i = self.ndma[eng]
            self.ndma[eng] += 1
            o.slot = (eng, i % NSLOT)
            o.target = 16 * (i // NSLOT + 1)
            self.all_dma.append(o)
        self.ops[eng].append(o)
        return o

    def dma_in(self, eng, out, in_, reads=(), writes=(), **kw):
        return self.op(eng, lambda e: e.dma_start(out=out, in_=in_, **kw), reads, writes, dma=True)

    def emit(self, nc, es):
        for e in ENGINES:
            for o in self.ops[e]:
                for p in o.deps.values():
                    p.signal = True
        sems = {}
        for e in ENGINES:
            n = 0
            for o in self.ops[e]:
                if o.signal:
                    n += 1
                    o.count = n
            nep = n // EPOCH + 1
            sems[e] = [es.enter_context(nc.semaphore(f"s_{e}_{k}")) for k in range(nep)]
        slot_sems = {}
        for e in ENGINES:
            if self.ndma[e]:
                for k in range(NSLOT):
                    slot_sems[(e, k)] = es.enter_context(nc.semaphore(f"d_{e}_{k}"))
        final_targets = {}
        for o in self.all_dma:
            final_targets[o.slot] = max(final_targets.get(o.slot, 0), o.target)

        def run_engine(ename, handle):
            waited = {}
            dwaited = {}
            for o in self.ops[ename]:
                for pe_, p in o.deps.items():
                    if waited.get(pe_, 0) >= p.count:
                        continue
                    waited[pe_] = p.count
                    k, v = divmod(p.count - 1, EPOCH)
                    handle.wait_ge(sems[pe_][k], v + 1)
                for p in o.dma_deps:
                    if dwaited.get(p.slot, 0) >= p.target:
                        continue
                    dwaited[p.slot] = p.target
                    handle.wait_ge(slot_sems[p.slot], p.target)
                if o.dma and o.target > 16:
                    if dwaited.get(o.slot, 0) < o.target - 16:
                        dwaited[o.slot] = o.target - 16
                        handle.wait_ge(slot_sems[o.slot], o.target - 16)
                ins = o.fn(handle)
                if o.dma:
                    ins.then_inc(slot_sems[o.slot], 16)
                elif o.signal:
                    k, v = divmod(o.count - 1, EPOCH)
                    ins.then_inc(sems[ename][k], 1)
            if ename == "sp":
                for slot, tgt in final_targets.items():
                    if dwaited.get(slot, 0) < tgt:
                        handle.wait_ge(slot_sems[slot], tgt)

        with nc.Block() as block:
            @block.tensor
            def _(h):
                run_engine("pe", h)

            @block.scalar
            def _(h):
                run_engine("act", h)

            @block.vector
            def _(h):
                run_engine("dve", h)

            @block.gpsimd
            def _(h):
                run_engine("pool", h)

            @block.sync
            def _(h):
                run_engine("sp", h)


D_MODEL = 2048
SEQ = 4096
DEPTH = 2
NS = 4
PAST = 16384
HD = 128
NH = 8
WINDOWS = (128, 512, 2048)
DILS = (1, 4, 16)
QKV = 3072
TC = 512
NTOK = TC + NS
EPS = 1e-6
O_Q, O_K, O_V, O_GA, O_U, O_GS, O_MA, O_MS = 0, 3072, 6144, 9216, 10240, 11264, 12288, 14336
IN_COLS = 16384
SCALE = float(HD) ** -0.5
C0 = 384
TWO_PI = 6.283185307179586
CW1 = 6.28125
CW2 = TWO_PI - CW1
MAGIC = 12582912.0


def _rope_tables():
    half = HD // 2
    inv = np.power(np.float32(10000.0), -np.arange(half, dtype=np.float32) * np.float32(2.0 / HD)).astype(np.float32)
    pos = np.concatenate([np.arange(SEQ, dtype=np.float32), PAST + np.arange(NS, dtype=np.float32)])
    ang = (pos[:, None] * inv[None, :]).astype(np.float32)
    return np.cos(ang).astype(np.float32), np.sin(ang).astype(np.float32)


def _const_inputs():
    c = {}
    c["cos_t"], c["sin_t"] = _rope_tables()
    c["ident"] = np.eye(128, dtype=np.float32)
    k = np.arange(128)[:, None]
    for g in range(3):
        wid = 896 + WINDOWS[g]
        d = (np.arange(wid)[None, :] - C0) - k
        c[f"mask{g}"] = ((d >= 0) & (d <= WINDOWS[g]) & (d % DILS[g] == 0)).astype(np.float32)
        nt = WINDOWS[g] // 128
        e = (np.arange(nt)[None, :, None] * 128 + k[:, :, None])
        tq = np.arange(NS)[None, None, :]
        dd = WINDOWS[g] + tq - e
        c[f"smask{g}"] = ((dd >= 0) & (dd <= WINDOWS[g]) & (dd % DILS[g] == 0)).astype(np.float32)
    sn = np.zeros((NS, 3, NS), np.float32)
    for g in range(3):
        for e in range(NS):
            for t in range(NS):
                dd = t - e
                sn[e, g, t] = float(dd >= 0 and dd % DILS[g] == 0 and dd <= WINDOWS[g])
    c["snew"] = sn
    s_idx = np.arange(128) // 16
    c["imask"] = (s_idx[:, None] <= s_idx[None, :]).astype(np.float32)
    zz = np.zeros((128, 8, 240), np.float32)
    for sp in range(8):
        for cc in range(16):
            zz[sp * 16 + cc, sp, 112 + cc] = 1.0
    c["zz"] = zz
    return c


CONST_SHAPES = None


def build_program(nch=8, debug=False):
    nc = bass.Bass("TRN2", target_bir_lowering=False)
    consts = _const_inputs()
    din = lambda n, s: nc.dram_tensor(n, list(s), F32, kind="ExternalInput").ap()
    dout = lambda n, s: nc.dram_tensor(n, list(s), F32, kind="ExternalOutput").ap()
    xp = din("xp", [SEQ, D_MODEL])
    xs = din("xs", [NS, D_MODEL])
    cks = [din(f"ckv{i}", [DEPTH, WINDOWS[i], 2, NH, HD]) for i in range(3)]
    sst = din("sst", [DEPTH, 64, 64, 2])
    w_in = din("w_in", [DEPTH, D_MODEL, IN_COLS])
    w_glu = din("w_glu", [DEPTH, 1024, 1024])
    w_bra = din("w_bra", [DEPTH, 1024, D_MODEL])
    w_brs = din("w_brs", [DEPTH, 1024, D_MODEL])
    w_out = din("w_out", [DEPTH, D_MODEL, D_MODEL])
    normw = din("normw", [DEPTH, D_MODEL])
    qnw = din("qnw", [DEPTH, HD])
    knw = din("knw", [DEPTH, HD])
    b_glu = din("b_glu", [DEPTH, 1024])
    ssm_d = din("ssm_d", [DEPTH, 1024])
    lam_re = din("lam_re", [DEPTH, 64, 64])
    lam_im = din("lam_im", [DEPTH, 64, 64])
    log_dt = din("log_dt", [DEPTH, 64])
    b_re = din("b_re", [DEPTH, 64, 64, 16])
    b_im = din("b_im", [DEPTH, 64, 64, 16])
    c_re = din("c_re", [DEPTH, 64, 16, 64])
    c_im = din("c_im", [DEPTH, 64, 16, 64])
    cd = {k: din(k, v.shape) for k, v in consts.items()}
    y_p = dout("y_p", [SEQ, D_MODEL])
    y_s = dout("y_s", [NS, D_MODEL])
    kvp = [dout(f"kvp{i}", [DEPTH, WINDOWS[i], 2, NH, HD]) for i in range(3)]
    kvs = [dout(f"kvs{i}", [DEPTH, WINDOWS[i], 2, NH, HD]) for i in range(3)]
    ssm_p = dout("ssm_p", [DEPTH, 64, 64, 2])
    ssm_s = dout("ssm_s", [DEPTH, 64, 64, 2])
    dbg = {}
    if debug:
        for n_, s_ in [("d_xnT", [128, 16, NTOK]), ("d_sT", [128, 8, NTOK]), ("d_yT", [128, 8, NTOK]),
                       ("d_attnT", [128, 8, NTOK]), ("d_mergedT", [128, 16, NTOK]), ("d_U", [128, 64, 65]),
                       ("d_Ys", [128, 64, 65])]:
            dbg[n_] = dout(n_, s_)
    dscr = lambda n, s, d: nc.dram_tensor(n, list(s), d, kind="Internal").ap()
    kT_scr = dscr("kT_scr", [DEPTH, 24, 128, SEQ], BF16)
    v_scr = dscr("v_scr", [DEPTH, 24, SEQ, HD], BF16)
    wst_scr = dscr("wst_scr", [DEPTH, 2, 128, 64, 64], BF16)
    wi_scr = dscr("wi_scr", [DEPTH, 128, 64, 128], BF16)
    wc_scr = dscr("wc_scr", [DEPTH, 2, 64, 64, 128], BF16)

    P = Prog()
    with ExitStack() as es:
        sb = lambda n, s, d: es.enter_context(nc.sbuf_tensor("sb_" + n, list(s), d))
        psb = lambda n, s, d=F32: es.enter_context(nc.psum_tensor("ps_" + n, list(s), d))
        xres = sb("xres", [128, 5, D_MODEL], F32)
        xnT = sb("xnT", [128, 16, NTOK], BF16)
        wbuf = [sb(f"wbuf{i}", [128, 16, 384], BF16) for i in range(2)]
        wsm = [sb(f"wsm{i}", [128, 16, 128], BF16) for i in range(2)]
        attnT = sb("attnT", [128, 8, NTOK], BF16)
        yT = sb("yT", [128, 8, NTOK], BF16)
        identf = sb("identf", [128, 128], F32)
        identb = sb("identb", [128, 128], BF16)
        onesb = sb("onesb", [128, 128], BF16)
        zzb = sb("zzb", [128, 8, 240], BF16)
        imask = sb("imask", [128, 128], F32)
        smallc = sb("smallc", [128, 64], F32)
        bglu_t = sb("bglu_t", [128, DEPTH, 8], F32)
        dcol = sb("dcol", [128, DEPTH, 64], F32)
        qnw_bc = sb("qnw_bc", [128, DEPTH, HD], F32)
        knw_bc = sb("knw_bc", [128, DEPTH, HD], F32)
        Hst = sb("Hst", [64, DEPTH, 2, 64], F32)
        A8r = sb("A8r", [64, DEPTH, 2, 64], F32)
        A8i = sb("A8i", [64, DEPTH, 2, 64], F32)
        A4 = sb("A4", [64, DEPTH, 2, 64], F32)
        stat = sb("stat", [128, 16], F32)
        bar = sb("bar", [128, 2], F32)
        NAB, NAF = 27800, 7900
        ARB = sb("ARB", [128, NAB], BF16)
        ARF = sb("ARF", [128, NAF], F32)
        Bf = psb("Bf", [128, 6, 512])
        pm = [Bf[:, i, :] for i in range(2)]
        pt = [psb(f"pt{i}", [128, 8, 128], BF16) for i in range(2)]
        pa = [Bf[:, 2 + i, :] for i in range(4)]
        PMK = ["pm0", "pm1"]
        PTK = ["pt0", "pt1"]
        PAK = ["pa0", "pa1", "pa2", "pa3"]

        OP = P.op
        cnt = {"w": 0, "ws": 0, "pm": 0, "pt": 0, "pa": 0}

        class Carve:
            def __init__(self, t, n):
                self.t, self.n, self.off = t, n, 0

            def reset(self):
                self.off = 0

            def get(self, parts, *shape):
                size = int(np.prod(shape))
                assert self.off + size <= self.n, (self.off, size, self.n)
                ap = self.t[0:parts, self.off:self.off + size]
                self.off += size
                if len(shape) > 1:
                    names = " ".join(f"d{i}" for i in range(len(shape)))
                    kw = {f"d{i}": int(shape[i]) for i in range(len(shape))}
                    ap = ap.rearrange(f"p ({names}) -> p {names}", **kw)
                return ap

        cb = Carve(ARB, NAB)
        cf = Carve(ARF, NAF)

        def barrier():
            OP("dve", lambda e: e.memset(bar[:, 0:1], 0.0), [], ["__ph"])
            cb.reset()
            cf.reset()

        def dma(out, in_, r=(), w=(), q="sp", nobar=False, **kw):
            kw.setdefault("allow_slow_non_contiguous", True)
            if nobar:
                return P.op(q, lambda e: e.dma_start(out=out, in_=in_, **kw), list(r), list(w) + ["__nobar"], dma=True)
            return P.dma_in(q, out, in_, reads=r, writes=w, **kw)

        def next_pm():
            i = cnt["pm"] % 2
            cnt["pm"] += 1
            return i

        def next_pt():
            i = cnt["pt"] % 2
            cnt["pt"] += 1
            return i

        def load_w(src_ap, wd, extra=False):
            i = cnt["w"] % 2
            cnt["w"] += 1
            keys = [f"wbuf{i}"] + ([f"wbuf{i}_{s}" for s in range(3)] if extra else [])
            dma(wbuf[i][:, :, 0:wd], src_ap, w=keys, q="pool", nobar=True)
            return i

        def load_ws(src_ap, nk):
            i = cnt["ws"] % 2
            cnt["ws"] += 1
            dma(wsm[i][:, 0:nk, :], src_ap, w=[f"wsm{i}"], q="pool", nobar=True)
            return i

        def wcols(l, c0, wd):
            return w_in[l][:, c0:c0 + wd].rearrange("(k p) n -> p k n", p=128)


        def MM(out, lhsT, rhs, start, stop, r, w):
            OP("pe", lambda e: e.matmul(out, lhsT=lhsT, rhs=rhs, start=start, stop=stop), r, w)

        def TR(out, in_, ident, r, w):
            OP("pe", lambda e: e.transpose(out=out, in_=in_, identity=ident), r, w)

        def ACTF(out, in_, func, r, w, **kw):
            OP("act", lambda e: e.activation(out=out, in_=in_, func=func, **kw), r, w)

        def ACOPY(out, in_, r, w):
            OP("act", lambda e: e.copy(out=out, in_=in_), r, w)

        def VCOPY(out, in_, r, w, eng="dve"):
            OP(eng, lambda e: e.tensor_copy(out=out, in_=in_), r, w)

        def TT(out, in0, in1, op, r, w, eng="dve"):
            OP(eng, lambda e: e.tensor_tensor(out=out, in0=in0, in1=in1, op=op), r, w)

        def TS(out, in0, s1, s2, op0, op1, r, w, eng="dve"):
            if s2 is None:
                OP(eng, lambda e: e.tensor_scalar(out=out, in0=in0, scalar1=s1, scalar2=None, op0=op0), r, w)
            else:
                OP(eng, lambda e: e.tensor_scalar(out=out, in0=in0, scalar1=s1, scalar2=s2, op0=op0, op1=op1), r, w)

        def STT(out, in0, scalar, in1, op0, op1, r, w, eng="dve"):
            OP(eng, lambda e: e.scalar_tensor_tensor(out=out, in0=in0, scalar=scalar, in1=in1, op0=op0, op1=op1), r, w)

        def MSET(ap, val, r, w, eng="dve"):
            OP(eng, lambda e: e.memset(ap, val), r, w)

        def RECIP(out, in_, r, w):
            OP("dve", lambda e: e.reciprocal(out=out, in_=in_), r, w)

        def RSUM(out, in_, r, w):
            OP("dve", lambda e: e.tensor_reduce(out=out, in_=in_, axis=AX.X, op=ALU.add), r, w)

        dma(identf[:], cd["ident"], w=["identf"])
        VCOPY(identb[:], identf[:], ["identf"], ["identb"])
        MSET(onesb[:], 1.0, [], ["onesb"])
        dma(zzb[:], cd["zz"], w=["zzb"], q="pool")
        dma(imask[:], cd["imask"], w=["imask"])
        MSET(Hst[:], 0.0, [], ["Hst"])
        for l in range(DEPTH):
            for o in range(8):
                dma(bglu_t[:, l, o:o + 1], b_glu[l:l + 1, o * 128:(o + 1) * 128].rearrange("a p -> p a"), w=["bglu_t"])
            for s in range(8):
                dma(dcol[s * 16:(s + 1) * 16, l, :], ssm_d[l:l + 1, :].rearrange("a (g c) -> c (a g)", c=16), w=["dcol"])
            dma(qnw_bc[:, l, :], qnw[l:l + 1, :].partition_broadcast(128), w=["qnw_bc"])
            dma(knw_bc[:, l, :], knw[l:l + 1, :].partition_broadcast(128), w=["knw_bc"])
        for g in range(3):
            W = WINDOWS[g]
            for l in range(DEPTH):
                dma(kvs[g][l, 0:W - NS].rearrange("t a h d -> t (a h d)"), cks[g][l, NS:W].rearrange("t a h d -> t (a h d)"),
                    w=[f"kvs{g}"])

        def cmul(out_r, out_i, ar, ai, br, bi, t1, t2, tag, extra=()):
            rd = [tag] + list(extra)
            TT(t1, ar, br, ALU.mult, rd, [tag])
            TT(t2, ai, bi, ALU.mult, rd, [tag])
            TT(out_r, t1, t2, ALU.subtract, rd, [tag])
            TT(t1, ar, bi, ALU.mult, rd, [tag])
            TT(t2, ai, br, ALU.mult, rd, [tag])
            TT(out_i, t1, t2, ALU.add, rd, [tag])

        def sin_reduced(out, x, tmp, tag):
            TS(tmp, x, 1.0 / TWO_PI, MAGIC, ALU.mult, ALU.add, [tag], [tag])
            TS(tmp, tmp, -MAGIC, None, ALU.add, None, [tag], [tag])
            STT(out, tmp, -CW1, x, ALU.mult, ALU.add, [tag], [tag])
            STT(out, tmp, -CW2, out, ALU.mult, ALU.add, [tag], [tag])
            TS(out, out, 3.1415925, -3.1415925, ALU.min, ALU.max, [tag], [tag])
            ACTF(out, out, AF.Sin, [tag], [tag])

        def ssm_prep(l):
            barrier()
            T = "prep"
            g64 = lambda: cf.get(64, 64)
            lre, lim, dt, LR, LI = g64(), g64(), g64(), g64(), g64()
            tA, tB, tC = g64(), g64(), g64()
            stage = cf.get(128, 64)
            for src, dst in ((lam_re, lre), (lam_im, lim)):
                dma(stage[0:64, :], src[l], w=[T])
                pi = next_pm()
                TR(pm[pi][0:64, 0:64], stage[0:64, :], identf[0:64, 0:64], [T, "identf"], [PMK[pi]])
                VCOPY(dst, pm[pi][0:64, 0:64], [], [PMK[pi], T])
            dma(dt, log_dt[l:l + 1, :].partition_broadcast(64), w=[T])
            ACTF(dt, dt, AF.Exp, [T], [T])
            TT(LR, lre, dt, ALU.mult, [T], [T])
            TT(LI, lim, dt, ALU.mult, [T], [T])
            PP = cf.get(64, 9, 2, 64)
            PN = cf.get(64, 8, 2, 64)
            mag, sn, cs, xc = g64(), g64(), g64(), g64()
            ACTF(mag, LR, AF.Exp, [T], [T])
            sin_reduced(sn, LI, tA, T)
            TS(xc, LI, TWO_PI / 4, None, ALU.add, None, [T], [T])
            sin_reduced(cs, xc, tA, T)
            TT(PP[:, 1, 0, :], mag, cs, ALU.mult, [T], [T])
            TT(PP[:, 1, 1, :], mag, sn, ALU.mult, [T], [T])
            for arr in (PP, PN):
                MSET(arr[:, 0, 0, :], 1.0, [T], [T])
                MSET(arr[:, 0, 1, :], 0.0, [T], [T])
            TT(tB, mag, mag, ALU.mult, [T], [T])
            RECIP(tB, tB, [T], [T])
            TT(PN[:, 1, 0, :], PP[:, 1, 0, :], tB, ALU.mult, [T], [T])
            STT(PN[:, 1, 1, :], PP[:, 1, 1, :], -1.0, tB, ALU.mult, ALU.mult, [T], [T])
            for k in range(2, 9):
                cmul(PP[:, k, 0, :], PP[:, k, 1, :], PP[:, k - 1, 0, :], PP[:, k - 1, 1, :], PP[:, 1, 0, :], PP[:, 1, 1, :], tA, tC, T)
            for k in range(2, 8):
                cmul(PN[:, k, 0, :], PN[:, k, 1, :], PN[:, k - 1, 0, :], PN[:, k - 1, 1, :], PN[:, 1, 0, :], PN[:, 1, 1, :], tA, tC, T)
            VCOPY(A8r[:, l, 0, :], PP[:, 8, 0, :], [T], ["A8"])
            VCOPY(A8r[:, l, 1, :], PP[:, 8, 0, :], [T], ["A8"])
            TS(A8i[:, l, 0, :], PP[:, 8, 1, :], -1.0, None, ALU.mult, None, [T], ["A8"])
            VCOPY(A8i[:, l, 1, :], PP[:, 8, 1, :], [T], ["A8"])
            VCOPY(A4[:, l, :, :], PP[:, 4, :, :], [T], ["A8"])
            qr, qi = g64(), g64()
            TS(tA, PP[:, 1, 0, :], -1.0, None, ALU.add, None, [T], [T])
            TT(tB, lre, lre, ALU.mult, [T], [T])
            TT(tC, lim, lim, ALU.mult, [T], [T])
            TT(tB, tB, tC, ALU.add, [T], [T])
            RECIP(tB, tB, [T], [T])
            TT(qr, tA, lre, ALU.mult, [T], [T])
            TT(tC, PP[:, 1, 1, :], lim, ALU.mult, [T], [T])
            TT(qr, qr, tC, ALU.add, [T], [T])
            TT(qr, qr, tB, ALU.mult, [T], [T])
            TT(qi, PP[:, 1, 1, :], lre, ALU.mult, [T], [T])
            TT(tC, tA, lim, ALU.mult, [T], [T])
            TT(qi, qi, tC, ALU.subtract, [T], [T])
            TT(qi, qi, tB, ALU.mult, [T], [T])
            GQ = 2
            Bre, Bim = cf.get(64, GQ, 16), cf.get(64, GQ, 16)
            bbr, bbi = cf.get(64, GQ, 16), cf.get(64, GQ, 16)
            Cr, Ci = cf.get(64, GQ, 16), cf.get(64, GQ, 16)
            Fr, Fi = cf.get(64, GQ, 8, 16), cf.get(64, GQ, 8, 16)
            Er, Eni = cf.get(64, GQ, 8, 16), cf.get(64, GQ, 8, 16)
            Gr, Gi = cf.get(64, GQ, 8, 16), cf.get(64, GQ, 8, 16)
            u1, u2 = cf.get(64, GQ, 8, 16), cf.get(64, GQ, 8, 16)
            cst = cf.get(128, 64)
            Gb = cb.get(128, 2, GQ, 64)
            Wib = cb.get(128, GQ, 128)
            Wcb = cb.get(64, 2, GQ, 128)
            bc3 = lambda ap, n: ap.unsqueeze(2).to_broadcast([64, GQ, n])
            bc4 = lambda ap: ap.unsqueeze(2).unsqueeze(3).to_broadcast([64, GQ, 8, 16])
            fl = lambda ap: ap.rearrange("p s c -> p (s c)")
            for g0 in range(0, 64, GQ):
                gs = slice(g0, g0 + GQ)
                dma(Bre, b_re[l, gs].rearrange("g p c -> p g c"), w=[T])
                dma(Bim, b_im[l, gs].rearrange("g p c -> p g c"), w=[T])
                for src, dst in ((c_re, Cr), (c_im, Ci)):
                    dma(cst[0:GQ * 16, :], src[l, gs].rearrange("g c p -> (g c) p"), w=[T])
                    pi = next_pm()
                    TR(pm[pi][0:64, 0:GQ * 16], cst[0:GQ * 16, :], identf[0:GQ * 16, 0:GQ * 16], [T, "identf"], [PMK[pi]])
                    VCOPY(dst.rearrange("p g c -> p (g c)"), pm[pi][0:64, 0:GQ * 16], [], [PMK[pi], T])
                cmul(bbr, bbi, bc3(qr[:, gs], 16), bc3(qi[:, gs], 16), Bre, Bim, u1[:, :, 0, :], u2[:, :, 0, :], T)
                pw = lambda arr, ri: arr[:, 0:8, ri, gs].rearrange("p s g -> p g s").unsqueeze(3).to_broadcast([64, GQ, 8, 16])
                bs_ = lambda ap: ap.unsqueeze(2).to_broadcast([64, GQ, 8, 16])
                cmul(Fr, Fi, pw(PN, 0), pw(PN, 1), bs_(bbr), bs_(bbi), u1, u2, T)
                cmul(Er, Eni, pw(PP, 0), pw(PP, 1), bs_(Cr), bs_(Ci), u1, u2, T)
                cmul(Gr, Gi, bc4(PP[:, 7, 0, gs]), bc4(PP[:, 7, 1, gs]), Fr, Fi, u1, u2, T)
                for ri, Gx in ((0, Gr), (1, Gi)):
                    pi = next_pm()
                    for gg in range(GQ):
                        TR(pm[pi][:, gg * 64:(gg + 1) * 64], fl(Gx[:, gg]), identf[0:64, 0:64], [T, "identf"], [PMK[pi]])
                    ACOPY(Gb[:, ri].rearrange("p g q -> p (g q)"), pm[pi][:, 0:GQ * 64], [], [PMK[pi], "Gb"])
                dma(wst_scr[l, :, :, gs, :].rearrange("r k g p -> k r g p"), Gb, r=["Gb"], w=["wst_scr"])
                cmul(Gr, Gi, bc4(PN[:, 7, 0, gs]), bc4(PN[:, 7, 1, gs]), Er, Eni, u1, u2, T)
                ACOPY(Wcb[:, 0].rearrange("p g n -> p (g n)"), Gr.rearrange("p g s c -> p (g s c)"), [T], ["Wcb"])
                ACTF(Wcb[:, 1].rearrange("p g n -> p (g n)"), Gi.rearrange("p g s c -> p (g s c)"), AF.Copy, [T], ["Wcb"], scale=-1.0)
                dma(wc_scr[l, :, :, gs, :].rearrange("r p g n -> p r g n"), Wcb, r=["Wcb"], w=["wc_scr"])
                TS(Eni, Eni, -1.0, None, ALU.mult, None, [T], [T])
                pi = next_pm()
                for gg in range(GQ):
                    MM(pm[pi][:, gg * 128:(gg + 1) * 128], fl(Fr[:, gg]), fl(Er[:, gg]), True, False, [T], [PMK[pi]])
                    MM(pm[pi][:, gg * 128:(gg + 1) * 128], fl(Fi[:, gg]), fl(Eni[:, gg]), False, True, [T], [PMK[pi]])
                for gg in range(GQ):
                    g = g0 + gg
                    TT(Wib[:, gg, :], pm[pi][:, gg * 128:(gg + 1) * 128], imask[:], ALU.mult, ["imask"], [PMK[pi], "Wib"])
                    STT(Wib[:, gg, :], identb[:], dcol[:, l, g:g + 1], Wib[:, gg, :], ALU.mult, ALU.add,
                        ["identb", "dcol", "Wib"], ["Wib"])
                dma(wi_scr[l, :, gs, :], Wib, r=["Wib"], w=["wi_scr"])

        for l in range(DEPTH):
            ssm_prep(l)

        ALLT = ["xnT", "attnT", "yT", "U", "mergedT"]

        def fm_linear(ws_i, nk, rhsT, has_s, bm, bmk, bs, bsk, s_off):
            fm_linear_ap(wsm[ws_i], f"wsm{ws_i}", nk, rhsT, has_s, bm, bmk, bs, bsk, s_off)

        def fm_linear_ap(wap, wkey, nk, rhsT, has_s, bm, bmk, bs, bsk, s_off):
            for k in range(nk):
                MM(bm[:, 0:TC], wap[:, k, :], rhsT[:, k, 0:TC], k == 0, k == nk - 1, ALLT + [wkey], [bmk])
            if has_s:
                for k in range(nk):
                    MM(bs[:, s_off:s_off + NS], wap[:, k, :], rhsT[:, k, TC:NTOK], k == 0, k == nk - 1, ALLT + [wkey], [bsk])

        def norm_tile(x_ap, M, col0, junk, xn_tok, normw_bc):
            ACTF(junk[0:M, :], x_ap, AF.Square, ["xres"], ["junk", "stat"], accum_out=stat[0:M, 0:1])
            TS(stat[0:M, 1:2], stat[0:M, 0:1], 1.0 / D_MODEL, EPS, ALU.mult, ALU.add, ["stat"], ["stat"])
            ACTF(stat[0:M, 1:2], stat[0:M, 1:2], AF.Sqrt, ["stat"], ["stat"])
            RECIP(stat[0:M, 2:3], stat[0:M, 1:2], ["stat"], ["stat"])
            STT(xn_tok[0:M, :], x_ap, stat[0:M, 2:3], normw_bc[0:M, :], ALU.mult, ALU.mult, ["xres", "stat", "normw_bc"], ["xn_tok"])
            for kb in range(2):
                ti = next_pt()
                for kk in range(8):
                    k = kb * 8 + kk
                    TR(pt[ti][:, kk, 0:M], xn_tok[0:M, k * 128:(k + 1) * 128], identb[0:M, 0:M], ["xn_tok", "identb"], [PTK[ti]])
                ACOPY(xnT[:, kb * 8:(kb + 1) * 8, col0:col0 + M], pt[ti][:, :, 0:M], [], [PTK[ti], "xnT"])

        def head_norm_rope(s3, srck, M, nh, wn_bc, cosb, sinb, cs_slot, o3, ok, tmp):
            ksq, kn, t1, t2 = tmp
            ACTF(ksq[0:M, 0:nh], s3, AF.Square, [], [srck, "hn_ksq"])
            RSUM(stat[0:M, 4:4 + nh], ksq[0:M, 0:nh], ["hn_ksq"], ["stat"])
            TS(stat[0:M, 4:4 + nh], stat[0:M, 4:4 + nh], 1.0 / HD, EPS, ALU.mult, ALU.add, ["stat"], ["stat"])
            ACTF(stat[0:M, 4:4 + nh], stat[0:M, 4:4 + nh], AF.Sqrt, ["stat"], ["stat"])
            RECIP(stat[0:M, 4:4 + nh], stat[0:M, 4:4 + nh], ["stat"], ["stat"])
            TT(kn[0:M, 0:nh], s3, stat[0:M, 4:4 + nh].unsqueeze(2).to_broadcast([M, nh, HD]), ALU.mult, ["stat"], [srck, "hn_kn"])
            TT(kn[0:M, 0:nh], kn[0:M, 0:nh], wn_bc[0:M].unsqueeze(1).to_broadcast([M, nh, HD]), ALU.mult,
               ["hn_kn", "qnw_bc", "knw_bc"], ["hn_kn"])
            cbk = cosb[0:M, cs_slot].unsqueeze(1).to_broadcast([M, nh, 64])
            snk = sinb[0:M, cs_slot].unsqueeze(1).to_broadcast([M, nh, 64])
            x1 = kn[0:M, 0:nh, 0:64]
            x2 = kn[0:M, 0:nh, 64:128]
            TT(t1[0:M, 0:nh], x1, cbk, ALU.mult, ["hn_kn", "cs"], ["hn_t1"])
            TT(t2[0:M, 0:nh], x2, snk, ALU.mult, ["hn_kn", "cs"], ["hn_t2"])
            TT(o3[0:M, 0:nh, 0:64], t1[0:M, 0:nh], t2[0:M, 0:nh], ALU.subtract, ["hn_t1", "hn_t2"], [ok])
            TT(t1[0:M, 0:nh], x2, cbk, ALU.mult, ["hn_kn", "cs"], ["hn_t1"])
            TT(t2[0:M, 0:nh], x1, snk, ALU.mult, ["hn_kn", "cs"], ["hn_t2"])
            TT(o3[0:M, 0:nh, 64:128], t1[0:M, 0:nh], t2[0:M, 0:nh], ALU.add, ["hn_t1", "hn_t2"], [ok])

        def head_norm_rope_qk(s4, srckeys, M, wqk, cosb, sinb, cs_slot, o4, ok, tmp):
            ksq, kn, t1, t2, t3, t4 = tmp
            st = stat[0:M, 4:10].rearrange("p (a h) -> p a h", a=2)
            ACTF(ksq[0:M], s4, AF.Square, [], list(srckeys) + ["hn_ksq"])
            RSUM(st, ksq[0:M], ["hn_ksq"], ["stat"])
            TS(st, st, 1.0 / HD, EPS, ALU.mult, ALU.add, ["stat"], ["stat"])
            ACTF(st, st, AF.Sqrt, ["stat"], ["stat"])
            RECIP(st, st, ["stat"], ["stat"])
            TT(kn[0:M], s4, st.unsqueeze(3).to_broadcast([M, 2, 3, HD]), ALU.mult, ["stat"], list(srckeys) + ["hn_kn"])
            TT(kn[0:M], kn[0:M], wqk[0:M].unsqueeze(2).to_broadcast([M, 2, 3, HD]), ALU.mult, ["hn_kn", "wqk"], ["hn_kn"])
            cbk = cosb[0:M, cs_slot].unsqueeze(1).unsqueeze(1).to_broadcast([M, 2, 3, 64])
            snk = sinb[0:M, cs_slot].unsqueeze(1).unsqueeze(1).to_broadcast([M, 2, 3, 64])
            x1 = kn[0:M, :, :, 0:64]
            x2 = kn[0:M, :, :, 64:128]
            TT(t1[0:M], x1, cbk, ALU.mult, ["hn_kn", "cs"], ["hn_t1"])
            TT(t2[0:M], x2, snk, ALU.mult, ["hn_kn", "cs"], ["hn_t2"])
            TT(o4[0:M, :, :, 0:64], t1[0:M], t2[0:M], ALU.subtract, ["hn_t1", "hn_t2"], [ok])
            TT(t3[0:M], x2, cbk, ALU.mult, ["hn_kn", "cs"], ["hn_t1"])
            TT(t4[0:M], x1, snk, ALU.mult, ["hn_kn", "cs"], ["hn_t2"])
            TT(o4[0:M, :, :, 64:128], t3[0:M], t4[0:M], ALU.add, ["hn_t1", "hn_t2"], [ok + "b"])

        def phase_norm(l, ci, has_s):
            barrier()
            junk = cb.get(128, D_MODEL)
            xn_tok = cb.get(128, D_MODEL)
            normw_bc = cf.get(128, D_MODEL)
            dma(normw_bc, normw[l:l + 1, :].partition_broadcast(128), w=["normw_bc"])
            for tt in range(4):
                norm_tile(xres[:, tt, :], 128, tt * 128, junk, xn_tok, normw_bc)
            if has_s:
                norm_tile(xres[0:NS, 4, :], NS, TC, junk, xn_tok, normw_bc)
            if debug and ci == 0 and l == 0:
                dma(dbg["d_xnT"], xnT[:], r=["xnT"], q="pool")

        def phase_ssm(l, ci, has_s):
            barrier()
            UT = cb.get(65, 64, 8, 16)
            U = cb.get(128, 64, 65)
            Ys = cb.get(128, 64, 65)
            Hmov = cb.get(64, 2, 32, 65)
            wst = cb.get(128, 2, 8, 64)
            wib = cb.get(128, 8, 128)
            wcb = cb.get(64, 2, 8, 128)
            us = cb.get(NS, 1024)
            Sall = cf.get(64, 65, 2, 32)
            Tt = cf.get(64, 2, 32)
            M1 = cf.get(64, 2, 32)
            M2 = cf.get(64, 2, 32)
            h0 = cf.get(64, 64, 2)
            gA = cf.get(128, 260)
            gB = cf.get(128, 260)
            fin = cf.get(64, 32, 2)
            sT = U.rearrange("p g j -> p (g j)")[:, 0:8 * NTOK].rearrange("p (t n) -> p t n", t=8)
            xnTp = xnT[:, :, 0:TC].rearrange("p k (j s) -> p k s j", s=8)
            MSET(UT[64:65].rearrange("p g s c -> p (g s c)"), 0.0, [], ["UT"])
            for c0, wd in ((0, 384), (384, 384), (768, 256)):
                wi = load_w(wcols(l, O_U + c0, wd), wd)
                g0, ng = c0 // 16, wd // 16
                for s in range(8):
                    pi = next_pm()
                    for k in range(16):
                        MM(pm[pi][0:64, 0:wd], xnTp[:, k, s, :], wbuf[wi][:, k, 0:wd], k == 0, k == 15, ["xnT", f"wbuf{wi}"], [PMK[pi]])
                    ACOPY(UT[0:64, g0:g0 + ng, s, :], pm[pi][0:64, 0:wd].rearrange("p (g c) -> p g c", c=16), [], [PMK[pi], "UT"])
                if has_s:
                    pi = next_pm()
                    for k in range(16):
                        MM(pm[pi][0:NS, 0:wd], xnT[:, k, TC:NTOK], wbuf[wi][:, k, 0:wd], k == 0, k == 15, ["xnT", f"wbuf{wi}"], [PMK[pi]])
                    ACOPY(us[:, c0:c0 + wd], pm[pi][0:NS, 0:wd], [], [PMK[pi], "us"])
            if has_s:
                for tau in range(NS):
                    dma(UT[64:65, :, 4 + tau, :], us[tau:tau + 1, :].rearrange("p (g c) -> p g c", c=16), r=["us"], w=["UT"])
                dma(h0, sst[l].rearrange("g p r -> p g r"), w=["h0"])
            for gb in range(8):
                ti = next_pt()
                for gg in range(8):
                    g = gb * 8 + gg
                    TR(pt[ti][:, gg, 0:65], UT[0:65, g].rearrange("p s c -> p (s c)"), identb[0:65, 0:65], ["UT", "identb"], [PTK[ti]])
                ACOPY(U[:, gb * 8:(gb + 1) * 8, :], pt[ti][:, :, 0:65], [], [PTK[ti], "U"])
            if debug and ci == 0 and l == 0:
                dma(dbg["d_U"], U, r=["U"], q="pool")
            for hf in range(2):
                h0g = hf * 32
                for e8 in range(4):
                    ge = h0g + e8 * 8
                    dma(wst, wst_scr[l, :, :, ge:ge + 8, :].rearrange("r k g p -> k r g p"), r=["wst_scr"], w=["wst"])
                    for b3 in range(0, 8, 3):
                        ng = min(3, 8 - b3)
                        pi = next_pm()
                        for gg in range(ng):
                            for ri in range(2):
                                sl = (gg * 2 + ri) * 65
                                MM(pm[pi][0:64, sl:sl + 65], wst[:, ri, b3 + gg, :], U[:, ge + b3 + gg, :], True, True, ["wst", "U"], [PMK[pi]])
                        gl = e8 * 8 + b3
                        ACOPY(Sall[:, :, :, gl:gl + ng].rearrange("p j r g -> p g r j"),
                              pm[pi][0:64, 0:ng * 130].rearrange("p (g r j) -> p g r j", g=ng, r=2), [], [PMK[pi], "Sall"])
                Hv = Hst[:, l, :, h0g:h0g + 32]
                ar2 = A8r[:, l, :, h0g:h0g + 32]
                ai2 = A8i[:, l, :, h0g:h0g + 32]
                ACOPY(Hmov[:, :, :, 0], Hv, ["Hst"], ["Hmov"])
                for j in range(64):
                    TT(Tt, Hv, Sall[:, j, :, :], ALU.add, ["Hst", "Sall"], ["Tt"])
                    if j == 63 and ci == 7:
                        VCOPY(fin, Tt.rearrange("p r g -> p g r"), ["Tt"], ["fin"])
                        dma(ssm_p[l, h0g:h0g + 32].rearrange("g p r -> p g r"), fin, r=["fin"])
                    TT(M1, Tt, ar2, ALU.mult, ["Tt", "A8"], ["M1"])
                    TT(M2[:, 0, :], Tt[:, 1, :], ai2[:, 0, :], ALU.mult, ["Tt", "A8"], ["M2"])
                    TT(M2[:, 1, :], Tt[:, 0, :], ai2[:, 1, :], ALU.mult, ["Tt", "A8"], ["M2"])
                    TT(Hv, M1, M2, ALU.add, ["M1", "M2"], ["Hst"])
                    if j < 63:
                        ACOPY(Hmov[:, :, :, j + 1], Hv, ["Hst"], ["Hmov"])
                if has_s:
                    hr, hi = h0[:, h0g:h0g + 32, 0], h0[:, h0g:h0g + 32, 1]
                    a4r, a4i = A4[:, l, 0, h0g:h0g + 32], A4[:, l, 1, h0g:h0g + 32]
                    cmul(M1[:, 0, :], M1[:, 1, :], hr, hi, a4r, a4i, M2[:, 0, :], M2[:, 1, :], "M1", extra=["h0", "A8", "M2", "Tt"])
                    ACOPY(Hmov[:, :, :, 64], M1, ["M1", "h0", "A8"], ["Hmov"])
                    TT(Tt, M1, Sall[:, 64, :, :], ALU.add, ["M1", "Sall"], ["Tt"])
                    VCOPY(fin, Tt.rearrange("p r g -> p g r"), ["Tt"], ["fin"])
                    dma(ssm_s[l, h0g:h0g + 32].rearrange("g p r -> p g r"), fin, r=["fin"])
                else:
                    MSET(Hmov[:, :, :, 64], 0.0, [], ["Hmov"])
                for e8 in range(4):
                    ge = h0g + e8 * 8
                    dma(wib, wi_scr[l, :, ge:ge + 8, :], r=["wi_scr"], w=["wib"])
                    dma(wcb, wc_scr[l, :, :, ge:ge + 8, :].rearrange("r p g n -> p r g n"), r=["wc_scr"], w=["wcb"])
                    for b4 in range(2):
                        pi = next_pm()
                        for gg in range(4):
                            gi = b4 * 4 + gg
                            dst = pm[pi][:, gg * 65:(gg + 1) * 65]
                            MM(dst, wib[:, gi, :], U[:, ge + gi, :], True, False, ["wib", "U"], [PMK[pi]])
                            MM(dst, wcb[:, 0, gi, :], Hmov[:, 0, e8 * 8 + gi, :], False, False, ["wcb", "Hmov"], [PMK[pi]])
                            MM(dst, wcb[:, 1, gi, :], Hmov[:, 1, e8 * 8 + gi, :], False, True, ["wcb", "Hmov"], [PMK[pi]])
                        gq = ge + b4 * 4
                        ACOPY(gA, pm[pi][:, 0:260], [], [PMK[pi], "gA"])
                        TT(gB, gA, gA, ALU.mult, ["gA"], ["gB"])
                        TS(gB, gB, 0.044715, 1.0, ALU.mult, ALU.add, ["gB"], ["gB"])
                        TT(gB, gB, gA, ALU.mult, ["gA", "gB"], ["gB"])
                        ACTF(gB, gB, AF.Sigmoid, ["gB"], ["gB"], scale=1.5957691216057308)
                        TT(Ys[:, gq:gq + 4, :].rearrange("p g j -> p (g j)"), gA, gB, ALU.mult, ["gA", "gB"], ["Ys"])
            if debug and ci == 0 and l == 0:
                dma(dbg["d_Ys"], Ys, r=["Ys"], q="pool")
            for tl in range(8):
                for h in range(2):
                    for s4 in range(4):
                        sp = h * 4 + s4
                        for g8 in range(8):
                            MM(pa[h][:, s4 * 65:(s4 + 1) * 65], zzb[:, sp, 112 - 16 * g8:240 - 16 * g8], Ys[:, tl * 8 + g8, :],
                               g8 == 0, g8 == 7, ["zzb", "Ys"], [PAK[h]])
                    src = pa[h][:, 0:260].rearrange("p (s j) -> p s j", s=4)
                    ACOPY(sT[:, tl, h * 256:(h + 1) * 256].rearrange("p (s j) -> p s j", s=4), src[:, :, 0:64], [], [PAK[h], "U"])
                    if has_s and h == 1:
                        ACOPY(sT[:, tl, TC:NTOK], src[:, :, 64], [], [PAK[h], "U"])
            if debug and ci == 0 and l == 0:
                dma(dbg["d_sT"], sT, r=["U"], q="pool")
            return sT

        def phase_glu(l, ci, has_s, sT):
            t1 = cf.get(128, NTOK)
            t2 = cf.get(128, NTOK)
            for ob in range(8):
                wg = load_ws(w_glu[l][:, ob * 128:(ob + 1) * 128].rearrange("(k p) n -> p k n", p=128), 8)
                bA, bAk, bB, bBk = (pa[2], PAK[2], pm[0], PMK[0]) if ob % 2 == 0 else (pa[0], PAK[0], pa[1], PAK[1])
                so = 0 if ob % 2 == 0 else 16
                fm_linear(wg, 8, sT, has_s, bA, bAk, pa[3], PAK[3], so)
                ws_ = load_ws(wcols(l, O_GS + ob * 128, 128), 16)
                fm_linear(ws_, 16, xnT, has_s, bB, bBk, pa[3], PAK[3], so + 8)
                bias = bglu_t[:, l, ob:ob + 1]
                ACTF(t1[:, 0:TC], bA[:, 0:TC], AF.Sigmoid, ["bglu_t"], [bAk, "g_t1"], bias=bias)
                ACTF(t2[:, 0:TC], bB[:, 0:TC], AF.Silu, [], [bBk, "g_t2"])
                TT(t1[:, 0:TC], t1[:, 0:TC], sT[:, ob, 0:TC], ALU.mult, ["g_t1", "U"], ["g_t1"])
                TT(yT[:, ob, 0:TC].rearrange("p (j s) -> p s j", s=8), t1[:, 0:TC].rearrange("p (s j) -> p s j", s=8),
                   t2[:, 0:TC].rearrange("p (j s) -> p s j", s=8), ALU.mult, ["g_t1", "g_t2"], ["yT"])
                if has_s:
                    ACTF(t1[:, TC:NTOK], pa[3][:, so:so + NS], AF.Sigmoid, ["bglu_t"], [PAK[3], "g_t1"], bias=bias)
                    ACTF(t2[:, TC:NTOK], pa[3][:, so + 8:so + 8 + NS], AF.Silu, [], [PAK[3], "g_t2"])
                    TT(t1[:, TC:NTOK], t1[:, TC:NTOK], sT[:, ob, TC:NTOK], ALU.mult, ["g_t1", "U"], ["g_t1"])
                    TT(yT[:, ob, TC:NTOK], t1[:, TC:NTOK], t2[:, TC:NTOK], ALU.mult, ["g_t1", "g_t2"], ["yT"])
            if debug and ci == 0 and l == 0:
                dma(dbg["d_yT"], yT[:], r=["yT"], q="pool")

        def phase_attn(l, ci, has_s):
            barrier()
            t0 = ci * TC
            mk = [cb.get(128, 896 + WINDOWS[g]) for g in range(3)]
            qbs = [cb.get(128, 2, 3, 128) for _ in range(2)]
            qT = cb.get(128, 3, NTOK)
            kT = cb.get(128, 3, NTOK)
            vtok = cb.get(128, 5, 3, 128)
            kTh = [cb.get(128, WINDOWS[g]) for g in range(3)]
            vh = [cb.get(128, WINDOWS[g] // 128, 128) for g in range(3)]
            Eb = [cb.get(128, TC) for _ in range(2)]
            Pb = [cb.get(128, TC) for _ in range(2)]
            cosb = cf.get(128, 5, 64)
            sinb = cf.get(128, 5, 64)
            tmp = (cf.get(128, 2, 3, 128), cf.get(128, 2, 3, 128), cf.get(128, 2, 3, 64), cf.get(128, 2, 3, 64),
                   cf.get(128, 2, 3, 64), cf.get(128, 2, 3, 64))
            ko = [cf.get(128, 2, 3, 128) for _ in range(2)]
            vo = [cf.get(128, 3, 128) for _ in range(2)]
            ao = cf.get(128, NTOK)
            rl = cf.get(128, NTOK)
            sg = cf.get(128, NTOK)
            if has_s:
                kc = cb.get(128, 21, 128)
                kTc = cb.get(128, 21, 128)
                vc = cb.get(128, 21, 128)
                smk = cb.get(128, 21, NS)
                snw = cb.get(NS, 3, NS)
                Es = cb.get(128, 96)
                Ps = cb.get(128, 96)
                toff = (0, 1, 5)
                for g in range(3):
                    nt = WINDOWS[g] // 128
                    dma(smk[:, toff[g]:toff[g] + nt, :], cd[f"smask{g}"], w=["smk"], q="pool")
                dma(snw, cd["snew"], w=["snw"], q="pool")
            for g in range(3):
                dma(mk[g], cd[f"mask{g}"], w=[f"mk{g}"], q="pool", max_dma_last_dim=4096)
            dma(cosb[:, 0:4, :], cd["cos_t"][t0:t0 + TC].rearrange("(t p) d -> p t d", p=128), w=["cs"])
            dma(sinb[:, 0:4, :], cd["sin_t"][t0:t0 + TC].rearrange("(t p) d -> p t d", p=128), w=["cs"])
            if has_s:
                dma(cosb[0:NS, 4, :], cd["cos_t"][SEQ:SEQ + NS], w=["cs"])
                dma(sinb[0:NS, 4, :], cd["sin_t"][SEQ:SEQ + NS], w=["cs"])
            tiles = [(tt, 128, tt * 128, tt) for tt in range(4)] + ([(4, NS, TC, 4)] if has_s else [])
            kcnt = [0]
            BQ = [Bf[:, 0:2, :], Bf[:, 2:4, :]]
            BQK = [[PMK[0], PMK[1]], [PAK[0], PAK[1]]]
            wqk = cf.get(128, 2, HD)
            VCOPY(wqk[:, 0, :], qnw_bc[:, l, :], ["qnw_bc"], ["wqk"])
            VCOPY(wqk[:, 1, :], knw_bc[:, l, :], ["knw_bc"], ["wqk"])
            nxt_w = [0, 0]

            def issue_qk(jj):
                out = []
                for base in (O_Q, O_K):
                    wi = cnt["w"] % 2
                    cnt["w"] += 1
                    for g in range(3):
                        dma(wbuf[wi][:, :, g * 128:(g + 1) * 128], wcols(l, base + g * 1024 + jj * 128, 128), w=[f"wbuf{wi}"], q="pool", nobar=True)
                    out.append(wi)
                return out

            for j in range(NH):
                pend = []

                def flush():
                    while pend:
                        pend.pop(0)()

                if j == 0:
                    nxt_w[:] = issue_qk(0)
                wis = list(nxt_w)
                ws_ga = load_ws(wcols(l, O_GA + j * 128, 128), 16)
                for tt, M, col0, cslot in tiles:
                    bq = kcnt[0] % 2
                    kcnt[0] += 1
                    pair, pkeys = BQ[bq], BQK[bq]
                    for a in range(2):
                        for k in range(16):
                            MM(pair[0:M, a, 0:384], xnT[:, k, col0:col0 + M], wbuf[wis[a]][:, k, 0:384], k == 0, k == 15,
                               ["xnT", f"wbuf{wis[a]}"], [pkeys[a]])
                    flush()
                    s4 = pair[0:M, :, 0:384].rearrange("p a (h d) -> p a h d", h=3)
                    kob, kok = ko[bq], f"ko{bq}"
                    head_norm_rope_qk(s4, pkeys, M, wqk, cosb, sinb, cslot, kob, kok, tmp)
                    qbi, qbk = qbs[bq], f"qb{bq}"
                    ACOPY(qbi[0:M], kob[0:M], [kok, kok + "b"], [qbk])
                    for g in range(3):
                        W = WINDOWS[g]
                        if tt < 4:
                            r0 = t0 + tt * 128 - (SEQ - W)
                            if r0 >= 0:
                                dma(kvp[g][l, r0:r0 + 128, 0, j, :], kob[:, 1, g, :], r=[kok, kok + "b"])
                        else:
                            dma(kvs[g][l, W - NS:W, 0, j, :], kob[0:NS, 1, g, :], r=[kok, kok + "b"], w=[f"kvs{g}"])

                    def later(M=M, col0=col0, qbi=qbi, qbk=qbk):
                        ti = next_pt()
                        for a in range(2):
                            for g in range(3):
                                TR(pt[ti][:, a * 3 + g, 0:M], qbi[0:M, a, g, :], identb[0:M, 0:M], [qbk, "identb"], [PTK[ti]])
                        ACOPY(qT[:, :, col0:col0 + M], pt[ti][:, 0:3, 0:M], [], [PTK[ti], "qT"])
                        ACOPY(kT[:, :, col0:col0 + M], pt[ti][:, 3:6, 0:M], [], [PTK[ti], "kT"])
                    pend.append(later)
                wi = cnt["w"] % 2
                cnt["w"] += 1
                for g in range(3):
                    dma(wbuf[wi][:, :, g * 128:(g + 1) * 128], wcols(l, O_V + g * 1024 + j * 128, 128), w=[f"wbuf{wi}"], q="pool", nobar=True)
                for tt, M, col0, cslot in tiles:
                    pi = next_pm()
                    for k in range(16):
                        MM(pm[pi][0:M, 0:384], xnT[:, k, col0:col0 + M], wbuf[wi][:, k, 0:384], k == 0, k == 15,
                           ["xnT", f"wbuf{wi}"], [PMK[pi]])
                    flush()
                    vb = kcnt[0] % 2
                    kcnt[0] += 1
                    vob, vok = vo[vb], f"vo{vb}"
                    ACOPY(vob[0:M].rearrange("p h d -> p (h d)"), pm[pi][0:M, 0:384], [], [PMK[pi], vok])
                    VCOPY(vtok[0:M, tt], vob[0:M], [vok], ["vtok"])
                    for g in range(3):
                        W = WINDOWS[g]
                        if tt < 4:
                            r0 = t0 + tt * 128 - (SEQ - W)
                            if r0 >= 0:
                                dma(kvp[g][l, r0:r0 + 128, 1, j, :], vob[:, g, :], r=[vok])
                        else:
                            dma(kvs[g][l, W - NS:W, 1, j, :], vob[0:NS, g, :], r=[vok], w=[f"kvs{g}"])
                flush()
                if j + 1 < NH:
                    nxt_w[:] = issue_qk(j + 1)
                for g in range(3):
                    dma(kT_scr[l, g * 8 + j, :, t0:t0 + TC], kT[:, g, 0:TC], r=["kT"], w=["kscr"])
                    dma(v_scr[l, g * 8 + j, t0:t0 + TC, :].rearrange("(t p) d -> p t d", p=128), vtok[:, 0:4, g, :],
                        r=["vtok"], w=["vscr"])
                nht = [min(WINDOWS[g], t0) // 128 for g in range(3)]
                for g in range(3):
                    if nht[g]:
                        h = g * 8 + j
                        lo = t0 - nht[g] * 128
                        dma(kTh[g][:, 0:nht[g] * 128], kT_scr[l, h, :, lo:t0], r=["kscr"], w=[f"kTh{g}"])
                        dma(vh[g][:, 0:nht[g], :], v_scr[l, h, lo:t0, :].rearrange("(t p) d -> p t d", p=128), r=["vscr"], w=[f"vh{g}"])
                klist = []
                for g in range(3):
                    for i in range(nht[g]):
                        klist.append((g, -(nht[g] - i) * 128, kTh[g][:, i * 128:(i + 1) * 128], vh[g][:, i, :], [f"kTh{g}", f"vh{g}"]))
                    for tt in range(4):
                        klist.append((g, tt * 128, kT[:, g, tt * 128:(tt + 1) * 128], vtok[:, tt, g, :], ["kT", "vtok"]))
                def score(idx):
                    g, o, Kap, Vap, keys = klist[idx]
                    ai = idx % 2
                    MM(pa[ai][:, 0:TC], Kap, qT[:, g, 0:TC], True, True, ["qT"] + keys, [PAK[ai]])
                    ACTF(Eb[ai], pa[ai][:, 0:TC], AF.Exp, [], [PAK[ai], f"Eb{ai}"], scale=SCALE)
                    TT(Pb[ai], Eb[ai], mk[g][:, C0 - o:C0 - o + TC], ALU.mult, [f"Eb{ai}", f"mk{g}"], [f"Pb{ai}"],
                       eng=("pool" if idx % 2 == 0 else "dve"))

                def pv(idx):
                    g, o, Kap, Vap, keys = klist[idx]
                    ai = idx % 2
                    first, last = idx == 0, idx == len(klist) - 1
                    MM(pa[2][:, 0:TC], Vap, Pb[ai], first, last, [f"Pb{ai}"] + keys, [PAK[2]])
                    MM(pa[3][:, 0:TC], onesb[:], Pb[ai], first, last, [f"Pb{ai}", "onesb"], [PAK[3]])

                score(0)
                for idx in range(len(klist)):
                    if idx + 1 < len(klist):
                        score(idx + 1)
                    pv(idx)
                RECIP(rl[:, 0:TC], pa[3][:, 0:TC], [], [PAK[3], "rl"])
                TT(ao[:, 0:TC], pa[2][:, 0:TC], rl[:, 0:TC], ALU.mult, ["rl"], [PAK[2], "ao"])
                fm_linear(ws_ga, 16, xnT, has_s, pm[0], PMK[0], pm[1], PMK[1], 0)
                ACTF(sg[:, 0:TC], pm[0][:, 0:TC], AF.Silu, [], [PMK[0], "sg"])
                TT(attnT[:, j, 0:TC], ao[:, 0:TC], sg[:, 0:TC], ALU.mult, ["ao", "sg"], ["attnT"])
                if has_s:
                    for g in range(3):
                        nt = WINDOWS[g] // 128
                        o_ = toff[g]
                        dma(kc[:, o_:o_ + nt, :], cks[g][l, :, 0, j, :].rearrange("(t p) d -> p t d", p=128), w=["kc"], q="pool")
                        dma(vc[:, o_:o_ + nt, :], cks[g][l, :, 1, j, :].rearrange("(t p) d -> p t d", p=128), w=["vc"], q="pool")
                    for b0 in range(0, 21, 8):
                        n = min(8, 21 - b0)
                        ti = next_pt()
                        for t in range(n):
                            TR(pt[ti][:, t, :], kc[:, b0 + t, :], identb[:], ["kc", "identb"], [PTK[ti]])
                        ACOPY(kTc[:, b0:b0 + n, :], pt[ti][:, 0:n, :], [], [PTK[ti], "kTc"])
                    for g in range(3):
                        nt = WINDOWS[g] // 128
                        for t in range(nt):
                            c = (toff[g] + t) * NS
                            MM(pa[0][:, c:c + NS], kTc[:, toff[g] + t, :], qT[:, g, TC:NTOK], True, True, ["kTc", "qT"], [PAK[0]])
                        MM(pa[0][0:NS, 84 + g * NS:84 + (g + 1) * NS], kT[:, g, TC:NTOK], qT[:, g, TC:NTOK], True, True, ["kT", "qT"], [PAK[0]])
                    ACTF(Es[:, 0:84], pa[0][:, 0:84], AF.Exp, [], [PAK[0], "Es"], scale=SCALE)
                    ACTF(Es[0:NS, 84:96], pa[0][0:NS, 84:96], AF.Exp, [], [PAK[0], "Es"], scale=SCALE)
                    TT(Ps[:, 0:84], Es[:, 0:84], smk.rearrange("p t q -> p (t q)"), ALU.mult, ["Es", "smk"], ["Ps"])
                    TT(Ps[0:NS, 84:96], Es[0:NS, 84:96], snw.rearrange("p g q -> p (g q)"), ALU.mult, ["Es", "snw"], ["Ps"])
                    for dst0, use_v in ((16, True), (24, False)):
                        items = []
                        for g in range(3):
                            nt = WINDOWS[g] // 128
                            for t in range(nt):
                                c = (toff[g] + t) * NS
                                items.append((vc[:, toff[g] + t, :] if use_v else onesb[:], Ps[:, c:c + NS]))
                            items.append((vtok[0:NS, 4, g, :] if use_v else onesb[0:NS, :], Ps[0:NS, 84 + g * NS:84 + (g + 1) * NS]))
                        for ii, (lh, rh) in enumerate(items):
                            MM(pm[1][:, dst0:dst0 + NS], lh, rh, ii == 0, ii == len(items) - 1, ["vc", "vtok", "Ps", "onesb"], [PMK[1]])
                    RECIP(rl[:, TC:NTOK], pm[1][:, 24:24 + NS], [], [PMK[1], "rl"])
                    TT(ao[:, TC:NTOK], pm[1][:, 16:16 + NS], rl[:, TC:NTOK], ALU.mult, ["rl"], [PMK[1], "ao"])
                    ACTF(sg[:, TC:NTOK], pm[1][:, 0:NS], AF.Silu, [], [PMK[1], "sg"])
                    TT(attnT[:, j, TC:NTOK], ao[:, TC:NTOK], sg[:, TC:NTOK], ALU.mult, ["ao", "sg"], ["attnT"])
            if debug and ci == 0 and l == 0:
                dma(dbg["d_attnT"], attnT[:], r=["attnT"], q="pool")

        def phase_merge(l, ci, has_s):
            barrier()
            mergedT = cb.get(128, 16, NTOK)
            m1 = cf.get(128, NTOK)
            m2 = cf.get(128, NTOK)
            cols = [(slice(0, TC), None)] + ([(slice(TC, NTOK), 0)] if has_s else [])
            slots = [(wsm[0], "wsm0"), (wsm[1], "wsm1")] + \
                    [(wbuf[i][:, :, s_ * 128:(s_ + 1) * 128], f"wbuf{i}_{s_}") for i in range(2) for s_ in range(3)]
            scnt = [0]

            def load_slot(src_ap, nk):
                ap_, key_ = slots[scnt[0] % len(slots)]
                scnt[0] += 1
                dma(ap_[:, 0:nk, :], src_ap, w=[key_], q="pool")
                return ap_, key_

            for mb in range(16):
                cs_ = slice(mb * 128, (mb + 1) * 128)
                wa, wak = load_slot(w_bra[l][:, cs_].rearrange("(k p) n -> p k n", p=128), 8)
                wb, wbk = load_slot(w_brs[l][:, cs_].rearrange("(k p) n -> p k n", p=128), 8)
                wma, wmak = load_slot(wcols(l, O_MA + mb * 128, 128), 16)
                wms, wmsk = load_slot(wcols(l, O_MS + mb * 128, 128), 16)
                fm_linear_ap(wa, wak, 8, attnT, has_s, pm[0], PMK[0], pm[1], PMK[1], 0)
                fm_linear_ap(wb, wbk, 8, yT, has_s, pa[0], PAK[0], pm[1], PMK[1], 4)
                fm_linear_ap(wma, wmak, 16, xnT, has_s, pa[1], PAK[1], pm[1], PMK[1], 8)
                fm_linear_ap(wms, wmsk, 16, xnT, has_s, pa[2], PAK[2], pm[1], PMK[1], 12)
                ACTF(m1[:, 0:TC], pa[1][:, 0:TC], AF.Sigmoid, [], [PAK[1], "m1"])
                ACTF(m2[:, 0:TC], pa[2][:, 0:TC], AF.Sigmoid, [], [PAK[2], "m2"])
                TT(m1[:, 0:TC], pm[0][:, 0:TC], m1[:, 0:TC], ALU.mult, ["m1"], [PMK[0], "m1"])
                TT(m2[:, 0:TC], pa[0][:, 0:TC], m2[:, 0:TC], ALU.mult, ["m2"], [PAK[0], "m2"])
                TT(mergedT[:, mb, 0:TC], m1[:, 0:TC], m2[:, 0:TC], ALU.add, ["m1", "m2"], ["mergedT"])
                if has_s:
                    sc = slice(TC, NTOK)
                    ACTF(m1[:, sc], pm[1][:, 8:12], AF.Sigmoid, [], [PMK[1], "m1"])
                    ACTF(m2[:, sc], pm[1][:, 12:16], AF.Sigmoid, [], [PMK[1], "m2"])
                    TT(m1[:, sc], pm[1][:, 0:4], m1[:, sc], ALU.mult, ["m1"], [PMK[1], "m1"])
                    TT(m2[:, sc], pm[1][:, 4:8], m2[:, sc], ALU.mult, ["m2"], [PMK[1], "m2"])
                    TT(mergedT[:, mb, sc], m1[:, sc], m2[:, sc], ALU.add, ["m1", "m2"], ["mergedT"])
            if debug and ci == 0 and l == 0:
                dma(dbg["d_mergedT"], mergedT, r=["mergedT"], q="pool")
            tiles = [(tt, 128, tt * 128) for tt in range(4)] + ([(4, NS, TC)] if has_s else [])
            for c0, wd in ((0, 384), (384, 384), (768, 384), (1152, 384), (1536, 384), (1920, 128)):
                wi = load_w(w_out[l][:, c0:c0 + wd].rearrange("(k p) n -> p k n", p=128), wd, extra=True)
                for tt, M, col0 in tiles:
                    pi = next_pm()
                    for k in range(16):
                        MM(pm[pi][0:M, 0:wd], mergedT[:, k, col0:col0 + M], wbuf[wi][:, k, 0:wd], k == 0, k == 15,
                           ["mergedT", f"wbuf{wi}"], [PMK[pi]])
                    TT(xres[0:M, tt, c0:c0 + wd], pm[pi][0:M, 0:wd], xres[0:M, tt, c0:c0 + wd], ALU.add, ["xres"], [PMK[pi], "xres"])

        for ci in range(nch):
            has_s = ci == 0
            t0 = ci * TC
            dma(xres[:, 0:4, :], xp[t0:t0 + TC].rearrange("(t p) d -> p t d", p=128), w=["xres"])
            if has_s:
                dma(xres[0:NS, 4, :], xs, w=["xres"])
            for l in range(DEPTH):
                phase_norm(l, ci, has_s)
                sT = phase_ssm(l, ci, has_s)
                phase_glu(l, ci, has_s, sT)
                phase_attn(l, ci, has_s)
                phase_merge(l, ci, has_s)
            dma(y_p[t0:t0 + TC].rearrange("(t p) d -> p t d", p=128), xres[:, 0:4, :], r=["xres"])
            if has_s:
                dma(y_s, xres[0:NS, 4, :], r=["xres"])
        P.emit(nc, es)
    return nc, consts


_CACHE = {}


def _core_inputs(c, I, consts):
    f = lambda a: np.ascontiguousarray(np.asarray(a, dtype=np.float32))
    m = dict(xp=f(I["x_prompt"][c % 2]), xs=f(I["x_sample"][c]),
             ckv0=f(I["cache_kv_d1"][:, c]), ckv1=f(I["cache_kv_d4"][:, c]), ckv2=f(I["cache_kv_d16"][:, c]),
             sst=f(I["state_ssm"][:, c]))
    for k_, n_ in (("w_in", "w_in"), ("w_glu", "w_glu"), ("w_bra", "w_br_attn"), ("w_brs", "w_br_ssm"), ("w_out", "w_out"),
                   ("normw", "norm_w"), ("qnw", "q_norm_w"), ("knw", "k_norm_w"), ("b_glu", "b_glu"), ("ssm_d", "ssm_d"),
                   ("lam_re", "ssm_lambda_re"), ("lam_im", "ssm_lambda_im"), ("log_dt", "ssm_log_dt"),
                   ("b_re", "ssm_b_re"), ("b_im", "ssm_b_im"), ("c_re", "ssm_c_re"), ("c_im", "ssm_c_im")):
        m[k_] = I["_shared"][n_]
    m.update(consts)
    return m


def kernel(**I):
    f = lambda a: np.ascontiguousarray(np.asarray(a, dtype=np.float32))
    if "nc" not in _CACHE:
        _CACHE["nc"] = build_program()
    nc, consts = _CACHE["nc"]
    I = dict(I)
    I["_shared"] = {n: f(I[n]) for n in ("w_in", "w_glu", "w_br_attn", "w_br_ssm", "w_out", "norm_w", "q_norm_w", "k_norm_w",
                                           "b_glu", "ssm_d", "ssm_lambda_re", "ssm_lambda_im", "ssm_log_dt", "ssm_b_re",
                                           "ssm_b_im", "ssm_c_re", "ssm_c_im")}
    in_maps = [_core_inputs(c, I, consts) for c in range(8)]
    res = run_bass_kernel_spmd(nc, in_maps, core_ids=list(range(8)))
    R = res.results
    B = I["x_prompt"].shape[0]
    y_prompt = np.stack([R[b]["y_p"] for b in range(B)], axis=0)
    y_sample = np.stack([R[b]["y_s"] for b in range(8)], axis=0)
    kvp = [np.stack([R[b][f"kvp{i}"] for b in range(B)], axis=1) for i in range(3)]
    kvs = [np.stack([R[b][f"kvs{i}"] for b in range(8)], axis=1) for i in range(3)]
    ssm_p = np.stack([R[b]["ssm_p"] for b in range(B)], axis=1)
    ssm_s = np.stack([R[b]["ssm_s"] for b in range(8)], axis=1)
    return (y_prompt, y_sample, kvp[0], kvp[1], kvp[2], ssm_p, kvs[0], kvs[1], kvs[2], ssm_s)
```

```python
import numpy as np
from contextlib import ExitStack
import concourse.bass as bass
import concourse.mybir as mybir
from concourse.bass_utils import run_bass_kernel_spmd

F32 = mybir.dt.float32
BF16 = mybir.dt.bfloat16
AF = mybir.ActivationFunctionType
ALU = mybir.AluOpType
AX = mybir.AxisListType

ENGINES = ["pe", "act", "dve", "pool", "sp"]
EPOCH = 20000
NSLOT = 8


class Op:
    __slots__ = ("eng", "fn", "dma", "idx", "deps", "dma_deps", "signal", "count", "slot", "target")

    def __init__(self, eng, fn, dma, idx):
        self.eng = eng
        self.fn = fn
        self.dma = dma
        self.idx = idx
        self.deps = {}
        self.dma_deps = set()
        self.signal = False
        self.count = 0
        self.slot = None
        self.target = 0


class Prog:
    def __init__(self):
        self.ops = {e: [] for e in ENGINES}
        self.lastw = {}
        self.readers = {}
        self.ndma = {e: 0 for e in ENGINES}
        self.all_dma = []

    def op(self, eng, fn, reads=(), writes=(), dma=False):
        o = Op(eng, fn, dma, len(self.ops[eng]))
        if "__ph" not in writes and "__nobar" not in writes:
            reads = list(reads) + ["__ph"]
        writes = [w for w in writes if w != "__nobar"]

        def add(p):
            if p is None or p is o:
                return
            if p.dma:
                o.dma_deps.add(p)
            else:
                if p.eng == "pe" and eng == "pe" and not dma:
                    return
                cur = o.deps.get(p.eng)
                if cur is None or p.idx > cur.idx:
                    o.deps[p.eng] = p

        for r in reads:
            add(self.lastw.get(r))
        for w in writes:
            add(self.lastw.get(w))
            for rd in self.readers.get(w, ()):
                add(rd)
        for r in reads:
            self.readers.setdefault(r, []).append(o)
        for w in writes:
            self.lastw[w] = o
            self.readers[w] = []
        if dma:
            i = self.ndma[eng]
            self.ndma[eng] += 1
            o.slot = (eng, i % NSLOT)
            o.target = 16 * (i // NSLOT + 1)
            self.all_dma.append(o)
        self.ops[eng].append(o)
        return o

    def dma_in(self, eng, out, in_, reads=(), writes=(), **kw):
        return self.op(eng, lambda e: e.dma_start(out=out, in_=in_, **kw), reads, writes, dma=True)

    def emit(self, nc, es):
        for e in ENGINES:
            for o in self.ops[e]:
                for p in o.deps.values():
                    p.signal = True
        sems = {}
        for e in ENGINES:
            n = 0
            for o in self.ops[e]:
                if o.signal:
                    n += 1
                    o.count = n
            nep = n // EPOCH + 1
            sems[e] = [es.enter_context(nc.semaphore(f"s_{e}_{k}")) for k in range(nep)]
        slot_sems = {}
        for e in ENGINES:
            if self.ndma[e]:
                for k in range(NSLOT):
                    slot_sems[(e, k)] = es.enter_context(nc.semaphore(f"d_{e}_{k}"))
        final_targets = {}
        for o in self.all_dma:
            final_targets[o.slot] = max(final_targets.get(o.slot, 0), o.target)

        def run_engine(ename, handle):
            waited = {}
            dwaited = {}
            for o in self.ops[ename]:
                for pe_, p in o.deps.items():
                    if waited.get(pe_, 0) >= p.count:
                        continue
                    waited[pe_] = p.count
                    k, v = divmod(p.count - 1, EPOCH)
                    handle.wait_ge(sems[pe_][k], v + 1)
                for p in o.dma_deps:
                    if dwaited.get(p.slot, 0) >= p.target:
                        continue
                    dwaited[p.slot] = p.target
                    handle.wait_ge(slot_sems[p.slot], p.target)
                if o.dma and o.target > 16:
                    if dwaited.get(o.slot, 0) < o.target - 16:
                        dwaited[o.slot] = o.target - 16
                        handle.wait_ge(slot_sems[o.slot], o.target - 16)
                ins = o.fn(handle)
                if o.dma:
                    ins.then_inc(slot_sems[o.slot], 16)
                elif o.signal:
                    k, v = divmod(o.count - 1, EPOCH)
                    ins.then_inc(sems[ename][k], 1)
            if ename == "sp":
                for slot, tgt in final_targets.items():
                    if dwaited.get(slot, 0) < tgt:
                        handle.wait_ge(slot_sems[slot], tgt)

        with nc.Block() as block:
            @block.tensor
            def _(h):
                run_engine("pe", h)

            @block.scalar
            def _(h):
                run_engine("act", h)

            @block.vector
            def _(h):
                run_engine("dve", h)

            @block.gpsimd
            def _(h):
                run_engine("pool", h)

            @block.sync
            def _(h):
                run_engine("sp", h)


D_MODEL = 2048
SEQ = 4096
DEPTH = 2
NS = 4
PAST = 16384
HD = 128
NH = 8
WINDOWS = (128, 512, 2048)
DILS = (1, 4, 16)
QKV = 3072
TC = 512
NTOK = TC + NS
EPS = 1e-6
O_Q, O_K, O_V, O_GA, O_U, O_GS, O_MA, O_MS = 0, 3072, 6144, 9216, 10240, 11264, 12288, 14336
IN_COLS = 16384
SCALE = float(HD) ** -0.5
C0 = 384
TWO_PI = 6.283185307179586
CW1 = 6.28125
CW2 = TWO_PI - CW1
MAGIC = 12582912.0


def _rope_tables():
    half = HD // 2
    inv = np.power(np.float32(10000.0), -np.arange(half, dtype=np.float32) * np.float32(2.0 / HD)).astype(np.float32)
    pos = np.concatenate([np.arange(SEQ, dtype=np.float32), PAST + np.arange(NS, dtype=np.float32)])
    ang = (pos[:, None] * inv[None, :]).astype(np.float32)
    return np.cos(ang).astype(np.float32), np.sin(ang).astype(np.float32)


def _const_inputs():
    c = {}
    c["cos_t"], c["sin_t"] = _rope_tables()
    c["ident"] = np.eye(128, dtype=np.float32)
    k = np.arange(128)[:, None]
    for g in range(3):
        wid = 896 + WINDOWS[g]
        d = (np.arange(wid)[None, :] - C0) - k
        c[f"mask{g}"] = ((d >= 0) & (d <= WINDOWS[g]) & (d % DILS[g] == 0)).astype(np.float32)
        nt = WINDOWS[g] // 128
        e = (np.arange(nt)[None, :, None] * 128 + k[:, :, None])
        tq = np.arange(NS)[None, None, :]
        dd = WINDOWS[g] + tq - e
        c[f"smask{g}"] = ((dd >= 0) & (dd <= WINDOWS[g]) & (dd % DILS[g] == 0)).astype(np.float32)
    sn = np.zeros((NS, 3, NS), np.float32)
    for g in range(3):
        for e in range(NS):
            for t in range(NS):
                dd = t - e
                sn[e, g, t] = float(dd >= 0 and dd % DILS[g] == 0 and dd <= WINDOWS[g])
    c["snew"] = sn
    s_idx = np.arange(128) // 16
    c["imask"] = (s_idx[:, None] <= s_idx[None, :]).astype(np.float32)
    zz = np.zeros((128, 8, 240), np.float32)
    for sp in range(8):
        for cc in range(16):
            zz[sp * 16 + cc, sp, 112 + cc] = 1.0
    c["zz"] = zz
    return c


CONST_SHAPES = None


def build_program(nch=8, debug=False):
    nc = bass.Bass("TRN2", target_bir_lowering=False)
    consts = _const_inputs()
    din = lambda n, s: nc.dram_tensor(n, list(s), F32, kind="ExternalInput").ap()
    dout = lambda n, s: nc.dram_tensor(n, list(s), F32, kind="ExternalOutput").ap()
    xp = din("xp", [SEQ, D_MODEL])
    xs = din("xs", [NS, D_MODEL])
    cks = [din(f"ckv{i}", [DEPTH, WINDOWS[i], 2, NH, HD]) for i in range(3)]
    sst = din("sst", [DEPTH, 64, 64, 2])
    w_in = din("w_in", [DEPTH, D_MODEL, IN_COLS])
    w_glu = din("w_glu", [DEPTH, 1024, 1024])
    w_bra = din("w_bra", [DEPTH, 1024, D_MODEL])
    w_brs = din("w_brs", [DEPTH, 1024, D_MODEL])
    w_out = din("w_out", [DEPTH, D_MODEL, D_MODEL])
    normw = din("normw", [DEPTH, D_MODEL])
    qnw = din("qnw", [DEPTH, HD])
    knw = din("knw", [DEPTH, HD])
    b_glu = din("b_glu", [DEPTH, 1024])
    ssm_d = din("ssm_d", [DEPTH, 1024])
    lam_re = din("lam_re", [DEPTH, 64, 64])
    lam_im = din("lam_im", [DEPTH, 64, 64])
    log_dt = din("log_dt", [DEPTH, 64])
    b_re = din("b_re", [DEPTH, 64, 64, 16])
    b_im = din("b_im", [DEPTH, 64, 64, 16])
    c_re = din("c_re", [DEPTH, 64, 16, 64])
    c_im = din("c_im", [DEPTH, 64, 16, 64])
    cd = {k: din(k, v.shape) for k, v in consts.items()}
    y_p = dout("y_p", [SEQ, D_MODEL])
    y_s = dout("y_s", [NS, D_MODEL])
    kvp = [dout(f"kvp{i}", [DEPTH, WINDOWS[i], 2, NH, HD]) for i in range(3)]
    kvs = [dout(f"kvs{i}", [DEPTH, WINDOWS[i], 2, NH, HD]) for i in range(3)]
    ssm_p = dout("ssm_p", [DEPTH, 64, 64, 2])
    ssm_s = dout("ssm_s", [DEPTH, 64, 64, 2])
    dbg = {}
    if debug:
        for n_, s_ in [("d_xnT", [128, 16, NTOK]), ("d_sT", [128, 8, NTOK]), ("d_yT", [128, 8, NTOK]),
                       ("d_attnT", [128, 8, NTOK]), ("d_mergedT", [128, 16, NTOK]), ("d_U", [128, 64, 65]),
                       ("d_Ys", [128, 64, 65])]:
            dbg[n_] = dout(n_, s_)
    dscr = lambda n, s, d: nc.dram_tensor(n, list(s), d, kind="Internal").ap()
    kT_scr = dscr("kT_scr", [DEPTH, 24, 128, SEQ], BF16)
    v_scr = dscr("v_scr", [DEPTH, 24, SEQ, HD], BF16)
    wst_scr = dscr("wst_scr", [DEPTH, 2, 128, 64, 64], BF16)
    wi_scr = dscr("wi_scr", [DEPTH, 128, 64, 128], BF16)
    wc_scr = dscr("wc_scr", [DEPTH, 2, 64, 64, 128], BF16)

    P = Prog()
    with ExitStack() as es:
        sb = lambda n, s, d: es.enter_context(nc.sbuf_tensor("sb_" + n, list(s), d))
        psb = lambda n, s, d=F32: es.enter_context(nc.psum_tensor("ps_" + n, list(s), d))
        xres = sb("xres", [128, 5, D_MODEL], F32)
        xnT = sb("xnT", [128, 16, NTOK], BF16)
        wbuf = [sb(f"wbuf{i}", [128, 16, 384], BF16) for i in range(2)]
        wsm = [sb(f"wsm{i}", [128, 16, 128], BF16) for i in range(2)]
        attnT = sb("attnT", [128, 8, NTOK], BF16)
        yT = sb("yT", [128, 8, NTOK], BF16)
        identf = sb("identf", [128, 128], F32)
        identb = sb("identb", [128, 128], BF16)
        onesb = sb("onesb", [128, 128], BF16)
        zzb = sb("zzb", [128, 8, 240], BF16)
        imask = sb("imask", [128, 128], F32)
        smallc = sb("smallc", [128, 64], F32)
        bglu_t = sb("bglu_t", [128, DEPTH, 8], F32)
        dcol = sb("dcol", [128, DEPTH, 64], F32)
        qnw_bc = sb("qnw_bc", [128, DEPTH, HD], F32)
        knw_bc = sb("knw_bc", [128, DEPTH, HD], F32)
        Hst = sb("Hst", [64, DEPTH, 2, 64], F32)
        A8r = sb("A8r", [64, DEPTH, 2, 64], F32)
        A8i = sb("A8i", [64, DEPTH, 2, 64], F32)
        A4 = sb("A4", [64, DEPTH, 2, 64], F32)
        stat = sb("stat", [128, 16], F32)
        bar = sb("bar", [128, 2], F32)
        NAB, NAF = 27800, 7900
        ARB = sb("ARB", [128, NAB], BF16)
        ARF = sb("ARF", [128, NAF], F32)
        Bf = psb("Bf", [128, 6, 512])
        pm = [Bf[:, i, :] for i in range(2)]
        pt = [psb(f"pt{i}", [128, 8, 128], BF16) for i in range(2)]
        pa = [Bf[:, 2 + i, :] for i in range(4)]
        PMK = ["pm0", "pm1"]
        PTK = ["pt0", "pt1"]
        PAK = ["pa0", "pa1", "pa2", "pa3"]

        OP = P.op
        cnt = {"w": 0, "ws": 0, "pm": 0, "pt": 0, "pa": 0}

        class Carve:
            def __init__(self, t, n):
                self.t, self.n, self.off = t, n, 0

            def reset(self):
                self.off = 0

            def get(self, parts, *shape):
                size = int(np.prod(shape))
                assert self.off + size <= self.n, (self.off, size, self.n)
                ap = self.t[0:parts, self.off:self.off + size]
                self.off += size
                if len(shape) > 1:
                    names = " ".join(f"d{i}" for i in range(len(shape)))
                    kw = {f"d{i}": int(shape[i]) for i in range(len(shape))}
                    ap = ap.rearrange(f"p ({names}) -> p {names}", **kw)
                return ap

        cb = Carve(ARB, NAB)
        cf = Carve(ARF, NAF)

        def barrier():
            OP("dve", lambda e: e.memset(bar[:, 0:1], 0.0), [], ["__ph"])
            cb.reset()
            cf.reset()

        def dma(out, in_, r=(), w=(), q="sp", nobar=False, **kw):
            kw.setdefault("allow_slow_non_contiguous", True)
            if nobar:
                return P.op(q, lambda e: e.dma_start(out=out, in_=in_, **kw), list(r), list(w) + ["__nobar"], dma=True)
            return P.dma_in(q, out, in_, reads=r, writes=w, **kw)

        def next_pm():
            i = cnt["pm"] % 2
            cnt["pm"] += 1
            return i

        def next_pt():
            i = cnt["pt"] % 2
            cnt["pt"] += 1
            return i

        def load_w(src_ap, wd, extra=False):
            i = cnt["w"] % 2
            cnt["w"] += 1
            keys = [f"wbuf{i}"] + ([f"wbuf{i}_{s}" for s in range(3)] if extra else [])
            dma(wbuf[i][:, :, 0:wd], src_ap, w=keys, q="pool", nobar=True)
            return i

        def load_ws(src_ap, nk):
            i = cnt["ws"] % 2
            cnt["ws"] += 1
            dma(wsm[i][:, 0:nk, :], src_ap, w=[f"wsm{i}"], q="pool", nobar=True)
            return i

        def wcols(l, c0, wd):
            return w_in[l][:, c0:c0 + wd].rearrange("(k p) n -> p k n", p=128)


        def MM(out, lhsT, rhs, start, stop, r, w):
            OP("pe", lambda e: e.matmul(out, lhsT=lhsT, rhs=rhs, start=start, stop=stop), r, w)

        def TR(out, in_, ident, r, w):
            OP("pe", lambda e: e.transpose(out=out, in_=in_, identity=ident), r, w)

        def ACTF(out, in_, func, r, w, **kw):
            OP("act", lambda e: e.activation(out=out, in_=in_, func=func, **kw), r, w)

        def ACOPY(out, in_, r, w):
            OP("act", lambda e: e.copy(out=out, in_=in_), r, w)

        def VCOPY(out, in_, r, w, eng="dve"):
            OP(eng, lambda e: e.tensor_copy(out=out, in_=in_), r, w)

        def TT(out, in0, in1, op, r, w, eng="dve"):
            OP(eng, lambda e: e.tensor_tensor(out=out, in0=in0, in1=in1, op=op), r, w)

        def TS(out, in0, s1, s2, op0, op1, r, w, eng="dve"):
            if s2 is None:
                OP(eng, lambda e: e.tensor_scalar(out=out, in0=in0, scalar1=s1, scalar2=None, op0=op0), r, w)
            else:
                OP(eng, lambda e: e.tensor_scalar(out=out, in0=in0, scalar1=s1, scalar2=s2, op0=op0, op1=op1), r, w)

        def STT(out, in0, scalar, in1, op0, op1, r, w, eng="dve"):
            OP(eng, lambda e: e.scalar_tensor_tensor(out=out, in0=in0, scalar=scalar, in1=in1, op0=op0, op1=op1), r, w)

        def MSET(ap, val, r, w, eng="dve"):
            OP(eng, lambda e: e.memset(ap, val), r, w)

        def RECIP(out, in_, r, w):
            OP("dve", lambda e: e.reciprocal(out=out, in_=in_), r, w)

        def RSUM(out, in_, r, w):
            OP("dve", lambda e: e.tensor_reduce(out=out, in_=in_, axis=AX.X, op=ALU.add), r, w)

        dma(identf[:], cd["ident"], w=["identf"])
        VCOPY(identb[:], identf[:], ["identf"], ["identb"])
        MSET(onesb[:], 1.0, [], ["onesb"])
        dma(zzb[:], cd["zz"], w=["zzb"], q="pool")
        dma(imask[:], cd["imask"], w=["imask"])
        MSET(Hst[:], 0.0, [], ["Hst"])
        for l in range(DEPTH):
            for o in range(8):
                dma(bglu_t[:, l, o:o + 1], b_glu[l:l + 1, o * 128:(o + 1) * 128].rearrange("a p -> p a"), w=["bglu_t"])
            for s in range(8):
                dma(dcol[s * 16:(s + 1) * 16, l, :], ssm_d[l:l + 1, :].rearrange("a (g c) -> c (a g)", c=16), w=["dcol"])
            dma(qnw_bc[:, l, :], qnw[l:l + 1, :].partition_broadcast(128), w=["qnw_bc"])
            dma(knw_bc[:, l, :], knw[l:l + 1, :].partition_broadcast(128), w=["knw_bc"])
        for g in range(3):
            W = WINDOWS[g]
            for l in range(DEPTH):
                dma(kvs[g][l, 0:W - NS].rearrange("t a h d -> t (a h d)"), cks[g][l, NS:W].rearrange("t a h d -> t (a h d)"),
                    w=[f"kvs{g}"])

        def cmul(out_r, out_i, ar, ai, br, bi, t1, t2, tag, extra=()):
            rd = [tag] + list(extra)
            TT(t1, ar, br, ALU.mult, rd, [tag])
            TT(t2, ai, bi, ALU.mult, rd, [tag])
            TT(out_r, t1, t2, ALU.subtract, rd, [tag])
            TT(t1, ar, bi, ALU.mult, rd, [tag])
            TT(t2, ai, br, ALU.mult, rd, [tag])
            TT(out_i, t1, t2, ALU.add, rd, [tag])

        def sin_reduced(out, x, tmp, tag):
            TS(tmp, x, 1.0 / TWO_PI, MAGIC, ALU.mult, ALU.add, [tag], [tag])
            TS(tmp, tmp, -MAGIC, None, ALU.add, None, [tag], [tag])
            STT(out, tmp, -CW1, x, ALU.mult, ALU.add, [tag], [tag])
            STT(out, tmp, -CW2, out, ALU.mult, ALU.add, [tag], [tag])
            TS(out, out, 3.1415925, -3.1415925, ALU.min, ALU.max, [tag], [tag])
            ACTF(out, out, AF.Sin, [tag], [tag])

        def ssm_prep(l):
            barrier()
            T = "prep"
            g64 = lambda: cf.get(64, 64)
            lre, lim, dt, LR, LI = g64(), g64(), g64(), g64(), g64()
            tA, tB, tC = g64(), g64(), g64()
            stage = cf.get(128, 64)
            for src, dst in ((lam_re, lre), (lam_im, lim)):
                dma(stage[0:64, :], src[l], w=[T])
                pi = next_pm()
                TR(pm[pi][0:64, 0:64], stage[0:64, :], identf[0:64, 0:64], [T, "identf"], [PMK[pi]])
                VCOPY(dst, pm[pi][0:64, 0:64], [], [PMK[pi], T])
            dma(dt, log_dt[l:l + 1, :].partition_broadcast(64), w=[T])
            ACTF(dt, dt, AF.Exp, [T], [T])
            TT(LR, lre, dt, ALU.mult, [T], [T])
            TT(LI, lim, dt, ALU.mult, [T], [T])
            PP = cf.get(64, 9, 2, 64)
            PN = cf.get(64, 8, 2, 64)
            mag, sn, cs, xc = g64(), g64(), g64(), g64()
            ACTF(mag, LR, AF.Exp, [T], [T])
            sin_reduced(sn, LI, tA, T)
            TS(xc, LI, TWO_PI / 4, None, ALU.add, None, [T], [T])
            sin_reduced(cs, xc, tA, T)
            TT(PP[:, 1, 0, :], mag, cs, ALU.mult, [T], [T])
            TT(PP[:, 1, 1, :], mag, sn, ALU.mult, [T], [T])
            for arr in (PP, PN):
                MSET(arr[:, 0, 0, :], 1.0, [T], [T])
                MSET(arr[:, 0, 1, :], 0.0, [T], [T])
            TT(tB, mag, mag, ALU.mult, [T], [T])
            RECIP(tB, tB, [T], [T])
            TT(PN[:, 1, 0, :], PP[:, 1, 0, :], tB, ALU.mult, [T], [T])
            STT(PN[:, 1, 1, :], PP[:, 1, 1, :], -1.0, tB, ALU.mult, ALU.mult, [T], [T])
            for k in range(2, 9):
                cmul(PP[:, k, 0, :], PP[:, k, 1, :], PP[:, k - 1, 0, :], PP[:, k - 1, 1, :], PP[:, 1, 0, :], PP[:, 1, 1, :], tA, tC, T)
            for k in range(2, 8):
                cmul(PN[:, k, 0, :], PN[:, k, 1, :], PN[:, k - 1, 0, :], PN[:, k - 1, 1, :], PN[:, 1, 0, :], PN[:, 1, 1, :], tA, tC, T)
            VCOPY(A8r[:, l, 0, :], PP[:, 8, 0, :], [T], ["A8"])
            VCOPY(A8r[:, l, 1, :], PP[:, 8, 0, :], [T], ["A8"])
            TS(A8i[:, l, 0, :], PP[:, 8, 1, :], -1.0, None, ALU.mult, None, [T], ["A8"])
            VCOPY(A8i[:, l, 1, :], PP[:, 8, 1, :], [T], ["A8"])
            VCOPY(A4[:, l, :, :], PP[:, 4, :, :], [T], ["A8"])
            qr, qi = g64(), g64()
            TS(tA, PP[:, 1, 0, :], -1.0, None, ALU.add, None, [T], [T])
            TT(tB, lre, lre, ALU.mult, [T], [T])
            TT(tC, lim, lim, ALU.mult, [T], [T])
            TT(tB, tB, tC, ALU.add, [T], [T])
            RECIP(tB, tB, [T], [T])
            TT(qr, tA, lre, ALU.mult, [T], [T])
            TT(tC, PP[:, 1, 1, :], lim, ALU.mult, [T], [T])
            TT(qr, qr, tC, ALU.add, [T], [T])
            TT(qr, qr, tB, ALU.mult, [T], [T])
            TT(qi, PP[:, 1, 1, :], lre, ALU.mult, [T], [T])
            TT(tC, tA, lim, ALU.mult, [T], [T])
            TT(qi, qi, tC, ALU.subtract, [T], [T])
            TT(qi, qi, tB, ALU.mult, [T], [T])
            GQ = 4
            Bre, Bim = cf.get(64, GQ, 16), cf.get(64, GQ, 16)
            bbr, bbi = cf.get(64, GQ, 16), cf.get(64, GQ, 16)
            Cr, Ci = cf.get(64, GQ, 16), cf.get(64, GQ, 16)
            Fr, Fi = cf.get(64, GQ, 8, 16), cf.get(64, GQ, 8, 16)
            Er, Eni = cf.get(64, GQ, 8, 16), cf.get(64, GQ, 8, 16)
            Gr, Gi = cf.get(64, GQ, 8, 16), cf.get(64, GQ, 8, 16)
            u1, u2 = cf.get(64, GQ, 8, 16), cf.get(64, GQ, 8, 16)
            cst = cf.get(128, 64)
            Gb = cb.get(128, 2, GQ, 64)
            Wib = cb.get(128, GQ, 128)
            Wcb = cb.get(64, 2, GQ, 128)
            bc3 = lambda ap, n: ap.unsqueeze(2).to_broadcast([64, GQ, n])
            bc4 = lambda ap: ap.unsqueeze(2).unsqueeze(3).to_broadcast([64, GQ, 8, 16])
            fl = lambda ap: ap.rearrange("p s c -> p (s c)")
            for g0 in range(0, 64, GQ):
                gs = slice(g0, g0 + GQ)
                dma(Bre, b_re[l, gs].rearrange("g p c -> p g c"), w=[T])
                dma(Bim, b_im[l, gs].rearrange("g p c -> p g c"), w=[T])
                for src, dst in ((c_re, Cr), (c_im, Ci)):
                    dma(cst[0:GQ * 16, :], src[l, gs].rearrange("g c p -> (g c) p"), w=[T])
                    pi = next_pm()
                    TR(pm[pi][0:64, 0:GQ * 16], cst[0:GQ * 16, :], identf[0:GQ * 16, 0:GQ * 16], [T, "identf"], [PMK[pi]])
                    VCOPY(dst.rearrange("p g c -> p (g c)"), pm[pi][0:64, 0:GQ * 16], [], [PMK[pi], T])
                cmul(bbr, bbi, bc3(qr[:, gs], 16), bc3(qi[:, gs], 16), Bre, Bim, u1[:, :, 0, :], u2[:, :, 0, :], T)
                pw = lambda arr, ri: arr[:, 0:8, ri, gs].rearrange("p s g -> p g s").unsqueeze(3).to_broadcast([64, GQ, 8, 16])
                bs_ = lambda ap: ap.unsqueeze(2).to_broadcast([64, GQ, 8, 16])
                cmul(Fr, Fi, pw(PN, 0), pw(PN, 1), bs_(bbr), bs_(bbi), u1, u2, T)
                cmul(Er, Eni, pw(PP, 0), pw(PP, 1), bs_(Cr), bs_(Ci), u1, u2, T)
                cmul(Gr, Gi, bc4(PP[:, 7, 0, gs]), bc4(PP[:, 7, 1, gs]), Fr, Fi, u1, u2, T)
                for ri, Gx in ((0, Gr), (1, Gi)):
                    pi = next_pm()
                    for gg in range(GQ):
                        TR(pm[pi][:, gg * 64:(gg + 1) * 64], fl(Gx[:, gg]), identf[0:64, 0:64], [T, "identf"], [PMK[pi]])
                    ACOPY(Gb[:, ri].rearrange("p g q -> p (g q)"), pm[pi][:, 0:GQ * 64], [], [PMK[pi], "Gb"])
                dma(wst_scr[l, :, :, gs, :].rearrange("r k g p -> k r g p"), Gb, r=["Gb"], w=["wst_scr"])
                cmul(Gr, Gi, bc4(PN[:, 7, 0, gs]), bc4(PN[:, 7, 1, gs]), Er, Eni, u1, u2, T)
                ACOPY(Wcb[:, 0].rearrange("p g n -> p (g n)"), Gr.rearrange("p g s c -> p (g s c)"), [T], ["Wcb"])
                ACTF(Wcb[:, 1].rearrange("p g n -> p (g n)"), Gi.rearrange("p g s c -> p (g s c)"), AF.Copy, [T], ["Wcb"], scale=-1.0)
                dma(wc_scr[l, :, :, gs, :].rearrange("r p g n -> p r g n"), Wcb, r=["Wcb"], w=["wc_scr"])
                TS(Eni, Eni, -1.0, None, ALU.mult, None, [T], [T])
                pi = next_pm()
                for gg in range(GQ):
                    MM(pm[pi][:, gg * 128:(gg + 1) * 128], fl(Fr[:, gg]), fl(Er[:, gg]), True, False, [T], [PMK[pi]])
                    MM(pm[pi][:, gg * 128:(gg + 1) * 128], fl(Fi[:, gg]), fl(Eni[:, gg]), False, True, [T], [PMK[pi]])
                for gg in range(GQ):
                    g = g0 + gg
                    TT(Wib[:, gg, :], pm[pi][:, gg * 128:(gg + 1) * 128], imask[:], ALU.mult, ["imask"], [PMK[pi], "Wib"])
                    STT(Wib[:, gg, :], identb[:], dcol[:, l, g:g + 1], Wib[:, gg, :], ALU.mult, ALU.add,
                        ["identb", "dcol", "Wib"], ["Wib"])
                dma(wi_scr[l, :, gs, :], Wib, r=["Wib"], w=["wi_scr"])

        for l in range(DEPTH):
            ssm_prep(l)

        ALLT = ["xnT", "attnT", "yT", "U", "mergedT"]

        def fm_linear(ws_i, nk, rhsT, has_s, bm, bmk, bs, bsk, s_off):
            fm_linear_ap(wsm[ws_i], f"wsm{ws_i}", nk, rhsT, has_s, bm, bmk, bs, bsk, s_off)

        def fm_linear_ap(wap, wkey, nk, rhsT, has_s, bm, bmk, bs, bsk, s_off):
            for k in range(nk):
                MM(bm[:, 0:TC], wap[:, k, :], rhsT[:, k, 0:TC], k == 0, k == nk - 1, ALLT + [wkey], [bmk])
            if has_s:
                for k in range(nk):
                    MM(bs[:, s_off:s_off + NS], wap[:, k, :], rhsT[:, k, TC:NTOK], k == 0, k == nk - 1, ALLT + [wkey], [bsk])

        def norm_tile(x_ap, M, col0, junk, xn_tok, normw_bc):
            ACTF(junk[0:M, :], x_ap, AF.Square, ["xres"], ["junk", "stat"], accum_out=stat[0:M, 0:1])
            TS(stat[0:M, 1:2], stat[0:M, 0:1], 1.0 / D_MODEL, EPS, ALU.mult, ALU.add, ["stat"], ["stat"])
            ACTF(stat[0:M, 1:2], stat[0:M, 1:2], AF.Sqrt, ["stat"], ["stat"])
            RECIP(stat[0:M, 2:3], stat[0:M, 1:2], ["stat"], ["stat"])
            STT(xn_tok[0:M, :], x_ap, stat[0:M, 2:3], normw_bc[0:M, :], ALU.mult, ALU.mult, ["xres", "stat", "normw_bc"], ["xn_tok"])
            for kb in range(2):
                ti = next_pt()
                for kk in range(8):
                    k = kb * 8 + kk
                    TR(pt[ti][:, kk, 0:M], xn_tok[0:M, k * 128:(k + 1) * 128], identb[0:M, 0:M], ["xn_tok", "identb"], [PTK[ti]])
                ACOPY(xnT[:, kb * 8:(kb + 1) * 8, col0:col0 + M], pt[ti][:, :, 0:M], [], [PTK[ti], "xnT"])

        def head_norm_rope(s3, srck, M, nh, wn_bc, cosb, sinb, cs_slot, o3, ok, tmp):
            ksq, kn, t1, t2 = tmp
            ACTF(ksq[0:M, 0:nh], s3, AF.Square, [], [srck, "hn_ksq"])
            RSUM(stat[0:M, 4:4 + nh], ksq[0:M, 0:nh], ["hn_ksq"], ["stat"])
            TS(stat[0:M, 4:4 + nh], stat[0:M, 4:4 + nh], 1.0 / HD, EPS, ALU.mult, ALU.add, ["stat"], ["stat"])
            ACTF(stat[0:M, 4:4 + nh], stat[0:M, 4:4 + nh], AF.Sqrt, ["stat"], ["stat"])
            RECIP(stat[0:M, 4:4 + nh], stat[0:M, 4:4 + nh], ["stat"], ["stat"])
            TT(kn[0:M, 0:nh], s3, stat[0:M, 4:4 + nh].unsqueeze(2).to_broadcast([M, nh, HD]), ALU.mult, ["stat"], [srck, "hn_kn"])
            TT(kn[0:M, 0:nh], kn[0:M, 0:nh], wn_bc[0:M].unsqueeze(1).to_broadcast([M, nh, HD]), ALU.mult,
               ["hn_kn", "qnw_bc", "knw_bc"], ["hn_kn"])
            cbk = cosb[0:M, cs_slot].unsqueeze(1).to_broadcast([M, nh, 64])
            snk = sinb[0:M, cs_slot].unsqueeze(1).to_broadcast([M, nh, 64])
            x1 = kn[0:M, 0:nh, 0:64]
            x2 = kn[0:M, 0:nh, 64:128]
            TT(t1[0:M, 0:nh], x1, cbk, ALU.mult, ["hn_kn", "cs"], ["hn_t1"])
            TT(t2[0:M, 0:nh], x2, snk, ALU.mult, ["hn_kn", "cs"], ["hn_t2"])
            TT(o3[0:M, 0:nh, 0:64], t1[0:M, 0:nh], t2[0:M, 0:nh], ALU.subtract, ["hn_t1", "hn_t2"], [ok])
            TT(t1[0:M, 0:nh], x2, cbk, ALU.mult, ["hn_kn", "cs"], ["hn_t1"])
            TT(t2[0:M, 0:nh], x1, snk, ALU.mult, ["hn_kn", "cs"], ["hn_t2"])
            TT(o3[0:M, 0:nh, 64:128], t1[0:M, 0:nh], t2[0:M, 0:nh], ALU.add, ["hn_t1", "hn_t2"], [ok])

        def head_norm_rope_qk(s4, srckeys, M, wqk, cosb, sinb, cs_slot, o4, ok, tmp, okb=None):
            ksq, kn, t1, t2, t3, t4 = tmp
            st = stat[0:M, 4:10].rearrange("p (a h) -> p a h", a=2)
            ACTF(ksq[0:M], s4, AF.Square, [], list(srckeys) + ["hn_ksq"])
            RSUM(st, ksq[0:M], ["hn_ksq"], ["stat"])
            TS(st, st, 1.0 / HD, EPS, ALU.mult, ALU.add, ["stat"], ["stat"])
            ACTF(st, st, AF.Sqrt, ["stat"], ["stat"])
            RECIP(st, st, ["stat"], ["stat"])
            TT(kn[0:M], s4, st.unsqueeze(3).to_broadcast([M, 2, 3, HD]), ALU.mult, ["stat"], list(srckeys) + ["hn_kn"])
            TT(kn[0:M], kn[0:M], wqk[0:M].unsqueeze(2).to_broadcast([M, 2, 3, HD]), ALU.mult, ["hn_kn", "wqk"], ["hn_kn"])
            cbk = cosb[0:M, cs_slot].unsqueeze(1).unsqueeze(1).to_broadcast([M, 2, 3, 64])
            snk = sinb[0:M, cs_slot].unsqueeze(1).unsqueeze(1).to_broadcast([M, 2, 3, 64])
            x1 = kn[0:M, :, :, 0:64]
            x2 = kn[0:M, :, :, 64:128]
            TT(t1[0:M], x1, cbk, ALU.mult, ["hn_kn", "cs"], ["hn_t1"])
            TT(t2[0:M], x2, snk, ALU.mult, ["hn_kn", "cs"], ["hn_t2"])
            TT(o4[0:M, :, :, 0:64], t1[0:M], t2[0:M], ALU.subtract, ["hn_t1", "hn_t2"], [ok])
            TT(t3[0:M], x2, cbk, ALU.mult, ["hn_kn", "cs"], ["hn_t1"])
            TT(t4[0:M], x1, snk, ALU.mult, ["hn_kn", "cs"], ["hn_t2"])
            TT(o4[0:M, :, :, 64:128], t3[0:M], t4[0:M], ALU.add, ["hn_t1", "hn_t2"], [okb or (ok + "b")])

        def phase_norm(l, ci, has_s):
            barrier()
            junk = cb.get(128, D_MODEL)
            xn_tok = cb.get(128, D_MODEL)
            normw_bc = cf.get(128, D_MODEL)
            dma(normw_bc, normw[l:l + 1, :].partition_broadcast(128), w=["normw_bc"])
            for tt in range(4):
                norm_tile(xres[:, tt, :], 128, tt * 128, junk, xn_tok, normw_bc)
            if has_s:
                norm_tile(xres[0:NS, 4, :], NS, TC, junk, xn_tok, normw_bc)
            if debug and ci == 0 and l == 0:
                dma(dbg["d_xnT"], xnT[:], r=["xnT"], q="pool")

        def phase_ssm(l, ci, has_s):
            barrier()
            UT = cb.get(65, 64, 8, 16)
            U = cb.get(128, 64, 65)
            Ys = cb.get(128, 64, 65)
            Hmov = cb.get(64, 2, 32, 65)
            wst = cb.get(128, 2, 8, 64)
            wib = cb.get(128, 8, 128)
            wcb = cb.get(64, 2, 8, 128)
            us = cb.get(NS, 1024)
            Sall = cf.get(64, 65, 2, 32)
            Tt = cf.get(64, 2, 32)
            M1 = cf.get(64, 2, 32)
            M2 = cf.get(64, 2, 32)
            h0 = cf.get(64, 64, 2)
            gA = cf.get(128, 260)
            gB = cf.get(128, 260)
            fin = cf.get(64, 32, 2)
            sT = U.rearrange("p g j -> p (g j)")[:, 0:8 * NTOK].rearrange("p (t n) -> p t n", t=8)
            xnTp = xnT[:, :, 0:TC].rearrange("p k (j s) -> p k s j", s=8)
            MSET(UT[64:65].rearrange("p g s c -> p (g s c)"), 0.0, [], ["UT"])
            for c0, wd in ((0, 384), (384, 384), (768, 256)):
                wi = load_w(wcols(l, O_U + c0, wd), wd)
                g0, ng = c0 // 16, wd // 16
                for s in range(8):
                    pi = next_pm()
                    for k in range(16):
                        MM(pm[pi][0:64, 0:wd], xnTp[:, k, s, :], wbuf[wi][:, k, 0:wd], k == 0, k == 15, ["xnT", f"wbuf{wi}"], [PMK[pi]])
                    ACOPY(UT[0:64, g0:g0 + ng, s, :], pm[pi][0:64, 0:wd].rearrange("p (g c) -> p g c", c=16), [], [PMK[pi], "UT"])
                if has_s:
                    pi = next_pm()
                    for k in range(16):
                        MM(pm[pi][0:NS, 0:wd], xnT[:, k, TC:NTOK], wbuf[wi][:, k, 0:wd], k == 0, k == 15, ["xnT", f"wbuf{wi}"], [PMK[pi]])
                    ACOPY(us[:, c0:c0 + wd], pm[pi][0:NS, 0:wd], [], [PMK[pi], "us"])
            if has_s:
                for tau in range(NS):
                    dma(UT[64:65, :, 4 + tau, :], us[tau:tau + 1, :].rearrange("p (g c) -> p g c", c=16), r=["us"], w=["UT"])
                dma(h0, sst[l].rearrange("g p r -> p g r"), w=["h0"])
            for gb in range(8):
                ti = next_pt()
                for gg in range(8):
                    g = gb * 8 + gg
                    TR(pt[ti][:, gg, 0:65], UT[0:65, g].rearrange("p s c -> p (s c)"), identb[0:65, 0:65], ["UT", "identb"], [PTK[ti]])
                ACOPY(U[:, gb * 8:(gb + 1) * 8, :], pt[ti][:, :, 0:65], [], [PTK[ti], "U"])
            if debug and ci == 0 and l == 0:
                dma(dbg["d_U"], U, r=["U"], q="pool")
            for hf in range(2):
                h0g = hf * 32
                for e8 in range(4):
                    ge = h0g + e8 * 8
                    dma(wst, wst_scr[l, :, :, ge:ge + 8, :].rearrange("r k g p -> k r g p"), r=["wst_scr"], w=["wst"])
                    for b3 in range(0, 8, 3):
                        ng = min(3, 8 - b3)
                        pi = next_pm()
                        for gg in range(ng):
                            for ri in range(2):
                                sl = (gg * 2 + ri) * 65
                                MM(pm[pi][0:64, sl:sl + 65], wst[:, ri, b3 + gg, :], U[:, ge + b3 + gg, :], True, True, ["wst", "U"], [PMK[pi]])
                        gl = e8 * 8 + b3
                        ACOPY(Sall[:, :, :, gl:gl + ng].rearrange("p j r g -> p g r j"),
                              pm[pi][0:64, 0:ng * 130].rearrange("p (g r j) -> p g r j", g=ng, r=2), [], [PMK[pi], "Sall"])
                Hv = Hst[:, l, :, h0g:h0g + 32]
                ar2 = A8r[:, l, :, h0g:h0g + 32]
                ai2 = A8i[:, l, :, h0g:h0g + 32]
                ACOPY(Hmov[:, :, :, 0], Hv, ["Hst"], ["Hmov"])
                for j in range(64):
                    TT(Tt, Hv, Sall[:, j, :, :], ALU.add, ["Hst", "Sall"], ["Tt"])
                    if j == 63 and ci == 7:
                        VCOPY(fin, Tt.rearrange("p r g -> p g r"), ["Tt"], ["fin"])
                        dma(ssm_p[l, h0g:h0g + 32].rearrange("g p r -> p g r"), fin, r=["fin"])
                    TT(M1, Tt, ar2, ALU.mult, ["Tt", "A8"], ["M1"])
                    TT(M2[:, 0, :], Tt[:, 1, :], ai2[:, 0, :], ALU.mult, ["Tt", "A8"], ["M2"])
                    TT(M2[:, 1, :], Tt[:, 0, :], ai2[:, 1, :], ALU.mult, ["Tt", "A8"], ["M2"])
                    TT(Hv, M1, M2, ALU.add, ["M1", "M2"], ["Hst"])
                    if j < 63:
                        ACOPY(Hmov[:, :, :, j + 1], Hv, ["Hst"], ["Hmov"])
                if has_s:
                    hr, hi = h0[:, h0g:h0g + 32, 0], h0[:, h0g:h0g + 32, 1]
                    a4r, a4i = A4[:, l, 0, h0g:h0g + 32], A4[:, l, 1, h0g:h0g + 32]
                    cmul(M1[:, 0, :], M1[:, 1, :], hr, hi, a4r, a4i, M2[:, 0, :], M2[:, 1, :], "M1", extra=["h0", "A8", "M2", "Tt"])
                    ACOPY(Hmov[:, :, :, 64], M1, ["M1", "h0", "A8"], ["Hmov"])
                    TT(Tt, M1, Sall[:, 64, :, :], ALU.add, ["M1", "Sall"], ["Tt"])
                    VCOPY(fin, Tt.rearrange("p r g -> p g r"), ["Tt"], ["fin"])
                    dma(ssm_s[l, h0g:h0g + 32].rearrange("g p r -> p g r"), fin, r=["fin"])
                else:
                    MSET(Hmov[:, :, :, 64], 0.0, [], ["Hmov"])
                for e8 in range(4):
                    ge = h0g + e8 * 8
                    dma(wib, wi_scr[l, :, ge:ge + 8, :], r=["wi_scr"], w=["wib"])
                    dma(wcb, wc_scr[l, :, :, ge:ge + 8, :].rearrange("r p g n -> p r g n"), r=["wc_scr"], w=["wcb"])
                    for b4 in range(2):
                        pi = next_pm()
                        for gg in range(4):
                            gi = b4 * 4 + gg
                            dst = pm[pi][:, gg * 65:(gg + 1) * 65]
                            MM(dst, wib[:, gi, :], U[:, ge + gi, :], True, False, ["wib", "U"], [PMK[pi]])
                            MM(dst, wcb[:, 0, gi, :], Hmov[:, 0, e8 * 8 + gi, :], False, False, ["wcb", "Hmov"], [PMK[pi]])
                            MM(dst, wcb[:, 1, gi, :], Hmov[:, 1, e8 * 8 + gi, :], False, True, ["wcb", "Hmov"], [PMK[pi]])
                        gq = ge + b4 * 4
                        ACOPY(gA, pm[pi][:, 0:260], [], [PMK[pi], "gA"])
                        TT(gB, gA, gA, ALU.mult, ["gA"], ["gB"])
                        TS(gB, gB, 0.044715, 1.0, ALU.mult, ALU.add, ["gB"], ["gB"])
                        TT(gB, gB, gA, ALU.mult, ["gA", "gB"], ["gB"])
                        ACTF(gB, gB, AF.Sigmoid, ["gB"], ["gB"], scale=1.5957691216057308)
                        TT(Ys[:, gq:gq + 4, :].rearrange("p g j -> p (g j)"), gA, gB, ALU.mult, ["gA", "gB"], ["Ys"])
            if debug and ci == 0 and l == 0:
                dma(dbg["d_Ys"], Ys, r=["Ys"], q="pool")
            for tl in range(8):
                for h in range(2):
                    for s4 in range(4):
                        sp = h * 4 + s4
                        for g8 in range(8):
                            MM(pa[h][:, s4 * 65:(s4 + 1) * 65], zzb[:, sp, 112 - 16 * g8:240 - 16 * g8], Ys[:, tl * 8 + g8, :],
                               g8 == 0, g8 == 7, ["zzb", "Ys"], [PAK[h]])
                    src = pa[h][:, 0:260].rearrange("p (s j) -> p s j", s=4)
                    ACOPY(sT[:, tl, h * 256:(h + 1) * 256].rearrange("p (s j) -> p s j", s=4), src[:, :, 0:64], [], [PAK[h], "U"])
                    if has_s and h == 1:
                        ACOPY(sT[:, tl, TC:NTOK], src[:, :, 64], [], [PAK[h], "U"])
            if debug and ci == 0 and l == 0:
                dma(dbg["d_sT"], sT, r=["U"], q="pool")
            return sT

        def phase_glu(l, ci, has_s, sT):
            t1 = cf.get(128, NTOK)
            t2 = cf.get(128, NTOK)
            for ob in range(8):
                wg = load_ws(w_glu[l][:, ob * 128:(ob + 1) * 128].rearrange("(k p) n -> p k n", p=128), 8)
                bA, bAk, bB, bBk = (pa[2], PAK[2], pm[0], PMK[0]) if ob % 2 == 0 else (pa[0], PAK[0], pa[1], PAK[1])
                so = 0 if ob % 2 == 0 else 16
                fm_linear(wg, 8, sT, has_s, bA, bAk, pa[3], PAK[3], so)
                ws_ = load_ws(wcols(l, O_GS + ob * 128, 128), 16)
                fm_linear(ws_, 16, xnT, has_s, bB, bBk, pa[3], PAK[3], so + 8)
                bias = bglu_t[:, l, ob:ob + 1]
                ACTF(t1[:, 0:TC], bA[:, 0:TC], AF.Sigmoid, ["bglu_t"], [bAk, "g_t1"], bias=bias)
                ACTF(t2[:, 0:TC], bB[:, 0:TC], AF.Silu, [], [bBk, "g_t2"])
                TT(t1[:, 0:TC], t1[:, 0:TC], sT[:, ob, 0:TC], ALU.mult, ["g_t1", "U"], ["g_t1"])
                TT(yT[:, ob, 0:TC].rearrange("p (j s) -> p s j", s=8), t1[:, 0:TC].rearrange("p (s j) -> p s j", s=8),
                   t2[:, 0:TC].rearrange("p (j s) -> p s j", s=8), ALU.mult, ["g_t1", "g_t2"], ["yT"])
                if has_s:
                    ACTF(t1[:, TC:NTOK], pa[3][:, so:so + NS], AF.Sigmoid, ["bglu_t"], [PAK[3], "g_t1"], bias=bias)
                    ACTF(t2[:, TC:NTOK], pa[3][:, so + 8:so + 8 + NS], AF.Silu, [], [PAK[3], "g_t2"])
                    TT(t1[:, TC:NTOK], t1[:, TC:NTOK], sT[:, ob, TC:NTOK], ALU.mult, ["g_t1", "U"], ["g_t1"])
                    TT(yT[:, ob, TC:NTOK], t1[:, TC:NTOK], t2[:, TC:NTOK], ALU.mult, ["g_t1", "g_t2"], ["yT"])
            if debug and ci == 0 and l == 0:
                dma(dbg["d_yT"], yT[:], r=["yT"], q="pool")

        def phase_attn(l, ci, has_s):
            barrier()
            t0 = ci * TC
            mk = [cb.get(128, 896 + WINDOWS[g]) for g in range(3)]
            qbs = [cb.get(128, 2, 3, 128) for _ in range(2)]
            qT = cb.get(128, 3, NTOK)
            kT = cb.get(128, 3, NTOK)
            vtok = cb.get(128, 5, 3, 128)
            kTh = [cb.get(128, WINDOWS[g]) for g in range(3)]
            vh = [cb.get(128, WINDOWS[g] // 128, 128) for g in range(3)]
            Eb = [cb.get(128, TC) for _ in range(2)]
            Pb = [cb.get(128, TC) for _ in range(2)]
            cosb = cf.get(128, 5, 64)
            sinb = cf.get(128, 5, 64)
            tmp = (cf.get(128, 2, 3, 128), cf.get(128, 2, 3, 128), cf.get(128, 2, 3, 64), cf.get(128, 2, 3, 64),
                   cf.get(128, 2, 3, 64), cf.get(128, 2, 3, 64))
            ko = [cf.get(128, 2, 3, 128) for _ in range(2)]
            vo = [cf.get(128, 3, 128) for _ in range(2)]
            ao = cf.get(128, NTOK)
            rl = cf.get(128, NTOK)
            sg = cf.get(128, NTOK)
            if has_s:
                kc = cb.get(128, 21, 128)
                kTc = cb.get(128, 21, 128)
                vc = cb.get(128, 21, 128)
                smk = cb.get(128, 21, NS)
                snw = cb.get(NS, 3, NS)
                Es = cb.get(128, 96)
                Ps = cb.get(128, 96)
                toff = (0, 1, 5)
                for g in range(3):
                    nt = WINDOWS[g] // 128
                    dma(smk[:, toff[g]:toff[g] + nt, :], cd[f"smask{g}"], w=["smk"], q="pool")
                dma(snw, cd["snew"], w=["snw"], q="pool")
            for g in range(3):
                dma(mk[g], cd[f"mask{g}"], w=[f"mk{g}"], q="pool", max_dma_last_dim=4096)
            dma(cosb[:, 0:4, :], cd["cos_t"][t0:t0 + TC].rearrange("(t p) d -> p t d", p=128), w=["cs"])
            dma(sinb[:, 0:4, :], cd["sin_t"][t0:t0 + TC].rearrange("(t p) d -> p t d", p=128), w=["cs"])
            if has_s:
                dma(cosb[0:NS, 4, :], cd["cos_t"][SEQ:SEQ + NS], w=["cs"])
                dma(sinb[0:NS, 4, :], cd["sin_t"][SEQ:SEQ + NS], w=["cs"])
            tiles = [(tt, 128, tt * 128, tt) for tt in range(4)] + ([(4, NS, TC, 4)] if has_s else [])
            kcnt = [0]
            BQ = [Bf[:, 0:2, :], Bf[:, 2:4, :]]
            BQK = [[PMK[0], PMK[1]], [PAK[0], PAK[1]]]
            wqk = cf.get(128, 2, HD)
            VCOPY(wqk[:, 0, :], qnw_bc[:, l, :], ["qnw_bc"], ["wqk"])
            VCOPY(wqk[:, 1, :], knw_bc[:, l, :], ["knw_bc"], ["wqk"])
            nxt_w = [0, 0]

            def issue_qk(jj):
                out = []
                for base in (O_Q, O_K):
                    wi = cnt["w"] % 2
                    cnt["w"] += 1
                    for g in range(3):
                        dma(wbuf[wi][:, :, g * 128:(g + 1) * 128], wcols(l, base + g * 1024 + jj * 128, 128), w=[f"wbuf{wi}"], q="pool", nobar=True)
                    out.append(wi)
                return out

            for j in range(NH):
                pend = []

                def flush():
                    while pend:
                        pend.pop(0)()

                if j == 0:
                    nxt_w[:] = issue_qk(0)
                wis = list(nxt_w)
                ws_ga = load_ws(wcols(l, O_GA + j * 128, 128), 16)
                for tt, M, col0, cslot in tiles:
                    bq = kcnt[0] % 2
                    kcnt[0] += 1
                    pair, pkeys = BQ[bq], BQK[bq]
                    for a in range(2):
                        for k in range(16):
                            MM(pair[0:M, a, 0:384], xnT[:, k, col0:col0 + M], wbuf[wis[a]][:, k, 0:384], k == 0, k == 15,
                               ["xnT", f"wbuf{wis[a]}"], [pkeys[a]])
                    flush()
                    s4 = pair[0:M, :, 0:384].rearrange("p a (h d) -> p a h d", h=3)
                    kob, kok = ko[bq], f"ko{bq}"
                    qbi, qbk = qbs[bq], f"qb{bq}"
                    need_f32 = (tt == 4) or any(t0 + tt * 128 - (SEQ - WINDOWS[g]) >= 0 for g in range(3))
                    if need_f32:
                        head_norm_rope_qk(s4, pkeys, M, wqk, cosb, sinb, cslot, kob, kok, tmp)
                        ACOPY(qbi[0:M], kob[0:M], [kok, kok + "b"], [qbk])
                    else:
                        head_norm_rope_qk(s4, pkeys, M, wqk, cosb, sinb, cslot, qbi, qbk, tmp, okb=qbk)
                    for g in range(3):
                        W = WINDOWS[g]
                        if tt < 4:
                            r0 = t0 + tt * 128 - (SEQ - W)
                            if r0 >= 0:
                                dma(kvp[g][l, r0:r0 + 128, 0, j, :], kob[:, 1, g, :], r=[kok, kok + "b"])
                        else:
                            dma(kvs[g][l, W - NS:W, 0, j, :], kob[0:NS, 1, g, :], r=[kok, kok + "b"], w=[f"kvs{g}"])

                    def later(M=M, col0=col0, qbi=qbi, qbk=qbk):
                        ti = next_pt()
                        for a in range(2):
                            for g in range(3):
                                TR(pt[ti][:, a * 3 + g, 0:M], qbi[0:M, a, g, :], identb[0:M, 0:M], [qbk, "identb"], [PTK[ti]])
                        ACOPY(qT[:, :, col0:col0 + M], pt[ti][:, 0:3, 0:M], [], [PTK[ti], "qT"])
                        ACOPY(kT[:, :, col0:col0 + M], pt[ti][:, 3:6, 0:M], [], [PTK[ti], "kT"])
                    pend.append(later)
                wi = cnt["w"] % 2
                cnt["w"] += 1
                for g in range(3):
                    dma(wbuf[wi][:, :, g * 128:(g + 1) * 128], wcols(l, O_V + g * 1024 + j * 128, 128), w=[f"wbuf{wi}"], q="pool", nobar=True)
                for tt, M, col0, cslot in tiles:
                    pi = next_pm()
                    for k in range(16):
                        MM(pm[pi][0:M, 0:384], xnT[:, k, col0:col0 + M], wbuf[wi][:, k, 0:384], k == 0, k == 15,
                           ["xnT", f"wbuf{wi}"], [PMK[pi]])
                    flush()
                    vb = kcnt[0] % 2
                    kcnt[0] += 1
                    vob, vok = vo[vb], f"vo{vb}"
                    ACOPY(vob[0:M].rearrange("p h d -> p (h d)"), pm[pi][0:M, 0:384], [], [PMK[pi], vok])
                    VCOPY(vtok[0:M, tt], vob[0:M], [vok], ["vtok"])
                    for g in range(3):
                        W = WINDOWS[g]
                        if tt < 4:
                            r0 = t0 + tt * 128 - (SEQ - W)
                            if r0 >= 0:
                                dma(kvp[g][l, r0:r0 + 128, 1, j, :], vob[:, g, :], r=[vok])
                        else:
                            dma(kvs[g][l, W - NS:W, 1, j, :], vob[0:NS, g, :], r=[vok], w=[f"kvs{g}"])
                flush()
                if j + 1 < NH:
                    nxt_w[:] = issue_qk(j + 1)
                for g in range(3):
                    dma(kT_scr[l, g * 8 + j, :, t0:t0 + TC], kT[:, g, 0:TC], r=["kT"], w=["kscr"])
                    dma(v_scr[l, g * 8 + j, t0:t0 + TC, :].rearrange("(t p) d -> p t d", p=128), vtok[:, 0:4, g, :],
                        r=["vtok"], w=["vscr"])
                nht = [min(WINDOWS[g], t0) // 128 for g in range(3)]
                for g in range(3):
                    if nht[g]:
                        h = g * 8 + j
                        lo = t0 - nht[g] * 128
                        dma(kTh[g][:, 0:nht[g] * 128], kT_scr[l, h, :, lo:t0], r=["kscr"], w=[f"kTh{g}"])
                        dma(vh[g][:, 0:nht[g], :], v_scr[l, h, lo:t0, :].rearrange("(t p) d -> p t d", p=128), r=["vscr"], w=[f"vh{g}"])
                klist = []
                for g in range(3):
                    for i in range(nht[g]):
                        klist.append((g, -(nht[g] - i) * 128, kTh[g][:, i * 128:(i + 1) * 128], vh[g][:, i, :], [f"kTh{g}", f"vh{g}"]))
                    for tt in range(4):
                        klist.append((g, tt * 128, kT[:, g, tt * 128:(tt + 1) * 128], vtok[:, tt, g, :], ["kT", "vtok"]))
                def score(idx):
                    g, o, Kap, Vap, keys = klist[idx]
                    ai = idx % 2
                    MM(pa[ai][:, 0:TC], Kap, qT[:, g, 0:TC], True, True, ["qT"] + keys, [PAK[ai]])
                    ACTF(Eb[ai], pa[ai][:, 0:TC], AF.Exp, [], [PAK[ai], f"Eb{ai}"], scale=SCALE)
                    TT(Pb[ai], Eb[ai], mk[g][:, C0 - o:C0 - o + TC], ALU.mult, [f"Eb{ai}", f"mk{g}"], [f"Pb{ai}"],
                       eng=("pool" if idx % 2 == 0 else "dve"))

                def pv(idx):
                    g, o, Kap, Vap, keys = klist[idx]
                    ai = idx % 2
                    first, last = idx == 0, idx == len(klist) - 1
                    MM(pa[2][:, 0:TC], Vap, Pb[ai], first, last, [f"Pb{ai}"] + keys, [PAK[2]])
                    MM(pa[3][:, 0:TC], onesb[:], Pb[ai], first, last, [f"Pb{ai}", "onesb"], [PAK[3]])

                score(0)
                for idx in range(len(klist)):
                    if idx + 1 < len(klist):
                        score(idx + 1)
                    pv(idx)
                RECIP(rl[:, 0:TC], pa[3][:, 0:TC], [], [PAK[3], "rl"])
                TT(ao[:, 0:TC], pa[2][:, 0:TC], rl[:, 0:TC], ALU.mult, ["rl"], [PAK[2], "ao"])
                fm_linear(ws_ga, 16, xnT, has_s, pm[0], PMK[0], pm[1], PMK[1], 0)
                ACTF(sg[:, 0:TC], pm[0][:, 0:TC], AF.Silu, [], [PMK[0], "sg"])
                TT(attnT[:, j, 0:TC], ao[:, 0:TC], sg[:, 0:TC], ALU.mult, ["ao", "sg"], ["attnT"])
                if has_s:
                    for g in range(3):
                        nt = WINDOWS[g] // 128
                        o_ = toff[g]
                        dma(kc[:, o_:o_ + nt, :], cks[g][l, :, 0, j, :].rearrange("(t p) d -> p t d", p=128), w=["kc"], q="pool")
                        dma(vc[:, o_:o_ + nt, :], cks[g][l, :, 1, j, :].rearrange("(t p) d -> p t d", p=128), w=["vc"], q="pool")
                    for b0 in range(0, 21, 8):
                        n = min(8, 21 - b0)
                        ti = next_pt()
                        for t in range(n):
                            TR(pt[ti][:, t, :], kc[:, b0 + t, :], identb[:], ["kc", "identb"], [PTK[ti]])
                        ACOPY(kTc[:, b0:b0 + n, :], pt[ti][:, 0:n, :], [], [PTK[ti], "kTc"])
                    for g in range(3):
                        nt = WINDOWS[g] // 128
                        for t in range(nt):
                            c = (toff[g] + t) * NS
                            MM(pa[0][:, c:c + NS], kTc[:, toff[g] + t, :], qT[:, g, TC:NTOK], True, True, ["kTc", "qT"], [PAK[0]])
                        MM(pa[0][0:NS, 84 + g * NS:84 + (g + 1) * NS], kT[:, g, TC:NTOK], qT[:, g, TC:NTOK], True, True, ["kT", "qT"], [PAK[0]])
                    ACTF(Es[:, 0:84], pa[0][:, 0:84], AF.Exp, [], [PAK[0], "Es"], scale=SCALE)
                    ACTF(Es[0:NS, 84:96], pa[0][0:NS, 84:96], AF.Exp, [], [PAK[0], "Es"], scale=SCALE)
                    TT(Ps[:, 0:84], Es[:, 0:84], smk.rearrange("p t q -> p (t q)"), ALU.mult, ["Es", "smk"], ["Ps"])
                    TT(Ps[0:NS, 84:96], Es[0:NS, 84:96], snw.rearrange("p g q -> p (g q)"), ALU.mult, ["Es", "snw"], ["Ps"])
                    for dst0, use_v in ((16, True), (24, False)):
                        items = []
                        for g in range(3):
                            nt = WINDOWS[g] // 128
                            for t in range(nt):
                                c = (toff[g] + t) * NS
                                items.append((vc[:, toff[g] + t, :] if use_v else onesb[:], Ps[:, c:c + NS]))
                            items.append((vtok[0:NS, 4, g, :] if use_v else onesb[0:NS, :], Ps[0:NS, 84 + g * NS:84 + (g + 1) * NS]))
                        for ii, (lh, rh) in enumerate(items):
                            MM(pm[1][:, dst0:dst0 + NS], lh, rh, ii == 0, ii == len(items) - 1, ["vc", "vtok", "Ps", "onesb"], [PMK[1]])
                    RECIP(rl[:, TC:NTOK], pm[1][:, 24:24 + NS], [], [PMK[1], "rl"])
                    TT(ao[:, TC:NTOK], pm[1][:, 16:16 + NS], rl[:, TC:NTOK], ALU.mult, ["rl"], [PMK[1], "ao"])
                    ACTF(sg[:, TC:NTOK], pm[1][:, 0:NS], AF.Silu, [], [PMK[1], "sg"])
                    TT(attnT[:, j, TC:NTOK], ao[:, TC:NTOK], sg[:, TC:NTOK], ALU.mult, ["ao", "sg"], ["attnT"])
            if debug and ci == 0 and l == 0:
                dma(dbg["d_attnT"], attnT[:], r=["attnT"], q="pool")

        def phase_merge(l, ci, has_s):
            barrier()
            mergedT = cb.get(128, 16, NTOK)
            m1 = cf.get(128, NTOK)
            m2 = cf.get(128, NTOK)
            cols = [(slice(0, TC), None)] + ([(slice(TC, NTOK), 0)] if has_s else [])
            slots = [(wsm[0], "wsm0"), (wsm[1], "wsm1")] + \
                    [(wbuf[i][:, :, s_ * 128:(s_ + 1) * 128], f"wbuf{i}_{s_}") for i in range(2) for s_ in range(3)]
            scnt = [0]

            def load_slot(src_ap, nk):
                ap_, key_ = slots[scnt[0] % len(slots)]
                scnt[0] += 1
                dma(ap_[:, 0:nk, :], src_ap, w=[key_], q="pool")
                return ap_, key_

            for mb in range(16):
                cs_ = slice(mb * 128, (mb + 1) * 128)
                wa, wak = load_slot(w_bra[l][:, cs_].rearrange("(k p) n -> p k n", p=128), 8)
                wb, wbk = load_slot(w_brs[l][:, cs_].rearrange("(k p) n -> p k n", p=128), 8)
                wma, wmak = load_slot(wcols(l, O_MA + mb * 128, 128), 16)
                wms, wmsk = load_slot(wcols(l, O_MS + mb * 128, 128), 16)
                fm_linear_ap(wa, wak, 8, attnT, has_s, pm[0], PMK[0], pm[1], PMK[1], 0)
                fm_linear_ap(wb, wbk, 8, yT, has_s, pa[0], PAK[0], pm[1], PMK[1], 4)
                fm_linear_ap(wma, wmak, 16, xnT, has_s, pa[1], PAK[1], pm[1], PMK[1], 8)
                fm_linear_ap(wms, wmsk, 16, xnT, has_s, pa[2], PAK[2], pm[1], PMK[1], 12)
                ACTF(m1[:, 0:TC], pa[1][:, 0:TC], AF.Sigmoid, [], [PAK[1], "m1"])
                ACTF(m2[:, 0:TC], pa[2][:, 0:TC], AF.Sigmoid, [], [PAK[2], "m2"])
                TT(m1[:, 0:TC], pm[0][:, 0:TC], m1[:, 0:TC], ALU.mult, ["m1"], [PMK[0], "m1"])
                TT(m2[:, 0:TC], pa[0][:, 0:TC], m2[:, 0:TC], ALU.mult, ["m2"], [PAK[0], "m2"])
                TT(mergedT[:, mb, 0:TC], m1[:, 0:TC], m2[:, 0:TC], ALU.add, ["m1", "m2"], ["mergedT"])
                if has_s:
                    sc = slice(TC, NTOK)
                    ACTF(m1[:, sc], pm[1][:, 8:12], AF.Sigmoid, [], [PMK[1], "m1"])
                    ACTF(m2[:, sc], pm[1][:, 12:16], AF.Sigmoid, [], [PMK[1], "m2"])
                    TT(m1[:, sc], pm[1][:, 0:4], m1[:, sc], ALU.mult, ["m1"], [PMK[1], "m1"])
                    TT(m2[:, sc], pm[1][:, 4:8], m2[:, sc], ALU.mult, ["m2"], [PMK[1], "m2"])
                    TT(mergedT[:, mb, sc], m1[:, sc], m2[:, sc], ALU.add, ["m1", "m2"], ["mergedT"])
            if debug and ci == 0 and l == 0:
                dma(dbg["d_mergedT"], mergedT, r=["mergedT"], q="pool")
            tiles = [(tt, 128, tt * 128) for tt in range(4)] + ([(4, NS, TC)] if has_s else [])
            for c0, wd in ((0, 384), (384, 384), (768, 384), (1152, 384), (1536, 384), (1920, 128)):
                wi = load_w(w_out[l][:, c0:c0 + wd].rearrange("(k p) n -> p k n", p=128), wd, extra=True)
                for tt, M, col0 in tiles:
                    pi = next_pm()
                    for k in range(16):
                        MM(pm[pi][0:M, 0:wd], mergedT[:, k, col0:col0 + M], wbuf[wi][:, k, 0:wd], k == 0, k == 15,
                           ["mergedT", f"wbuf{wi}"], [PMK[pi]])
                    TT(xres[0:M, tt, c0:c0 + wd], pm[pi][0:M, 0:wd], xres[0:M, tt, c0:c0 + wd], ALU.add, ["xres"], [PMK[pi], "xres"])

        for ci in range(nch):
            has_s = ci == 0
            t0 = ci * TC
            dma(xres[:, 0:4, :], xp[t0:t0 + TC].rearrange("(t p) d -> p t d", p=128), w=["xres"])
            if has_s:
                dma(xres[0:NS, 4, :], xs, w=["xres"])
            for l in range(DEPTH):
                phase_norm(l, ci, has_s)
                sT = phase_ssm(l, ci, has_s)
                phase_glu(l, ci, has_s, sT)
                phase_attn(l, ci, has_s)
                phase_merge(l, ci, has_s)
            dma(y_p[t0:t0 + TC].rearrange("(t p) d -> p t d", p=128), xres[:, 0:4, :], r=["xres"])
            if has_s:
                dma(y_s, xres[0:NS, 4, :], r=["xres"])
        P.emit(nc, es)
    return nc, consts


_CACHE = {}


def _core_inputs(c, I, consts):
    f = lambda a: np.ascontiguousarray(np.asarray(a, dtype=np.float32))
    m = dict(xp=f(I["x_prompt"][c % 2]), xs=f(I["x_sample"][c]),
             ckv0=f(I["cache_kv_d1"][:, c]), ckv1=f(I["cache_kv_d4"][:, c]), ckv2=f(I["cache_kv_d16"][:, c]),
             sst=f(I["state_ssm"][:, c]))
    for k_, n_ in (("w_in", "w_in"), ("w_glu", "w_glu"), ("w_bra", "w_br_attn"), ("w_brs", "w_br_ssm"), ("w_out", "w_out"),
                   ("normw", "norm_w"), ("qnw", "q_norm_w"), ("knw", "k_norm_w"), ("b_glu", "b_glu"), ("ssm_d", "ssm_d"),
                   ("lam_re", "ssm_lambda_re"), ("lam_im", "ssm_lambda_im"), ("log_dt", "ssm_log_dt"),
                   ("b_re", "ssm_b_re"), ("b_im", "ssm_b_im"), ("c_re", "ssm_c_re"), ("c_im", "ssm_c_im")):
        m[k_] = I["_shared"][n_]
    m.update(consts)
    return m


def kernel(**I):
    f = lambda a: np.ascontiguousarray(np.asarray(a, dtype=np.float32))
    if "nc" not in _CACHE:
        _CACHE["nc"] = build_program()
    nc, consts = _CACHE["nc"]
    I = dict(I)
    I["_shared"] = {n: f(I[n]) for n in ("w_in", "w_glu", "w_br_attn", "w_br_ssm", "w_out", "norm_w", "q_norm_w", "k_norm_w",
                                           "b_glu", "ssm_d", "ssm_lambda_re", "ssm_lambda_im", "ssm_log_dt", "ssm_b_re",
                                           "ssm_b_im", "ssm_c_re", "ssm_c_im")}
    in_maps = [_core_inputs(c, I, consts) for c in range(8)]
    res = run_bass_kernel_spmd(nc, in_maps, core_ids=list(range(8)))
    R = res.results
    B = I["x_prompt"].shape[0]
    y_prompt = np.stack([R[b]["y_p"] for b in range(B)], axis=0)
    y_sample = np.stack([R[b]["y_s"] for b in range(8)], axis=0)
    kvp = [np.stack([R[b][f"kvp{i}"] for b in range(B)], axis=1) for i in range(3)]
    kvs = [np.stack([R[b][f"kvs{i}"] for b in range(8)], axis=1) for i in range(3)]
    ssm_p = np.stack([R[b]["ssm_p"] for b in range(B)], axis=1)
    ssm_s = np.stack([R[b]["ssm_s"] for b in range(8)], axis=1)
    return (y_prompt, y_sample, kvp[0], kvp[1], kvp[2], ssm_p, kvs[0], kvs[1], kvs[2], ssm_s)
```
